# Optimizing a Trainium2 kernel written in Bass

```python
import math
import jax, jax.numpy as jnp
from jax import lax
import numpy as np


D_MODEL = 4096
BATCH = 1
SEQ = 8192
DEPTH = 4

N_MIXERS = 3
HEAD_DIM = 128
MIX_WIDTH = D_MODEL
MEM_WIDTH = MIX_WIDTH // 4
TOK_WIDTH = MIX_WIDTH - MEM_WIDTH
MEM_TOKENS = 256
MEM_HEADS = 4
MEM_HEAD_DIM = MEM_WIDTH // MEM_HEADS
DIFF_HEADS = TOK_WIDTH // (2 * HEAD_DIM)
DIFF_QK_DIM = HEAD_DIM
DIFF_V_DIM = 2 * HEAD_DIM
FOX_HEADS = TOK_WIDTH // HEAD_DIM
SSD_HEAD_DIM = 64
SSD_INNER = TOK_WIDTH
SSD_HEADS = SSD_INNER // SSD_HEAD_DIM
SSD_GROUPS = 8
SSD_STATE = 128
SSD_CONV = 4
SSD_CHUNK = 128
SSD_CONV_CH = SSD_INNER + 2 * SSD_GROUPS * SSD_STATE
ROPE_THETA = 500000.0
ROPE_DIM = HEAD_DIM // 4
D_FF = 256 * ((8 * D_MODEL // 3 + 255) // 256)
FFN_CONV = 3
Q_BLOCK = 128
NORM_EPS = 1e-6

A_COLS = 3 * TOK_WIDTH + MEM_WIDTH
B_COLS = 3 * TOK_WIDTH + FOX_HEADS + MEM_WIDTH
C_COLS = SSD_INNER + SSD_CONV_CH + SSD_HEADS + MEM_WIDTH

kernel_name = "hybrid_diff_fox_ssd_block"


def rms_norm(x, g):
    xf = x.astype(jnp.float32)
    y = xf * lax.rsqrt(jnp.mean(xf * xf, axis=-1, keepdims=True) + NORM_EPS)
    return (y * g.astype(jnp.float32)).astype(x.dtype)


def partial_rotary(t, positions):
    half = ROPE_DIM // 2
    inv_freq = jnp.power(jnp.float32(ROPE_THETA), -jnp.arange(half, dtype=jnp.float32) * (2.0 / ROPE_DIM))
    ang = positions.astype(jnp.float32)[..., None] * inv_freq
    ang = ang.reshape(ang.shape[:2] + (1,) * (t.ndim - 3) + (half,))
    cos, sin = jnp.cos(ang), jnp.sin(ang)
    t1 = t[..., :half].astype(jnp.float32)
    t2 = t[..., half:ROPE_DIM].astype(jnp.float32)
    rot = jnp.concatenate([t1 * cos - t2 * sin, t2 * cos + t1 * sin], axis=-1).astype(t.dtype)
    return jnp.concatenate([rot, t[..., ROPE_DIM:]], axis=-1)


def causal_block_sweep(block_fn, seq):
    return jnp.concatenate([block_fn(s0, s0 + Q_BLOCK) for s0 in range(0, seq, Q_BLOCK)], axis=1)


def block_causal_mask(s0, s1):
    return jnp.arange(s1)[None, :] <= jnp.arange(s0, s1)[:, None]


def causal_dwconv(x, w, b):
    K, S = w.shape[0], x.shape[1]
    xp = jnp.pad(x, ((0, 0), (K - 1, 0), (0, 0)))
    out = b + xp[:, K - 1:K - 1 + S] * w[K - 1]
    for k in range(K - 1):
        out = out + xp[:, k:k + S] * w[k]
    return out


def lambda_init(layer_idx):
    return 0.8 - 0.6 * math.exp(-0.3 * layer_idx)


def diff_attention(proj, positions, lam_vecs, subln_g, lam_init):
    B, S, _ = proj.shape
    q = proj[..., :TOK_WIDTH].reshape(B, S, DIFF_HEADS, 2, DIFF_QK_DIM)
    k = proj[..., TOK_WIDTH:2 * TOK_WIDTH].reshape(B, S, DIFF_HEADS, 2, DIFF_QK_DIM)
    v = proj[..., 2 * TOK_WIDTH:3 * TOK_WIDTH].reshape(B, S, DIFF_HEADS, DIFF_V_DIM)
    q = partial_rotary(q, positions)
    k = partial_rotary(k, positions)
    lv = lam_vecs.astype(jnp.float32)
    lam = jnp.exp(jnp.sum(lv[0] * lv[1])) - jnp.exp(jnp.sum(lv[2] * lv[3])) + lam_init
    scale = DIFF_QK_DIM ** -0.5

    def block(s0, s1):
        s = jnp.einsum('bqhmd,bkhmd->bhmqk', q[:, s0:s1], k[:, :s1]).astype(jnp.float32) * scale
        s = jnp.where(block_causal_mask(s0, s1), s, -jnp.inf)
        p = jax.nn.softmax(s, axis=-1)
        w = (p[:, :, 0] - lam * p[:, :, 1]).astype(v.dtype)
        return jnp.einsum('bhqk,bkhe->bqhe', w, v[:, :s1])

    o = causal_block_sweep(block, S)
    o = rms_norm(o, subln_g) * (1.0 - lam_init)
    return o.reshape(B, S, TOK_WIDTH)


def forgetting_attention(proj, forget_bias):
    B, S, _ = proj.shape
    q = proj[..., :TOK_WIDTH].reshape(B, S, FOX_HEADS, HEAD_DIM)
    k = proj[..., TOK_WIDTH:2 * TOK_WIDTH].reshape(B, S, FOX_HEADS, HEAD_DIM)
    v = proj[..., 2 * TOK_WIDTH:3 * TOK_WIDTH].reshape(B, S, FOX_HEADS, HEAD_DIM)
    log_f = jax.nn.log_sigmoid(proj[..., 3 * TOK_WIDTH:].astype(jnp.float32) + forget_bias.astype(jnp.float32))
    cum = jnp.swapaxes(jnp.cumsum(log_f, axis=1), 1, 2)
    scale = HEAD_DIM ** -0.5

    def block(s0, s1):
        s = jnp.einsum('bqhd,bkhd->bhqk', q[:, s0:s1], k[:, :s1]).astype(jnp.float32) * scale
        s = s + cum[:, :, s0:s1, None] - cum[:, :, None, :s1]
        s = jnp.where(block_causal_mask(s0, s1), s, -jnp.inf)
        p = jax.nn.softmax(s, axis=-1).astype(v.dtype)
        return jnp.einsum('bhqk,bkhd->bqhd', p, v[:, :s1])

    return causal_block_sweep(block, S).reshape(B, S, TOK_WIDTH)


def ssd_chunked(X, A, Bm, Cm):
    B, S, H, P = X.shape
    G, N = Bm.shape[2], Bm.shape[3]
    R, L = H // G, SSD_CHUNK
    nc = S // L
    X = X.reshape(B, nc, L, G, R, P)
    A = A.reshape(B, nc, L, G, R)
    Bm = Bm.reshape(B, nc, L, G, N)
    Cm = Cm.reshape(B, nc, L, G, N)
    a_cs = jnp.cumsum(A, axis=2)
    mask = (jnp.arange(L)[None, :] <= jnp.arange(L)[:, None])[None, None, :, :, None, None]
    seg = a_cs[:, :, :, None] - a_cs[:, :, None, :]
    decay = jnp.exp(jnp.where(mask, seg, -jnp.inf))
    cb = jnp.einsum('bclgn,bcsgn->bclsg', Cm, Bm)
    y_diag = jnp.einsum('bclsg,bclsgr,bcsgrp->bclgrp', cb, decay, X)
    decay_to_end = jnp.exp(a_cs[:, :, -1:] - a_cs)
    states = jnp.einsum('bclgn,bclgr,bclgrp->bcgrpn', Bm, decay_to_end, X)
    chunk_decay = jnp.exp(a_cs[:, :, -1])

    def step(h, inp):
        st, dec = inp
        return h * dec[..., None, None] + st, h

    h0 = jnp.zeros((B, G, R, P, N), jnp.float32)
    _, h_in = lax.scan(step, h0, (jnp.moveaxis(states, 1, 0), jnp.moveaxis(chunk_decay, 1, 0)))
    h_in = jnp.moveaxis(h_in, 0, 1)
    y_off = jnp.einsum('bclgn,bcgrpn,bclgr->bclgrp', Cm, h_in, jnp.exp(a_cs))
    return (y_diag + y_off).reshape(B, S, H, P)


def ssd_mixer(proj, conv_w, conv_b, dt_bias, a_log, d_skip, norm_g):
    B, S, _ = proj.shape
    f32 = jnp.float32
    z = proj[..., :SSD_INNER].astype(f32)
    xbc = proj[..., SSD_INNER:SSD_INNER + SSD_CONV_CH]
    dt_raw = proj[..., SSD_INNER + SSD_CONV_CH:].astype(f32)
    xbc = jax.nn.silu(causal_dwconv(xbc, conv_w, conv_b)).astype(f32)
    xs = xbc[..., :SSD_INNER].reshape(B, S, SSD_HEADS, SSD_HEAD_DIM)
    Bm = xbc[..., SSD_INNER:SSD_INNER + SSD_GROUPS * SSD_STATE].reshape(B, S, SSD_GROUPS, SSD_STATE)
    Cm = xbc[..., SSD_INNER + SSD_GROUPS * SSD_STATE:].reshape(B, S, SSD_GROUPS, SSD_STATE)
    dt = jax.nn.softplus(dt_raw + dt_bias.astype(f32))
    A = -jnp.exp(a_log.astype(f32))
    y = ssd_chunked(xs * dt[..., None], dt * A, Bm, Cm)
    y = y + xs * d_skip.astype(f32)[:, None]
    g = (y.reshape(B, S, SSD_INNER) * jax.nn.silu(z)).reshape(B, S, SSD_GROUPS, SSD_INNER // SSD_GROUPS)
    g = g * lax.rsqrt(jnp.mean(g * g, axis=-1, keepdims=True) + NORM_EPS)
    out = g.reshape(B, S, SSD_INNER) * norm_g.astype(f32)
    return out.astype(proj.dtype)


def memory_attention(qm, mem_kv):
    B, S, _ = qm.shape
    M = mem_kv.shape[1]
    q = qm.reshape(B, S, MEM_HEADS, MEM_HEAD_DIM)
    k = mem_kv[..., :MEM_WIDTH].reshape(B, M, MEM_HEADS, MEM_HEAD_DIM)
    v = mem_kv[..., MEM_WIDTH:].reshape(B, M, MEM_HEADS, MEM_HEAD_DIM)
    s = jnp.einsum('bqhd,bmhd->bhqm', q, k).astype(jnp.float32) * (MEM_HEAD_DIM ** -0.5)
    p = jax.nn.softmax(s, axis=-1).astype(v.dtype)
    return jnp.einsum('bhqm,bmhd->bqhd', p, v).reshape(B, S, MEM_WIDTH)


def conv_glu_ffn(h, w_up, conv_w, conv_b, w_down):
    u = causal_dwconv(h @ w_up, conv_w, conv_b)
    gate, val = u[..., :D_FF], u[..., D_FF:]
    return (jax.nn.silu(gate) * val) @ w_down


def setup_inputs(seed: int = 0) -> dict:
    key = jax.random.key(seed)
    ks = jax.random.split(key, 32)
    f32 = jnp.float32
    n_a = len(range(0, DEPTH, N_MIXERS))
    n_b = len(range(1, DEPTH, N_MIXERS))
    n_c = len(range(2, DEPTH, N_MIXERS))

    def nrm(k, shape, scale):
        return jax.random.normal(k, shape, f32) * scale

    def gain(k, shape):
        return 1.0 + 0.02 * jax.random.normal(k, shape, f32)

    dt = jnp.exp(jax.random.uniform(ks[20], (n_c, SSD_HEADS), f32, math.log(1e-3), math.log(1e-1)))
    return {
        'x': nrm(ks[0], (BATCH, SEQ, D_MODEL), 1.0),
        'mem': nrm(ks[1], (BATCH, MEM_TOKENS, D_MODEL), 1.0),
        'positions': jnp.arange(SEQ, dtype=jnp.int32)[None, :] + jax.random.randint(ks[2], (BATCH, 1), 0, 1024, dtype=jnp.int32),
        'norm_mix': gain(ks[3], (DEPTH, D_MODEL)),
        'norm_mem': gain(ks[4], (DEPTH, D_MODEL)),
        'w_mem_kv': nrm(ks[5], (DEPTH, D_MODEL, 2 * MEM_WIDTH), D_MODEL ** -0.5),
        'w_out': nrm(ks[6], (DEPTH, MIX_WIDTH, D_MODEL), MIX_WIDTH ** -0.5),
        'norm_ffn': gain(ks[7], (DEPTH, D_MODEL)),
        'w_up': nrm(ks[8], (DEPTH, D_MODEL, 2 * D_FF), D_MODEL ** -0.5),
        'conv_ffn_w': nrm(ks[9], (DEPTH, FFN_CONV, 2 * D_FF), FFN_CONV ** -0.5),
        'conv_ffn_b': nrm(ks[10], (DEPTH, 2 * D_FF), 0.02),
        'w_down': nrm(ks[11], (DEPTH, D_FF, D_MODEL), D_FF ** -0.5),
        'a_w_in': nrm(ks[12], (n_a, D_MODEL, A_COLS), D_MODEL ** -0.5),
        'a_lambda': nrm(ks[13], (n_a, 4, DIFF_QK_DIM), 0.1),
        'a_subln': gain(ks[14], (n_a, DIFF_V_DIM)),
        'b_w_in': nrm(ks[15], (n_b, D_MODEL, B_COLS), D_MODEL ** -0.5),
        'b_forget_bias': jax.random.uniform(ks[16], (n_b, FOX_HEADS), f32, 1.0, 5.0),
        'c_w_in': nrm(ks[17], (n_c, D_MODEL, C_COLS), D_MODEL ** -0.5),
        'c_conv_w': nrm(ks[18], (n_c, SSD_CONV, SSD_CONV_CH), SSD_CONV ** -0.5),
        'c_conv_b': nrm(ks[19], (n_c, SSD_CONV_CH), 0.02),
        'c_dt_bias': dt + jnp.log(-jnp.expm1(-dt)),
        'c_a_log': jnp.log(jax.random.uniform(ks[21], (n_c, SSD_HEADS), f32, 1.0, 16.0)),
        'c_d_skip': gain(ks[22], (n_c, SSD_HEADS)),
        'c_norm_gate': gain(ks[23], (n_c, SSD_INNER)),
        'final_norm': gain(ks[24], (D_MODEL,)),
    }


def reference(x, mem, positions, norm_mix, norm_mem, w_mem_kv, w_out, norm_ffn, w_up, conv_ffn_w,
              conv_ffn_b, w_down, a_w_in, a_lambda, a_subln, b_w_in, b_forget_bias, c_w_in, c_conv_w,
              c_conv_b, c_dt_bias, c_a_log, c_d_skip, c_norm_gate, final_norm):
    for i in range(DEPTH):
        kind, j = i % N_MIXERS, i // N_MIXERS
        h = rms_norm(x, norm_mix[i])
        if kind == 0:
            proj = h @ a_w_in[j]
            tok = diff_attention(proj[..., :-MEM_WIDTH], positions, a_lambda[j], a_subln[j], lambda_init(i))
        elif kind == 1:
            proj = h @ b_w_in[j]
            tok = forgetting_attention(proj[..., :-MEM_WIDTH], b_forget_bias[j])
        else:
            proj = h @ c_w_in[j]
            tok = ssd_mixer(proj[..., :-MEM_WIDTH], c_conv_w[j], c_conv_b[j], c_dt_bias[j],
                            c_a_log[j], c_d_skip[j], c_norm_gate[j])
        mem_kv = rms_norm(mem, norm_mem[i]) @ w_mem_kv[i]
        ctx = memory_attention(proj[..., -MEM_WIDTH:], mem_kv)
        x = x + jnp.concatenate([tok, ctx], axis=-1) @ w_out[i]
        x = x + conv_glu_ffn(rms_norm(x, norm_ffn[i]), w_up[i], conv_ffn_w[i], conv_ffn_b[i], w_down[i])
    return rms_norm(x, final_norm)
```

```python
import math
from contextlib import ExitStack
import numpy as np
import concourse.bass as bass
import concourse.mybir as mybir
from concourse.bass_utils import run_bass_kernel_spmd

F32 = mybir.dt.float32
BF16 = mybir.dt.bfloat16
ALU = mybir.AluOpType
AF = mybir.ActivationFunctionType

NCORES = 8
D = 4096
KC = D // 128
SEQ = 8192
T = SEQ // NCORES
TH = 512
NTH = T // TH
DEPTH = 4
MEMW = 1024
TOKW = 3072
MEMT = 256
DFF = 11008
FC = DFF // 128
EPS = 1e-6
NEG = -1.0e30
A_COLS = 3 * TOKW + MEMW
B_COLS = 3 * TOKW + 24 + MEMW
C_COLS = TOKW + 5120 + 48 + MEMW
NPRE = 56
ROPE_THETA = 500000.0


def lambda_init(i):
    return 0.8 - 0.6 * math.exp(-0.3 * i)


class Buf:
    __slots__ = ("w", "r", "name")

    def __init__(self, name=""):
        self.w = None
        self.r = {}
        self.name = name


class KB:
    def __init__(self, nc, es):
        self.nc = nc
        self.es = es
        self.eng = {"pe": nc.tensor, "act": nc.scalar, "dve": nc.vector, "pool": nc.gpsimd, "sp": nc.sync, "cc": nc.gpsimd}
        self.qmap = {"pool": "sp"}
        self.esem = {}
        self.ecnt = {}
        for q in ("pe", "act", "dve", "pool"):
            self.esem[q] = es.enter_context(nc.semaphore("s_" + q))
            self.ecnt[q] = 0
        self.waited = {q: {} for q in self.eng}
        self.waited["cc"] = self.waited["pool"]
        self.dpool = {}
        self.dnext = {}
        for q, n in (("sp", 32), ("cc", 16), ("act", 4)):
            self.dpool[q] = [[es.enter_context(nc.semaphore("d_%s%d" % (q, i))), 0] for i in range(n)]
            self.dnext[q] = 0
        self.ccpool = [[es.enter_context(nc.semaphore("s_cc%d" % i)), 0] for i in range(6)]
        self.ccnext = 0
        self.psum = []
        for i in range(8):
            t = es.enter_context(nc.psum_tensor("ps%d" % i, [128, 512], F32))
            self.psum.append((t, Buf("ps%d" % i)))
        self.psn = 0
        self.nuid = 0
        self.hslots = [Buf("hs%d" % i) for i in range(16)]
        self.hsn = 0

    def uid(self, p="t"):
        self.nuid += 1
        return "%s%d" % (p, self.nuid)

    def _wait(self, q, tok):
        if tok is None:
            return
        sem, val, src = tok
        if q == "pe" and src == "pe":
            return
        key = id(sem)
        if self.waited[q].get(key, 0) >= val:
            return
        self.eng[q].wait_ge(sem, val)
        self.waited[q][key] = val

    def deps(self, q, reads, writes):
        for b in reads:
            self._wait(q, b.w)
        for b in writes:
            self._wait(q, b.w)
            for t in b.r.values():
                self._wait(q, t)

    def done(self, tok, reads, writes):
        key = id(tok[0])
        for b in reads:
            old = b.r.get(key)
            if old is None or old[1] < tok[1]:
                b.r[key] = tok
        for b in writes:
            b.w = tok
            b.r = {}

    def op(self, q, fn, reads=(), writes=()):
        self.deps(q, reads, writes)
        ins = fn(self.eng[q])
        self.ecnt[q] += 1
        ins.then_inc(self.esem[q], 1)
        self.done((self.esem[q], self.ecnt[q], q), reads, writes)

    def dma(self, q, out, in_, reads=(), writes=()):
        q = self.qmap.get(q, q)
        pool = self.dpool[q]
        i = self.dnext[q]
        self.dnext[q] = (i + 1) % len(pool)
        sem, cnt = pool[i]
        if cnt > 0:
            self._wait(q, (sem, cnt, "dma"))
        self.deps(q, reads, writes)
        self.eng[q].dma_start(out=out, in_=in_).then_inc(sem, 16)
        pool[i][1] = cnt + 16
        self.done((sem, cnt + 16, "dma"), reads, writes)

    def allgather(self, groups, in_ap, out_ap, reads, writes):
        q = "cc"
        i = self.ccnext
        self.ccnext = (i + 1) % len(self.ccpool)
        sem, cnt = self.ccpool[i]
        if cnt > 0:
            self._wait(q, (sem, cnt, "cc"))
        self.deps(q, reads, writes)
        self.nc.gpsimd.collective_compute("AllGather", ALU.bypass, replica_groups=groups,
                                          ins=[in_ap], outs=[out_ap]).then_inc(sem, 1)
        self.ccpool[i][1] = cnt + 1
        self.done((sem, cnt + 1, "cc"), reads, writes)

    def barrier(self, full=False):
        toks = [(self.esem[q], self.ecnt[q], q) for q in self.esem if self.ecnt[q] > 0]
        for q in self.dpool:
            if q == "cc" and not full:
                continue
            for sem, cnt in self.dpool[q]:
                if cnt > 0:
                    toks.append((sem, cnt, "dma"))
        if full:
            for sem, cnt in self.ccpool:
                if cnt > 0:
                    toks.append((sem, cnt, "cc"))
        for q in ("pe", "act", "dve", "sp", "cc"):
            for t in toks:
                sem, val, src = t
                if src == q and q != "pe" and src in self.esem:
                    pass
                key = id(sem)
                if self.waited[q].get(key, 0) >= val:
                    continue
                self.eng[q].wait_ge(sem, val)
                self.waited[q][key] = val

    def ps(self):
        t, b = self.psum[self.psn % 7]
        self.psn += 1
        return t, b

    def hps(self):
        i = self.hsn % 16
        self.hsn += 1
        return self.psum[7][0][:, i * 8:(i + 1) * 8], self.hslots[i]

    def hps2(self):
        return self.psum[7]

    def sb(self, es, name, shape, dt):
        t = es.enter_context(self.nc.sbuf_tensor(self.uid(name), list(shape), dt))
        return t, Buf(name)

    def dram(self, name, shape, dt):
        t = self.nc.dram_tensor(self.uid(name), list(shape), dt)
        return t.ap(), Buf(name)


def two_stage_allgather(kb, src_ap, src_buf, rows, cols, dt, name):
    g1, g1b = kb.dram(name + "_g1", [4 * rows, cols], dt)
    g2, g2b = kb.dram(name + "_g2", [8 * rows, cols], dt)
    kb.allgather([[0, 1, 2, 3], [4, 5, 6, 7]], src_ap, g1, [src_buf], [g1b])
    kb.allgather([[0, 4], [1, 5], [2, 6], [3, 7]], g1, g2, [g1b], [g2b])
    return g2, g2b


class G:
    pass


class GW:
    def __init__(self, rs, cols, pw):
        self.rs, self.cols, self.pw = rs, cols, pw
        self.pieces = []
        self.cpr = (rs + 127) // 128
        self.nkc = 8 * self.cpr

    def ksz(self, kc):
        j = kc % self.cpr
        return min(128, self.rs - j * 128)

    def load(self, kb, qs, wt, wb, c0, w, k0, k1):
        nq = 0
        for (p0, pw, A, B, pbuf) in self.pieces:
            lo, hi = max(c0, p0), min(c0 + w, p0 + pw)
            if lo >= hi:
                continue
            for r in range(8):
                kr0, kr1 = r * self.cpr, (r + 1) * self.cpr
                a0, a1 = max(k0, kr0), min(k1, kr1)
                if a0 >= a1:
                    continue
                buf = A if r in (0, 1, 4, 5) else B
                slot = {0: 0, 1: 1, 4: 2, 5: 3, 2: 0, 3: 1, 6: 2, 7: 3}[r]
                base = slot * self.rs
                nfull = self.rs // 128
                f0, f1 = a0 - kr0, min(a1 - kr0, nfull)
                if f1 > f0:
                    src = buf[base + f0 * 128:base + f1 * 128, lo - p0:hi - p0].rearrange("(kc p) n -> p kc n", p=128)
                    kb.dma(qs[nq % len(qs)], wt[:, a0 - k0:a0 - k0 + (f1 - f0), lo - c0:hi - c0], src, list(pbuf), [wb])
                    nq += 1
                if a1 - kr0 > nfull:
                    rem = self.rs - nfull * 128
                    src = buf[base + nfull * 128:base + self.rs, lo - p0:hi - p0]
                    kb.dma(qs[nq % len(qs)], wt[0:rem, kr0 + nfull - k0, lo - c0:hi - c0], src, list(pbuf), [wb])
                    nq += 1


def prep_weight(kb, g, name, rows, cols, pw):
    nc = kb.nc
    src = nc.dram_tensor(name, [rows, cols], F32, kind="ExternalInput").ap()
    gw = GW(rows, cols, pw)
    gw.buf = Buf(name)
    for c0 in range(0, cols, pw):
        w = min(pw, cols - c0)
        bnc, bb = kb.dram(name + "_b", [rows, w], BF16)
        kb.dma("cc", bnc[:, :], src[:, c0:c0 + w], [], [bb])
        g1, g1b = kb.dram(name + "_g1", [4 * rows, w], BF16)
        kb.allgather([[0, 1, 2, 3], [4, 5, 6, 7]], bnc, g1, [bb], [g1b])
        A, _ = kb.dram(name + "_A", [4 * rows, w], BF16)
        B, _ = kb.dram(name + "_B", [4 * rows, w], BF16)
        pbuf = Buf(name)
        pbuf2 = Buf(name)
        kb.allgather([[0, 4], [1, 5], [2, 6], [3, 7]], g1[0:2 * rows, :], A, [g1b], [pbuf])
        kb.allgather([[0, 4], [1, 5], [2, 6], [3, 7]], g1[2 * rows:4 * rows, :], B, [g1b], [pbuf2])
        gw.pieces.append((c0, w, A, B, (pbuf, pbuf2)))
    return gw


def rmsnorm_fm(kb, g, x_ap, gcol0, out_ap, ncols, nkc=KC, dim=D, tails=None):
    nc = kb.nc
    with ExitStack() as es:
        xt = [kb.sb(es, "nx", [128, ncols], F32) for _ in range(3)]
        sq = [kb.sb(es, "nsq", [128, ncols], F32) for _ in range(2)]
        rstd, rstdb = kb.sb(es, "nrstd", [128, ncols], F32)
        ho = [kb.sb(es, "nh", [128, ncols], BF16) for _ in range(2)]
        nh = (ncols + 511) // 512
        pss = [kb.ps() for _ in range(nh)]
        for kc in range(nkc):
            x, xb = xt[kc % 3]
            kb.dma("sp", x[:], x_ap[kc * 128:(kc + 1) * 128, :], [], [xb])
            s, sb_ = sq[kc % 2]
            kb.op("act", lambda e: e.activation(out=s[:], in_=x[:], func=AF.Square), [xb], [sb_])
            for h in range(nh):
                c0, c1 = h * 512, min(ncols, (h + 1) * 512)
                pt, pb = pss[h]
                kb.op("pe", lambda e: e.matmul(pt[:, 0:c1 - c0], lhsT=g.ones_f[:], rhs=s[:, c0:c1],
                                                start=(kc == 0), stop=(kc == nkc - 1)), [sb_, g.ones_fb], [pb])
        for h in range(nh):
            c0, c1 = h * 512, min(ncols, (h + 1) * 512)
            pt, pb = pss[h]
            kb.op("dve", lambda e: e.tensor_scalar(out=rstd[:, c0:c1], in0=pt[:, 0:c1 - c0], scalar1=1.0 / dim,
                                                   scalar2=EPS, op0=ALU.mult, op1=ALU.add), [pb], [rstdb])
        kb.op("act", lambda e: e.activation(out=rstd[:], in_=rstd[:], func=AF.Sqrt), [rstdb], [rstdb])
        kb.op("dve", lambda e: e.reciprocal(out=rstd[:], in_=rstd[:]), [rstdb], [rstdb])
        for kc in range(nkc):
            x, xb = xt[kc % 3]
            kb.dma("sp", x[:], x_ap[kc * 128:(kc + 1) * 128, :], [], [xb])
            h_, hb = ho[kc % 2]
            kb.op("dve", lambda e: e.scalar_tensor_tensor(out=h_[:], in0=x[:], scalar=g.gains[:, gcol0 + kc:gcol0 + kc + 1],
                                                          in1=rstd[:], op0=ALU.mult, op1=ALU.mult),
                  [xb, rstdb, g.gainsb], [hb])
            kb.dma("pool", out_ap[kc * 128:(kc + 1) * 128, :], h_[:], [hb], [])
            if tails is not None:
                kb.op("act", lambda e: e.activation(out=tails[0][:, kc, :], in_=h_[:, ncols - 3:ncols], func=AF.Copy), [hb], [tails[1]])
    kb.barrier()


def dense(kb, g, rhs_loader, nkc, w_aps, tiles, evac, nth=NTH, ntok=TH, halo=0, halo_evac=None, kgrp=None):
    if kgrp is None:
        kgrp = nkc if nkc <= 43 else (nkc + 1) // 2
    ngrp = (nkc + kgrp - 1) // kgrp
    with ExitStack() as es:
        rhs, rhsb = kb.sb(es, "rhs", [128, nkc, halo + ntok], BF16)
        NW = 3
        wts = [kb.sb(es, "wt", [128, kgrp, 256], BF16) for _ in range(NW)]
        jobs = [(th, ti, kg) for th in range(nth) for ti in range(len(tiles)) for kg in range(ngrp)]

        def load_w(j):
            th, ti, kg = jobs[j]
            mode, wi, c0, w = tiles[ti]
            k0 = kg * kgrp
            k1 = min(nkc, k0 + kgrp)
            wt, wb = wts[j % NW]
            w_aps[wi].load(kb, ("sp", "pool") if j % 2 == 0 else ("pool", "sp"), wt, wb, c0, w, k0, k1)

        for j in range(min(2, len(jobs))):
            load_w(j)
        cur = None
        for j, (th, ti, kg) in enumerate(jobs):
            if j + 2 < len(jobs):
                load_w(j + 2)
            if ti == 0 and kg == 0:
                rhs_loader(th, rhs, rhsb)
            mode, wi, c0, w = tiles[ti]
            k0 = kg * kgrp
            k1 = min(nkc, k0 + kgrp)
            wt, wb = wts[j % NW]
            if mode == "fm":
                nsub = (w + 127) // 128
                if kg == 0:
                    cur = [kb.ps() for _ in range(nsub)]
                    curh = [kb.hps() for _ in range(nsub)] if halo else None
                for sub in range(nsub):
                    ws = min(128, w - sub * 128)
                    pt, pb = cur[sub]
                    for kc in range(k0, k1):
                        kz = w_aps[wi].ksz(kc)
                        kb.op("pe", lambda e: e.matmul(pt[0:ws, 0:ntok], lhsT=wt[0:kz, kc - k0, sub * 128:sub * 128 + ws],
                                                        rhs=rhs[0:kz, kc, halo:halo + ntok], start=(kc == 0), stop=(kc == nkc - 1)),
                              [wb, rhsb], [pb])
                    if halo:
                        ph, phb = curh[sub]
                        for kc in range(k0, k1):
                            kb.op("pe", lambda e: e.matmul(ph[0:ws, 0:halo], lhsT=wt[:, kc - k0, sub * 128:sub * 128 + ws],
                                                            rhs=rhs[:, kc, 0:halo], start=(kc == 0), stop=(kc == nkc - 1)),
                                  [wb, rhsb], [phb])
                if kg == ngrp - 1:
                    for sub in range(nsub):
                        ws = min(128, w - sub * 128)
                        if halo:
                            halo_evac(wi, c0 + sub * 128, ws, th, curh[sub][0], curh[sub][1])
                        evac("fm", wi, c0 + sub * 128, ws, th, cur[sub][0], cur[sub][1])
            else:
                ntt = ntok // 128
                if kg == 0:
                    cur = [kb.ps() for _ in range(ntt)]
                for tt in range(ntt):
                    pt, pb = cur[tt]
                    for kc in range(k0, k1):
                        kb.op("pe", lambda e: e.matmul(pt[:, 0:w], lhsT=rhs[:, kc, halo + tt * 128:halo + (tt + 1) * 128],
                                                        rhs=wt[:, kc - k0, 0:w], start=(kc == 0), stop=(kc == nkc - 1)),
                              [wb, rhsb], [pb])
                if kg == ngrp - 1:
                    for tt in range(ntt):
                        evac("tm", wi, c0, w, (th, tt), cur[tt][0], cur[tt][1])
    kb.barrier()


def exchange_halo(kb, g, tails):
    tl, tlb = tails
    bnc, bb = kb.dram("tailb", [128, KC * 3], BF16)
    kb.dma("cc", bnc[:, :], tl[:].rearrange("p k h -> p (k h)"), [tlb], [bb])
    ga, gab = two_stage_allgather(kb, bnc, bb, 128, KC * 3, BF16, "tail")
    with ExitStack() as es:
        al, alb = kb.sb(es, "tall", [128, 8, KC * 3], BF16)
        acc, accb = kb.sb(es, "tacc", [128, KC * 3], F32)
        kb.dma("pool", al[:], ga.rearrange("(r p) x -> p r x", p=128), [gab], [alb])
        for j in range(8):
            sel = g.ctab[:, 64 + j:65 + j]
            if j == 0:
                kb.op("dve", lambda e: e.tensor_scalar(out=acc[:], in0=al[:, j, :], scalar1=sel, scalar2=None, op0=ALU.mult),
                      [alb, g.ctabb], [accb])
            else:
                kb.op("dve", lambda e: e.scalar_tensor_tensor(out=acc[:], in0=al[:, j, :], scalar=sel, in1=acc[:],
                                                              op0=ALU.mult, op1=ALU.add), [alb, accb, g.ctabb], [accb])
        kb.op("dve", lambda e: e.tensor_copy(out=g.halo[:].rearrange("p k h -> p (k h)"), in_=acc[:]), [accb], [g.halob])
    kb.barrier()


def make_rhs_loader(kb, g, src_ap, nkc, halo, rs=512):
    cpr = (rs + 127) // 128
    nfull = rs // 128

    def loader(th, rhs, rhsb):
        if rs % 128 == 0:
            kb.dma("sp", rhs[:, :, halo:halo + TH], src_ap[:, th * TH:(th + 1) * TH].rearrange("(kc p) t -> p kc t", p=128), [], [rhsb])
        else:
            for r in range(8):
                q = "sp" if r % 2 == 0 else "pool"
                kb.dma(q, rhs[:, r * cpr:r * cpr + nfull, halo:halo + TH],
                       src_ap[r * rs:r * rs + nfull * 128, th * TH:(th + 1) * TH].rearrange("(kc p) t -> p kc t", p=128), [], [rhsb])
                kb.dma(q, rhs[0:rs - nfull * 128, r * cpr + nfull, halo:halo + TH],
                       src_ap[r * rs + nfull * 128:(r + 1) * rs, th * TH:(th + 1) * TH], [], [rhsb])
        if halo:
            if th == 0:
                kb.op("dve", lambda e: e.tensor_copy(out=rhs[:, :, 0:halo], in_=g.halo[:, :, 3 - halo:3]), [g.halob], [rhsb])
            else:
                kb.dma("sp", rhs[:, :, 0:halo], src_ap[:, th * TH - halo:th * TH].rearrange("(kc p) t -> p kc t", p=128), [], [rhsb])
    return loader


class Stager:
    def __init__(self, kb, es, n, dt, name="stg", width=512):
        self.kb = kb
        self.t = [kb.sb(es, name, [128, width], dt) for _ in range(n)]
        self.i = 0

    def get(self):
        r = self.t[self.i % len(self.t)]
        self.i += 1
        return r


def conv_silu(kb, uext, uextb, K, cw, cwb, blk, acc, accb, width):
    for k in range(K - 1, -1, -1):
        src = uext[:, k:k + width]
        wk = cw[:, blk, k:k + 1]
        if k == K - 1:
            kb.op("dve", lambda e: e.tensor_scalar(out=acc[:, 0:width], in0=src, scalar1=wk, scalar2=cw[:, blk, K:K + 1],
                                                   op0=ALU.mult, op1=ALU.add), [uextb, cwb], [accb])
        else:
            kb.op("dve", lambda e: e.scalar_tensor_tensor(out=acc[:, 0:width], in0=src, scalar=wk, in1=acc[:, 0:width],
                                                          op0=ALU.mult, op1=ALU.add), [uextb, cwb, accb], [accb])


def residual_dense(kb, g, src_ap, gw):
    with ExitStack() as es:
        xin = Stager(kb, es, 3, F32, "rx")
        xout = Stager(kb, es, 3, F32, "ro")

        def evac(mode, wi, c0, ws, th, pt, pb):
            xi, xib = xin.get()
            xo, xob = xout.get()
            reg = g.xres[c0:c0 + ws, th * TH:(th + 1) * TH]
            kb.dma("pool", xi[0:ws, :], reg, [], [xib])
            kb.op("dve", lambda e: e.tensor_tensor(out=xo[0:ws, :], in0=pt[0:ws, 0:TH], in1=xi[0:ws, :], op=ALU.add),
                  [pb, xib], [xob])
            kb.dma("pool", reg, xo[0:ws, :], [xob], [])

        tiles = [("fm", 0, c0, 256) for c0 in range(0, D, 256)]
        dense(kb, g, make_rhs_loader(kb, g, src_ap, gw.nkc, 0, rs=gw.rs), gw.nkc, [gw], tiles, evac)


def ffn_phase(kb, g, i):
    nc = kb.nc
    with ExitStack() as es0:
        tails = kb.sb(es0, "tails", [128, KC, 3], BF16)
        rmsnorm_fm(kb, g, g.xres, g.gcol["ffn"][i], g.hT, T, tails=tails)
        exchange_halo(kb, g, tails)
    g.hook()
    with ExitStack() as es:
        cw, cwb = kb.sb(es, "cwf", [128, 2 * FC, 4], F32)
        kb.dma("sp", cw[:], g.in_cwf[i], [], [cwb])
        uexts = [kb.sb(es, "uext", [128, 2 + TH], F32) for _ in range(3)]
        accs = [kb.sb(es, "facc", [128, TH], F32) for _ in range(2)]
        gates = {}
        gate_t = [kb.sb(es, "gate", [128, TH], F32) for _ in range(4)]
        outs = Stager(kb, es, 3, BF16, "fo")
        st = {"u": 0, "a": 0, "g": 0}
        pend = {}

        def halo_evac(wi, c0, ws, th, ph, phb):
            ue, ueb = uexts[st["u"] % 3]
            kb.op("act", lambda e: e.activation(out=ue[:, 0:2], in_=ph[:, 0:2], func=AF.Copy), [phb], [ueb])
            pend[(wi, c0)] = (ue, ueb)
            st["u"] += 1

        def evac(mode, wi, c0, ws, th, pt, pb):
            ue, ueb = pend.pop((wi, c0))
            kb.op("act", lambda e: e.activation(out=ue[:, 2:2 + TH], in_=pt[:, 0:TH], func=AF.Copy), [pb], [ueb])
            ac, acb = accs[st["a"] % 2]
            st["a"] += 1
            blk = c0 // 128 + (FC if wi == 1 else 0)
            conv_silu(kb, ue, ueb, 3, cw, cwb, blk, ac, acb, TH)
            if wi == 0:
                gt, gtb = gate_t[st["g"] % 4]
                st["g"] += 1
                kb.op("act", lambda e: e.activation(out=gt[:], in_=ac[:], func=AF.Silu), [acb], [gtb])
                gates[c0] = (gt, gtb)
            else:
                gt, gtb = gates.pop(c0)
                o, ob = outs.get()
                kb.op("dve", lambda e: e.tensor_tensor(out=o[:], in0=ac[:], in1=gt[:], op=ALU.mult), [acb, gtb], [ob])
                kb.dma("pool", g.actT[c0:c0 + 128, th * TH:(th + 1) * TH], o[:], [ob], [])

        tiles = []
        for c0 in range(0, DFF, 256):
            tiles.append(("fm", 0, c0, 256))
            tiles.append(("fm", 1, c0, 256))
        dense(kb, g, make_rhs_loader(kb, g, g.hT, KC, 2), KC, [g.w_upg[i], g.w_upv[i]], tiles, evac, halo=2, halo_evac=halo_evac)
    residual_dense(kb, g, g.actT, g.w_down[i])


IN_COLS = [A_COLS, B_COLS, C_COLS, A_COLS]
CST_COLS = 128 + 4 * 512 + 32 + 2


def build(layers=(0, 1, 2, 3), parts=("mix", "ffn"), final=True, dbg=()):
    nc = bass.Bass("TRN2", target_bir_lowering=False)
    with ExitStack() as es:
        block = es.enter_context(nc.Block())
        kb = KB(nc, es)
        g = G()
        g.layers = layers

        def ext(name, shape, dt=F32):
            return nc.dram_tensor(name, list(shape), dt, kind="ExternalInput").ap()

        in_xT = ext("xT", [D, T])
        in_cst = ext("cst", [128, CST_COLS])
        in_ctab = ext("ctab", [128, 80])
        in_gains = ext("gains", [128, 13 * KC])
        g.gcol = {"mix": [i * KC for i in range(4)], "ffn": [(4 + i) * KC for i in range(4)],
                  "mem": [(8 + i) * KC for i in range(4)], "final": 12 * KC}
        g.in_cwf = {}
        yT = nc.dram_tensor("yT", [D, T], F32, kind="ExternalOutput").ap()
        g.dbg_out = {}

        cst, cstb = kb.sb(es, "cst", [128, CST_COLS], F32)
        g.cst, g.cstb = cst, cstb
        g.ident = cst[:, 0:128]
        g.maskadd = [cst[:, 128 + r * 512:128 + (r + 1) * 512] for r in range(4)]
        g.swp = cst[0:32, 128 + 2048:128 + 2048 + 32]
        g.ropec = cst[0:32, 128 + 2048 + 32:128 + 2048 + 34]
        g.ctab, g.ctabb = kb.sb(es, "ctab", [128, 80], F32)
        g.gains, g.gainsb = kb.sb(es, "gains", [128, 13 * KC], F32)
        g.ones_f, g.ones_fb = kb.sb(es, "ones_f", [128, 128], F32)
        g.ones_h, g.ones_hb = kb.sb(es, "ones_h", [128, 128], BF16)
        g.halo, g.halob = kb.sb(es, "halo", [128, KC, 3], BF16)
        g.onecol = g.ones_f[:, 0:1]

        def body(_):
            kb.dma("sp", cst[:], in_cst, [], [cstb])
            kb.dma("sp", g.ctab[:], in_ctab, [], [g.ctabb])
            kb.dma("sp", g.gains[:], in_gains, [], [g.gainsb])
            kb.op("dve", lambda e: e.memset(g.ones_f[:], 1.0), [], [g.ones_fb])
            kb.op("dve", lambda e: e.memset(g.ones_h[:], 1.0), [], [g.ones_hb])
            g.xres, _ = kb.dram("xres", [D, T], F32)
            g.hT, _ = kb.dram("hT", [D, T], BF16)
            g.catT, _ = kb.dram("catT", [D, T], BF16)
            g.actT, _ = kb.dram("actT", [DFF, T], BF16)
            for q in range(8):
                kb.dma("sp" if q % 2 else "pool", g.xres[q * 512:(q + 1) * 512, :], in_xT[q * 512:(q + 1) * 512, :], [], [])
            g.w_in, g.w_out, g.w_upg, g.w_upv, g.w_down, g.w_mkv = {}, {}, {}, {}, {}, {}
            units = []
            for i in layers:
                g.in_cwf[i] = ext("cwf%d" % i, [128, 2 * FC, 4])
                if "mix" in parts:
                    units.append(("mix", i))
                if "ffn" in parts:
                    units.append(("ffn", i))

            def prep_unit(u):
                what, i = u
                if what == "mix":
                    g.w_in[i] = prep_weight(kb, g, "w_in%d" % i, 512, IN_COLS[i], 1024)
                    g.w_mkv[i] = prep_weight(kb, g, "w_mkv%d" % i, 512, 2 * MEMW, 1024)
                    g.w_out[i] = prep_weight(kb, g, "w_out%d" % i, 512, D, 1024)
                else:
                    g.w_upg[i] = prep_weight(kb, g, "w_upg%d" % i, 512, DFF, 1024)
                    g.w_upv[i] = prep_weight(kb, g, "w_upv%d" % i, 512, DFF, 1024)
                    g.w_down[i] = prep_weight(kb, g, "w_down%d" % i, DFF // 8, D, 256)

            def hook():
                if units:
                    prep_unit(units.pop(0))

            g.hook = hook
            hook()
            if len(parts) == 1:
                hook()
            if "mix" in parts:
                mixer_setup(kb, g, ext)
            kb.barrier()
            for i in layers:
                if "mix" in parts:
                    mixer_phase(kb, g, i)
                if "ffn" in parts:
                    ffn_phase(kb, g, i)
            if final:
                rmsnorm_final(kb, g, yT)
            else:
                for q in range(8):
                    kb.dma("sp", yT[q * 512:(q + 1) * 512, :], g.xres[q * 512:(q + 1) * 512, :], [], [])
            kb.barrier(full=True)
            if "catT" in dbg:
                dcat = nc.dram_tensor("dbg_catT", [D, T], BF16, kind="ExternalOutput").ap()
                for q in range(8):
                    kb.dma("sp", dcat[q * 512:(q + 1) * 512, :], g.catT[q * 512:(q + 1) * 512, :], [], [])
                kb.barrier(full=True)
            g.stats = dict(kb.ecnt)

        block.gpsimd(body)
        nc._kstats = g.stats
    return nc


def rmsnorm_final(kb, g, yT):
    with ExitStack() as es:
        xt = [kb.sb(es, "fx", [128, T], F32) for _ in range(3)]
        sq = [kb.sb(es, "fsq", [128, T], F32) for _ in range(2)]
        rstd, rstdb = kb.sb(es, "frstd", [128, T], F32)
        ho = [kb.sb(es, "fh", [128, T], F32) for _ in range(2)]
        pss = [kb.ps() for _ in range(2)]
        for kc in range(KC):
            x, xb = xt[kc % 3]
            kb.dma("sp", x[:], g.xres[kc * 128:(kc + 1) * 128, :], [], [xb])
            s, sb_ = sq[kc % 2]
            kb.op("act", lambda e: e.activation(out=s[:], in_=x[:], func=AF.Square), [xb], [sb_])
            for h in range(2):
                pt, pb = pss[h]
                kb.op("pe", lambda e: e.matmul(pt[:, 0:512], lhsT=g.ones_f[:], rhs=s[:, h * 512:(h + 1) * 512],
                                                start=(kc == 0), stop=(kc == KC - 1)), [sb_, g.ones_fb], [pb])
        for h in range(2):
            pt, pb = pss[h]
            kb.op("dve", lambda e: e.tensor_scalar(out=rstd[:, h * 512:(h + 1) * 512], in0=pt[:, 0:512], scalar1=1.0 / D,
                                                   scalar2=EPS, op0=ALU.mult, op1=ALU.add), [pb], [rstdb])
        kb.op("act", lambda e: e.activation(out=rstd[:], in_=rstd[:], func=AF.Sqrt), [rstdb], [rstdb])
        kb.op("dve", lambda e: e.reciprocal(out=rstd[:], in_=rstd[:]), [rstdb], [rstdb])
        gc = g.gcol["final"]
        for kc in range(KC):
            x, xb = xt[kc % 3]
            kb.dma("sp", x[:], g.xres[kc * 128:(kc + 1) * 128, :], [], [xb])
            h_, hb = ho[kc % 2]
            kb.op("dve", lambda e: e.scalar_tensor_tensor(out=h_[:], in0=x[:], scalar=g.gains[:, gc + kc:gc + kc + 1],
                                                          in1=rstd[:], op0=ALU.mult, op1=ALU.mult), [xb, rstdb, g.gainsb], [hb])
            kb.dma("pool", yT[kc * 128:(kc + 1) * 128, :], h_[:], [hb], [])


def gather_chunks(kb, name, src_ap, rows, cols, dt, by, csz):
    out = []
    n = rows if by == "r" else cols
    for lo in range(0, n, csz):
        hi = min(n, lo + csz)
        if by == "r":
            cr, cc = hi - lo, cols
            piece = src_ap[lo:hi, :]
        else:
            cr, cc = rows, hi - lo
            piece = src_ap[:, lo:hi]
        bnc, bb = kb.dram(name + "_b", [cr, cc], dt)
        kb.dma("cc", bnc[:, :], piece, [], [bb])
        g1, g1b = kb.dram(name + "_g1", [4 * cr, cc], dt)
        kb.allgather([[0, 1, 2, 3], [4, 5, 6, 7]], bnc, g1, [bb], [g1b])
        A, _ = kb.dram(name + "_A", [4 * cr, cc], dt)
        B, _ = kb.dram(name + "_B", [4 * cr, cc], dt)
        b1, b2 = Buf(name), Buf(name)
        kb.allgather([[0, 4], [1, 5], [2, 6], [3, 7]], g1[0:2 * cr, :], A, [g1b], [b1])
        kb.allgather([[0, 4], [1, 5], [2, 6], [3, 7]], g1[2 * cr:4 * cr, :], B, [g1b], [b2])
        out.append((lo, hi, A, B, (b1, b2)))
    return out


PAIRS = [("A", 0, 0), ("B", 0, 2), ("A", 2, 4), ("B", 2, 6)]


def mixer_setup(kb, g, ext):
    nc = kb.nc
    es = kb.es
    g.qT, _ = kb.dram("qT", [TOKW, T], BF16)
    g.kTb, _ = kb.dram("kTb", [TOKW, T], BF16)
    g.vb, _ = kb.dram("vb", [T, TOKW], BF16)
    g.qmT, _ = kb.dram("qmT", [MEMW, T], BF16)
    g.szT, _ = kb.dram("szT", [TOKW, T], BF16)
    g.xsT, _ = kb.dram("xsT", [TOKW, T], F32)
    g.cumb, _ = kb.dram("cumb", [T, 48], F32)
    g.cumHMd, _ = kb.dram("cumHMd", [48, T], F32)
    g.hmT, _ = kb.dram("hmT", [D, MEMT], BF16)
    g.kmT, _ = kb.dram("kmT", [MEMW, MEMT], BF16)
    g.vm, _ = kb.dram("vm", [MEMT, MEMW], BF16)
    g.memT = ext("memT", [D, MEMT])
    in_pos = nc.dram_tensor("pos", [1, T], mybir.dt.int32, kind="ExternalInput").ap()
    g.in_small = {}
    for i in g.layers:
        g.in_small[i] = ext("small%d" % i, [128, SMALL_COLS])
    g.cos_t, g.cos_tb = kb.sb(es, "cos_t", [32, T], F32)
    g.sin_t, g.sin_tb = kb.sb(es, "sin_t", [32, T], F32)
    g.aHM, g.aHMb = kb.sb(es, "aHM", [48, T], F32)
    g.dtHM, g.dtHMb = kb.sb(es, "dtHM", [48, T], F32)
    with ExitStack() as es2:
        pi_, pib = kb.sb(es2, "posi", [32, T], mybir.dt.int32)
        pf, pfb = kb.sb(es2, "posf", [32, T], F32)
        tmp, tmpb = kb.sb(es2, "postmp", [32, T], F32)
        kb.dma("sp", pi_[:], in_pos[0, :].partition_broadcast(32), [], [pib])
        kb.op("dve", lambda e: e.tensor_copy(out=pf[:], in_=pi_[:]), [pib], [pfb])
        kb.op("dve", lambda e: e.tensor_scalar(out=pf[:], in0=pf[:], scalar1=g.ropec[:, 0:1], scalar2=None, op0=ALU.mult),
              [pfb, g.cstb], [pfb])
        ki, kib = kb.sb(es2, "poski", [32, T], mybir.dt.int32)
        s1, s1b = kb.sb(es2, "poss1", [32, T], F32)
        hp, hpb = kb.sb(es2, "halfpi", [32, 1], F32)
        kb.op("dve", lambda e: e.memset(hp[:], 0.5 * math.pi), [], [hpb])
        for (dst, dstb, shift) in ((g.sin_t, g.sin_tb, 0.0), (g.cos_t, g.cos_tb, 0.5 * math.pi)):
            kb.op("dve", lambda e: e.tensor_scalar(out=tmp[:], in0=pf[:], scalar1=shift, scalar2=1.0 / (2.0 * math.pi),
                                                   op0=ALU.add, op1=ALU.mult), [pfb], [tmpb])
            kb.op("dve", lambda e: e.tensor_copy(out=ki[:], in_=tmp[:]), [tmpb], [kib])
            kb.op("dve", lambda e: e.tensor_copy(out=tmp[:], in_=ki[:]), [kib], [tmpb])
            kb.op("dve", lambda e: e.tensor_scalar(out=s1[:], in0=pf[:], scalar1=shift, scalar2=None, op0=ALU.add), [pfb], [s1b])
            kb.op("dve", lambda e: e.scalar_tensor_tensor(out=tmp[:], in0=tmp[:], scalar=-2.0 * math.pi, in1=s1[:], op0=ALU.mult, op1=ALU.add),
                  [tmpb, s1b], [tmpb])
            kb.op("act", lambda e: e.activation(out=s1[:], in_=tmp[:], func=AF.Sin, scale=0.5), [tmpb], [s1b])
            kb.op("act", lambda e: e.activation(out=tmp[:], in_=tmp[:], func=AF.Sin, scale=-0.5, bias=hp[:, 0:1]), [tmpb, hpb], [tmpb])
            kb.op("dve", lambda e: e.scalar_tensor_tensor(out=dst[:], in0=s1[:], scalar=2.0, in1=tmp[:], op0=ALU.mult, op1=ALU.mult),
                  [s1b, tmpb], [dstb])
        kb.op("dve", lambda e: e.tensor_scalar(out=g.sin_t[:], in0=g.sin_t[:], scalar1=g.ropec[:, 1:2], scalar2=None,
                                               op0=ALU.mult), [g.sin_tb, g.cstb], [g.sin_tb])
        kb.barrier()


SM_LAM = 0
SM_SUBLN = 4
SM_FB = 6
SM_DTB = 7
SM_ALOG = 8
SM_DSKIP = 9
SM_NORMG = 57
SM_CONV = 105
SMALL_COLS = 105 + 200


def mixer_phase(kb, g, i):
    kind, j = i % 3, i // 3
    es0 = ExitStack()
    sm, smb = kb.sb(es0, "small", [128, SMALL_COLS], F32)
    kb.dma("sp", sm[:], g.in_small[i], [], [smb])
    g.sm, g.smb = sm, smb
    with ExitStack() as es1:
        tails = kb.sb(es1, "tails", [128, KC, 3], BF16)
        rmsnorm_fm(kb, g, g.xres, g.gcol["mix"][i], g.hT, T, tails=tails if kind == 2 else None)
        if kind == 2:
            exchange_halo(kb, g, tails)
    rmsnorm_fm(kb, g, g.memT, g.gcol["mem"][i], g.hmT, MEMT)
    with ExitStack() as es:
        stg = Stager(kb, es, 3, BF16, "mstg")

        def evac_m(mode, wi, c0, ws, th, pt, pb):
            o, ob = stg.get()
            if mode == "fm":
                kb.op("act", lambda e: e.activation(out=o[0:ws, 0:MEMT], in_=pt[0:ws, 0:MEMT], func=AF.Copy), [pb], [ob])
                kb.dma("pool", g.kmT[c0:c0 + ws, :], o[0:ws, 0:MEMT], [ob], [])
            else:
                th_, tt = th
                kb.op("act", lambda e: e.activation(out=o[:, 0:ws], in_=pt[:, 0:ws], func=AF.Copy), [pb], [ob])
                kb.dma("pool", g.vm[tt * 128:(tt + 1) * 128, c0 - MEMW:c0 - MEMW + ws], o[:, 0:ws], [ob], [])

        def ld_m(th, rhs, rhsb):
            kb.dma("sp", rhs[:, :, 0:MEMT], g.hmT.rearrange("(kc p) t -> p kc t", p=128), [], [rhsb])

        tiles = [("fm", 0, c0, 256) for c0 in range(0, MEMW, 256)] + [("tm", 0, c0, 256) for c0 in range(MEMW, 2 * MEMW, 256)]
        dense(kb, g, ld_m, KC, [g.w_mkv[i]], tiles, evac_m, nth=1, ntok=MEMT)
    inproj(kb, g, i, kind)
    if kind == 2:
        ssd_post(kb, g)
    attention(kb, g, i, kind)
    memattn(kb, g)
    es0.close()
    kb.barrier()
    residual_dense(kb, g, g.catT, g.w_out[i])


def inproj(kb, g, i, kind):
    sm, smb = g.sm, g.smb
    with ExitStack() as es:
        stg = Stager(kb, es, 4, BF16, "pstg")
        q32s = [kb.sb(es, "q32", [32, TH], F32) for _ in range(2)]
        rt = [kb.sb(es, "rt", [32, TH], F32) for _ in range(4)]
        f32s = [kb.sb(es, "f32s", [128, 3 + TH], F32) for _ in range(3)]
        accs = [kb.sb(es, "iacc", [128, TH], F32) for _ in range(2)]
        st = {"n": 0}
        pend = {}
        if kind == 0:
            segs = {"q": (0, TOKW), "k": (TOKW, 2 * TOKW), "v": (2 * TOKW, 3 * TOKW), "m": (3 * TOKW, A_COLS)}
        elif kind == 1:
            segs = {"q": (0, TOKW), "k": (TOKW, 2 * TOKW), "v": (2 * TOKW, 3 * TOKW), "f": (3 * TOKW, 3 * TOKW + 24),
                    "m": (3 * TOKW + 24, B_COLS)}
        else:
            segs = {"z": (0, TOKW), "x": (TOKW, TOKW + 5120), "d": (TOKW + 5120, TOKW + 5168), "m": (TOKW + 5168, C_COLS)}

        def seg_of(c0):
            for k, (a, b) in segs.items():
                if a <= c0 < b:
                    return k, a
            raise ValueError(c0)

        def store(o, ob, ws, dst):
            kb.dma("pool", dst, o, [ob], [])

        def halo_evac(wi, c0, ws, th, ph, phb):
            sg, a = seg_of(c0)
            if sg != "x":
                return
            ue, ueb = f32s[st["n"] % 3]
            st["n"] += 1
            kb.op("act", lambda e: e.activation(out=ue[:, 0:3], in_=ph[:, 0:3], func=AF.Copy), [phb], [ueb])
            pend[c0] = (ue, ueb)

        def evac(mode, wi, c0, ws, th, pt, pb):
            sg, a = seg_of(c0)
            if mode == "tm":
                th_, tt = th
                o, ob = stg.get()
                kb.op("act", lambda e: e.activation(out=o[:, 0:ws], in_=pt[:, 0:ws], func=AF.Copy), [pb], [ob])
                t0 = th_ * TH + tt * 128
                store(o[:, 0:ws], ob, ws, g.vb[t0:t0 + 128, c0 - a:c0 - a + ws])
                return
            tsl = slice(th * TH, (th + 1) * TH)
            if sg in ("q", "k") and kind == 0:
                o, ob = stg.get()
                q3, q3b = q32s[st["n"] % 2]
                st["n"] += 1
                kb.op("act", lambda e: e.activation(out=q3[:], in_=pt[0:32, 0:TH], func=AF.Copy), [pb], [q3b])
                kb.op("act", lambda e: e.activation(out=o[32:64, :], in_=pt[32:64, 0:TH], func=AF.Copy), [pb], [ob])
                kb.op("act", lambda e: e.activation(out=o[64:128, :], in_=pt[64:128, 0:TH], func=AF.Copy), [pb], [ob])
                sw, swb = kb.hps2()
                kb.op("pe", lambda e: e.matmul(sw[0:32, 0:TH], lhsT=g.swp, rhs=q3[:], start=True, stop=True), [q3b, g.cstb], [swb])
                t1, t1b = rt[st["n"] % 4]
                t2, t2b = rt[(st["n"] + 2) % 4]
                kb.op("dve", lambda e: e.tensor_tensor(out=t1[:], in0=q3[:], in1=g.cos_t[:, tsl], op=ALU.mult), [q3b, g.cos_tb], [t1b])
                kb.op("dve", lambda e: e.tensor_tensor(out=t2[:], in0=sw[0:32, 0:TH], in1=g.sin_t[:, tsl], op=ALU.mult), [swb, g.sin_tb], [t2b])
                kb.op("dve", lambda e: e.tensor_tensor(out=o[0:32, :], in0=t1[:], in1=t2[:], op=ALU.add), [t1b, t2b], [ob])
                dst = (g.qT if sg == "q" else g.kTb)[c0 - a:c0 - a + ws, tsl]
                store(o[0:ws, :], ob, ws, dst)
            elif sg in ("q", "k", "m"):
                o, ob = stg.get()
                kb.op("act", lambda e: e.activation(out=o[0:ws, :], in_=pt[0:ws, 0:TH], func=AF.Copy), [pb], [ob])
                dst = {"q": g.qT, "k": g.kTb, "m": g.qmT}[sg][c0 - a:c0 - a + ws, tsl]
                store(o[0:ws, :], ob, ws, dst)
            elif sg == "f":
                e1, e1b = f32s[st["n"] % 3]
                st["n"] += 1
                kb.op("act", lambda e: e.activation(out=e1[0:24, 0:TH], in_=pt[0:24, 0:TH], func=AF.Exp, scale=-1.0,
                                                    bias=g.negfb[:, 0:1]), [pb, g.negfbb], [e1b])
                kb.op("act", lambda e: e.activation(out=e1[0:24, 0:TH], in_=e1[0:24, 0:TH], func=AF.Ln, bias=g.onecol[0:24, :]),
                      [e1b, g.ones_fb], [e1b])
                kb.op("dve", lambda e: e.tensor_scalar(out=g.aHM[0:24, tsl], in0=e1[0:24, 0:TH], scalar1=-1.0, scalar2=None,
                                                       op0=ALU.mult), [e1b], [g.aHMb])
            elif sg == "d":
                e1, e1b = f32s[st["n"] % 3]
                st["n"] += 1
                kb.op("act", lambda e: e.activation(out=e1[0:48, 0:TH], in_=pt[0:48, 0:TH], func=AF.Exp,
                                                    bias=sm[0:48, SM_DTB:SM_DTB + 1]), [pb, smb], [e1b])
                kb.op("act", lambda e: e.activation(out=g.dtHM[:, tsl], in_=e1[0:48, 0:TH], func=AF.Ln, bias=g.onecol[0:48, :]),
                      [e1b, g.ones_fb], [g.dtHMb])
                kb.op("dve", lambda e: e.tensor_scalar(out=g.aHM[:, tsl], in0=g.dtHM[:, tsl], scalar1=g.negA[:, 0:1], scalar2=None,
                                                       op0=ALU.mult), [g.dtHMb, g.negAb], [g.aHMb])
            elif sg == "z":
                o, ob = stg.get()
                kb.op("act", lambda e: e.activation(out=o[0:ws, :], in_=pt[0:ws, 0:TH], func=AF.Silu), [pb], [ob])
                store(o[0:ws, :], ob, ws, g.szT[c0 - a:c0 - a + ws, tsl])
            elif sg == "x":
                ue, ueb = pend.pop(c0)
                kb.op("act", lambda e: e.activation(out=ue[:, 3:3 + TH], in_=pt[:, 0:TH], func=AF.Copy), [pb], [ueb])
                ac, acb = accs[st["n"] % 2]
                st["n"] += 1
                blk = (c0 - a) // 128
                cwv = sm[:, SM_CONV:SM_CONV + 200].rearrange("p (b k) -> p b k", k=5)
                conv_silu(kb, ue, ueb, 4, cwv, smb, blk, ac, acb, TH)
                if blk < 24:
                    xo, xob = f32s[st["n"] % 3]
                    st["n"] += 1
                    kb.op("act", lambda e: e.activation(out=xo[:, 0:TH], in_=ac[:], func=AF.Silu), [acb], [xob])
                    kb.dma("pool", g.xsT[blk * 128:(blk + 1) * 128, tsl], xo[:, 0:TH], [xob], [])
                else:
                    o, ob = stg.get()
                    kb.op("act", lambda e: e.activation(out=o[:], in_=ac[:], func=AF.Silu), [acb], [ob])
                    if blk < 32:
                        store(o[:], ob, 128, g.kTb[(blk - 24) * 128:(blk - 23) * 128, tsl])
                    else:
                        store(o[:], ob, 128, g.qT[(blk - 32) * 128:(blk - 31) * 128, tsl])

        tiles = []
        for sgn, (a, b) in segs.items():
            mode = "tm" if sgn == "v" else "fm"
            for c0 in range(a, b, 256):
                tiles.append((mode, 0, c0, min(256, b - c0)))
        halo = 3 if kind == 2 else 0
        if kind == 1:
            g.negfb, g.negfbb = kb.sb(es, "negfb", [24, 1], F32)
            kb.op("dve", lambda e: e.tensor_scalar(out=g.negfb[:], in0=sm[0:24, SM_FB:SM_FB + 1], scalar1=-1.0, scalar2=None, op0=ALU.mult),
                  [smb], [g.negfbb])
        if kind == 2:
            g.negA, g.negAb = kb.sb(es, "negA", [48, 1], F32)
            kb.op("act", lambda e: e.activation(out=g.negA[:], in_=sm[0:48, SM_ALOG:SM_ALOG + 1], func=AF.Exp), [smb], [g.negAb])
            kb.op("dve", lambda e: e.tensor_scalar(out=g.negA[:], in0=g.negA[:], scalar1=-1.0, scalar2=None, op0=ALU.mult),
                  [g.negAb], [g.negAb])
        dense(kb, g, make_rhs_loader(kb, g, g.hT, KC, halo), KC, [g.w_in[i]], tiles, evac, halo=halo,
              halo_evac=halo_evac if halo else None)


def ssd_post(kb, g):
    with ExitStack() as es:
        dtTM, dtTMb = kb.sb(es, "dtTM", [128, 8, 48], F32)
        for tt in range(8):
            pt, pb = kb.ps()
            kb.op("pe", lambda e: e.transpose(out=pt[:, 0:48], in_=g.dtHM[:, tt * 128:(tt + 1) * 128], identity=g.ident[0:48, 0:48]),
                  [g.dtHMb, g.cstb], [pb])
            kb.op("act", lambda e: e.activation(out=dtTM[:, tt, :], in_=pt[:, 0:48], func=AF.Copy), [pb], [dtTMb])
        xs = [kb.sb(es, "xsl", [128, T], F32) for _ in range(2)]
        stg = Stager(kb, es, 3, BF16, "xstg", width=128)
        for cc in range(24):
            x, xb = xs[cc % 2]
            kb.dma("sp", x[:], g.xsT[cc * 128:(cc + 1) * 128, :], [], [xb])
            for tt in range(8):
                pt, pb = kb.ps()
                kb.op("pe", lambda e: e.transpose(out=pt[:, 0:128], in_=x[:, tt * 128:(tt + 1) * 128], identity=g.ident), [xb, g.cstb], [pb])
                o, ob = stg.get()
                for hh in range(2):
                    h = cc * 2 + hh
                    kb.op("dve", lambda e: e.tensor_scalar(out=o[:, hh * 64:(hh + 1) * 64], in0=pt[:, hh * 64:(hh + 1) * 64],
                                                           scalar1=dtTM[:, tt, h:h + 1], scalar2=None, op0=ALU.mult), [pb, dtTMb], [ob])
                kb.dma("pool", g.vb[tt * 128:(tt + 1) * 128, cc * 128:(cc + 1) * 128], o[:, 0:128], [ob], [])
    kb.barrier()


def attention(kb, g, i, kind):
    sm, smb = g.sm, g.smb
    NU = {0: 12, 1: 24, 2: 8}[kind]
    NM = 2 if kind == 0 else 1
    E = {0: 256, 1: 128, 2: 384}[kind]
    KR = 1024 if kind == 2 else TOKW
    H = {0: 0, 1: 24, 2: 48}[kind]
    scale = 128.0 ** -0.5
    kb.barrier()
    kch = gather_chunks(kb, "kg", g.kTb[0:KR, :], KR, T, BF16, "r", 512)
    vch = gather_chunks(kb, "vg", g.vb, T, TOKW, BF16, "c", 512)
    es = ExitStack()
    if H:
        cumHM, cumHMb = kb.sb(es, "cumHM", [48, T], F32)
        onesT, onesTb = kb.sb(es, "onesT", [48, T], F32)
        zc, zcb = kb.sb(es, "zc", [48, 1], F32)
        cumown, cumownb = kb.sb(es, "cumown", [128, 8, 48], F32)
        cumpre, cumpreb = kb.sb(es, "cumpre", [128, NPRE, 48], F32)
        totb, totbb = kb.sb(es, "totb", [128, 7, 48], F32)
        offb, offbb = kb.sb(es, "offb", [128, 7, 48], F32)
        offown, offownb = kb.sb(es, "offown", [128, 48], F32)
        doff, doffb = kb.sb(es, "doff", [128, 7, 48], F32)
        kb.op("dve", lambda e: e.memset(onesT[:], 1.0), [], [onesTb])
        kb.op("dve", lambda e: e.memset(zc[:], 0.0), [], [zcb])
        kb.op("dve", lambda e: e.memset(cumown[:], 0.0), [], [cumownb])
        kb.op("dve", lambda e: e.tensor_tensor_scan(out=cumHM[0:H, :], data0=onesT[0:H, :], data1=g.aHM[0:H, :], initial=zc[0:H, :],
                                                    op0=ALU.mult, op1=ALU.add), [onesTb, g.aHMb, zcb], [cumHMb])
        kb.dma("sp", g.cumHMd[0:H, :], cumHM[0:H, :], [cumHMb], [])
        for tt in range(8):
            pt, pb = kb.ps()
            kb.op("pe", lambda e: e.transpose(out=pt[:, 0:H], in_=cumHM[0:H, tt * 128:(tt + 1) * 128], identity=g.ident[0:H, 0:H]),
                  [cumHMb, g.cstb], [pb])
            kb.op("act", lambda e: e.activation(out=cumown[:, tt, 0:H], in_=pt[:, 0:H], func=AF.Copy), [pb], [cumownb])
        kb.dma("sp", g.cumb.rearrange("(j p) h -> p j h", p=128), cumown[:], [cumownb], [])
        kb.barrier()
        (lo, hi, CA, CB, cbufs) = gather_chunks(kb, "cg", g.cumb, T, 48, F32, "r", T)[0]
        bufsel = {"A": CA, "B": CB}
        for (bn, s0, r0) in PAIRS:
            n = 2 if r0 < 6 else 1
            src = bufsel[bn][s0 * T:(s0 + n) * T, :].rearrange("(x p) h -> p x h", p=128)
            kb.dma("sp", cumpre[:, r0 * 8:(r0 + n) * 8, :], src, list(cbufs), [cumpreb])
            for k in range(n):
                row = (s0 + k) * T + T - 1
                kb.dma("pool", totb[:, r0 + k, :], bufsel[bn][row, :].partition_broadcast(128), list(cbufs), [totbb])
        kb.op("dve", lambda e: e.memset(offb[:], 0.0), [], [offbb])
        for r in range(1, 7):
            kb.op("dve", lambda e: e.tensor_tensor(out=offb[:, r, :], in0=offb[:, r - 1, :], in1=totb[:, r - 1, :], op=ALU.add),
                  [offbb, totbb], [offbb])
        kb.op("dve", lambda e: e.tensor_scalar(out=offown[:], in0=totb[:, 0, :], scalar1=g.ctab[:, 72:73], scalar2=None, op0=ALU.mult),
              [totbb, g.ctabb], [offownb])
        for r in range(1, 7):
            kb.op("dve", lambda e: e.scalar_tensor_tensor(out=offown[:], in0=totb[:, r, :], scalar=g.ctab[:, 72 + r:73 + r], in1=offown[:],
                                                          op0=ALU.mult, op1=ALU.add), [totbb, offownb, g.ctabb], [offownb])
        for r in range(7):
            kb.op("dve", lambda e: e.tensor_tensor(out=doff[:, r, :], in0=offown[:], in1=offb[:, r, :], op=ALU.subtract),
                  [offownb, offbb], [doffb])
    g.hook()
    if kind == 0:
        lam, lamb = kb.sb(es, "lam", [128, 4], F32)
        kb.op("dve", lambda e: e.tensor_tensor(out=lam[:, 0:1], in0=sm[:, SM_LAM:SM_LAM + 1], in1=sm[:, SM_LAM + 1:SM_LAM + 2], op=ALU.mult), [smb], [lamb])
        kb.op("dve", lambda e: e.tensor_tensor(out=lam[:, 1:2], in0=sm[:, SM_LAM + 2:SM_LAM + 3], in1=sm[:, SM_LAM + 3:SM_LAM + 4], op=ALU.mult), [smb], [lamb])
        pt, pb = kb.ps()
        kb.op("pe", lambda e: e.matmul(pt[:, 0:2], lhsT=g.ones_f[:], rhs=lam[:, 0:2], start=True, stop=True), [lamb, g.ones_fb], [pb])
        kb.op("act", lambda e: e.activation(out=lam[:, 2:4], in_=pt[:, 0:2], func=AF.Exp), [pb], [lamb])
        kb.op("dve", lambda e: e.tensor_tensor(out=lam[:, 0:1], in0=lam[:, 3:4], in1=lam[:, 2:3], op=ALU.subtract), [lamb], [lamb])
        kb.op("dve", lambda e: e.tensor_scalar(out=lam[:, 0:1], in0=lam[:, 0:1], scalar1=-lambda_init(i), scalar2=None, op0=ALU.add), [lamb], [lamb])
        subg, subgb = kb.sb(es, "subg", [128, 2], F32)
        kb.op("dve", lambda e: e.tensor_scalar(out=subg[:], in0=sm[:, SM_SUBLN:SM_SUBLN + 2], scalar1=1.0 - lambda_init(i), scalar2=None,
                                               op0=ALU.mult), [smb], [subgb])

    Kpre = [kb.sb(es, "Kpre", [128, 7, T], BF16) for _ in range(NM)]
    Kown = [kb.sb(es, "Kown", [128, T], BF16) for _ in range(NM)]
    Vpre, Vpreb = kb.sb(es, "Vpre", [128, NPRE, E], BF16)
    Vown, Vownb = kb.sb(es, "Vown", [128, 8, E], BF16)
    qt = [kb.sb(es, "qt", [128, TH], BF16) for _ in range(2 * NM)]
    pts = [kb.sb(es, "pT", [128, TH], BF16) for _ in range(3)]
    tmps = [kb.sb(es, "atmp", [128, TH], F32) for _ in range(3)]
    npt = [0]
    NH = 6 if kind == 2 else 1
    if H:
        cqb = [kb.sb(es, "cqb", [128, TH], F32) for _ in range(NH)]
        bpre = [kb.sb(es, "bpre", [128, NPRE], F32) for _ in range(NH)]
        bown = [kb.sb(es, "bown", [128, 8], F32) for _ in range(NH)]
    ostg = Stager(kb, es, 3, BF16, "aostg")
    fin = [kb.sb(es, "afin", [128, TH], F32) for _ in range(6)]
    rden = [kb.sb(es, "rden", [128, TH], F32) for _ in range(2)]
    PS = kb.psum
    zero_col = None

    for u in range(NU):
        for m in range(NM):
            blk = (2 * u + m) if kind == 0 else u
            (lo, hi, A, B, kbufs) = kch[blk // 4]
            bsel = {"A": A, "B": B}
            Kp, Kpb = Kpre[m]
            for (bn, s0, r0) in PAIRS:
                n = 2 if r0 < 6 else 1
                v4 = bsel[bn].rearrange("(s c p) t -> p s c t", s=4, c=4, p=128)
                kb.dma("sp" if r0 % 4 == 0 else "pool", Kp[:, r0:r0 + n, :], v4[:, s0:s0 + n, blk % 4, :], list(kbufs), [Kpb])
            Ko, Kob = Kown[m]
            kb.dma("sp", Ko[:], g.kTb[blk * 128:(blk + 1) * 128, :], [], [Kob])
        c_lo, c_hi = u * E, (u + 1) * E
        Vp5 = Vpre[:].rearrange("p (r j) e -> p r j e", j=8)
        for (lo, hi, A, B, vbufs) in vch:
            a0, a1 = max(lo, c_lo), min(hi, c_hi)
            if a0 >= a1:
                continue
            bsel = {"A": A, "B": B}
            for (bn, s0, r0) in PAIRS:
                n = 2 if r0 < 6 else 1
                v4 = bsel[bn].rearrange("(s j p) c -> p s j c", s=4, j=8, p=128)
                kb.dma("sp" if r0 % 4 == 0 else "pool", Vp5[:, r0:r0 + n, :, a0 - c_lo:a1 - c_lo], v4[:, s0:s0 + n, :, a0 - lo:a1 - lo],
                       list(vbufs), [Vpreb])
        kb.dma("sp", Vown[:], g.vb[:, c_lo:c_hi].rearrange("(j p) c -> p j c", p=128), [], [Vownb])
        if H:
            for hh in range(NH):
                h = u * NH + hh
                bp, bpb = bpre[hh]
                bo, bob = bown[hh]
                kb.op("dve", lambda e: e.scalar_tensor_tensor(out=bp[:], in0=cumpre[:, :, h], scalar=-1.0, in1=g.ctab[:, 0:NPRE],
                                                              op0=ALU.mult, op1=ALU.add), [cumpreb, g.ctabb], [bpb])
                for r in range(7):
                    kb.op("dve", lambda e: e.tensor_scalar(out=bp[:, r * 8:(r + 1) * 8], in0=bp[:, r * 8:(r + 1) * 8],
                                                           scalar1=doff[:, r, h:h + 1], scalar2=None, op0=ALU.add), [bpb, doffb], [bpb])
                kb.op("dve", lambda e: e.tensor_scalar(out=bo[:], in0=cumown[:, :, h], scalar1=-1.0, scalar2=None, op0=ALU.mult),
                      [cumownb], [bob])
        for th in range(NTH):
            tsl = slice(th * TH, (th + 1) * TH)
            qs = []
            for m in range(NM):
                blk = (2 * u + m) if kind == 0 else u
                q_, qb_ = qt[(th * NM + m) % len(qt)]
                kb.dma("sp", q_[:], g.qT[blk * 128:(blk + 1) * 128, tsl], [], [qb_])
                qs.append((q_, qb_))
            if H:
                for hh in range(NH):
                    h = u * NH + hh
                    kb.dma("pool", cqb[hh][0][:], g.cumHMd[h, tsl].partition_broadcast(128), [], [cqb[hh][1]])
            slots = []
            for s_ in range(NPRE):
                slots.append(("pre", s_, None))
            for j in range(4 * th + 4):
                slots.append(("own", j, (j - 4 * th) if j >= 4 * th else None))
            nsl = len(slots)
            if kind == 0:
                obank = [[PS[0], PS[1]], [PS[2], PS[3]]]
                dbank = [PS[4], PS[5]]
                sbank = [PS[6], PS[7]]
            elif kind == 1:
                obank = [[PS[0]]]
                dbank = [PS[1]]
                sbank = [PS[2], PS[3], PS[4]]
            else:
                ybank = [PS[0], PS[1], PS[2], PS[3], PS[4], PS[5]]
                sbank = [PS[6], PS[7]]
            for si, (typ, idx, diag) in enumerate(slots):
                first, last = (si == 0), (si == nsl - 1)
                for m in range(NM):
                    if typ == "pre":
                        Kt = Kpre[m][0][:, idx // 8, (idx % 8) * 128:(idx % 8 + 1) * 128]
                        Ktb = Kpre[m][1]
                        Vt = lambda c0, c1: Vpre[:, idx, c0:c1]
                        Vtb = Vpreb
                    else:
                        Kt = Kown[m][0][:, idx * 128:(idx + 1) * 128]
                        Ktb = Kown[m][1]
                        Vt = lambda c0, c1: Vown[:, idx, c0:c1]
                        Vtb = Vownb
                    sT, sTb = sbank[(si * NM + m) % len(sbank)]
                    kb.op("pe", lambda e: e.matmul(sT[:, 0:TH], lhsT=Kt, rhs=qs[m][0][:], start=True, stop=True), [Ktb, qs[m][1]], [sTb])
                    if kind == 0:
                        p_, pb_ = pts[npt[0] % 3]
                        npt[0] += 1
                        if diag is not None:
                            t_, tb_ = tmps[npt[0] % 3]
                            kb.op("dve", lambda e: e.tensor_tensor(out=t_[:], in0=sT[:, 0:TH], in1=g.maskadd[diag], op=ALU.add), [sTb, g.cstb], [tb_])
                            kb.op("act", lambda e: e.activation(out=p_[:], in_=t_[:], func=AF.Exp, scale=scale), [tb_], [pb_])
                        elif typ == "pre":
                            kb.op("act", lambda e: e.activation(out=p_[:], in_=sT[:, 0:TH], func=AF.Exp, scale=scale, bias=g.ctab[:, idx:idx + 1]),
                                  [sTb, g.ctabb], [pb_])
                        else:
                            kb.op("act", lambda e: e.activation(out=p_[:], in_=sT[:, 0:TH], func=AF.Exp, scale=scale), [sTb], [pb_])
                        kb.op("pe", lambda e: e.matmul(dbank[m][0][:, 0:TH], lhsT=g.ones_h[:], rhs=p_[:], start=first, stop=last), [pb_, g.ones_hb], [dbank[m][1]])
                        for ec in range(2):
                            kb.op("pe", lambda e: e.matmul(obank[m][ec][0][:, 0:TH], lhsT=Vt(ec * 128, (ec + 1) * 128), rhs=p_[:], start=first, stop=last),
                                  [pb_, Vtb], [obank[m][ec][1]])
                    elif kind == 1:
                        bcol = (bpre[0][0][:, idx:idx + 1], bpre[0][1]) if typ == "pre" else (bown[0][0][:, idx:idx + 1], bown[0][1])
                        t_, tb_ = tmps[npt[0] % 3]
                        p_, pb_ = pts[npt[0] % 3]
                        npt[0] += 1
                        kb.op("dve", lambda e: e.scalar_tensor_tensor(out=t_[:], in0=sT[:, 0:TH], scalar=scale, in1=cqb[0][0][:], op0=ALU.mult, op1=ALU.add),
                              [sTb, cqb[0][1]], [tb_])
                        if diag is not None:
                            kb.op("dve", lambda e: e.tensor_tensor(out=t_[:], in0=t_[:], in1=g.maskadd[diag], op=ALU.add), [tb_, g.cstb], [tb_])
                        kb.op("act", lambda e: e.activation(out=p_[:], in_=t_[:], func=AF.Exp, bias=bcol[0]), [tb_, bcol[1]], [pb_])
                        kb.op("pe", lambda e: e.matmul(dbank[0][0][:, 0:TH], lhsT=g.ones_h[:], rhs=p_[:], start=first, stop=last), [pb_, g.ones_hb], [dbank[0][1]])
                        kb.op("pe", lambda e: e.matmul(obank[0][0][0][:, 0:TH], lhsT=Vt(0, 128), rhs=p_[:], start=first, stop=last), [pb_, Vtb], [obank[0][0][1]])
                    else:
                        for hh in range(6):
                            bcol = (bpre[hh][0][:, idx:idx + 1], bpre[hh][1]) if typ == "pre" else (bown[hh][0][:, idx:idx + 1], bown[hh][1])
                            t_, tb_ = tmps[npt[0] % 3]
                            p_, pb_ = pts[npt[0] % 3]
                            npt[0] += 1
                            if diag is not None:
                                kb.op("dve", lambda e: e.tensor_tensor(out=t_[:], in0=cqb[hh][0][:], in1=g.maskadd[diag], op=ALU.add), [cqb[hh][1], g.cstb], [tb_])
                                kb.op("act", lambda e: e.activation(out=t_[:], in_=t_[:], func=AF.Exp, bias=bcol[0]), [tb_, bcol[1]], [tb_])
                            else:
                                kb.op("act", lambda e: e.activation(out=t_[:], in_=cqb[hh][0][:], func=AF.Exp, bias=bcol[0]), [cqb[hh][1], bcol[1]], [tb_])
                            kb.op("dve", lambda e: e.tensor_tensor(out=p_[:], in0=sT[:, 0:TH], in1=t_[:], op=ALU.mult), [sTb, tb_], [pb_])
                            yb = ybank[hh]
                            kb.op("pe", lambda e: e.matmul(yb[0][0:64, 0:TH], lhsT=Vt(hh * 64, (hh + 1) * 64), rhs=p_[:],
                                                            start=first, stop=last), [pb_, Vtb], [yb[1]])
            if kind == 0:
                for m in range(2):
                    kb.op("dve", lambda e: e.reciprocal(out=rden[m][0][:], in_=dbank[m][0][:, 0:TH]), [dbank[m][1]], [rden[m][1]])
                outs = []
                for ec in range(2):
                    t1, t1b = fin[ec * 2]
                    t2, t2b = fin[ec * 2 + 1]
                    kb.op("dve", lambda e: e.tensor_tensor(out=t1[:], in0=obank[0][ec][0][:, 0:TH], in1=rden[0][0][:], op=ALU.mult), [obank[0][ec][1], rden[0][1]], [t1b])
                    kb.op("dve", lambda e: e.tensor_tensor(out=t2[:], in0=obank[1][ec][0][:, 0:TH], in1=rden[1][0][:], op=ALU.mult), [obank[1][ec][1], rden[1][1]], [t2b])
                    kb.op("dve", lambda e: e.scalar_tensor_tensor(out=t1[:], in0=t2[:], scalar=lam[:, 0:1], in1=t1[:], op0=ALU.mult, op1=ALU.add), [t2b, t1b, lamb], [t1b])
                    kb.op("act", lambda e: e.activation(out=t2[:], in_=t1[:], func=AF.Square), [t1b], [t2b])
                    kb.op("pe", lambda e: e.matmul(PS[6][0][:, 0:TH], lhsT=g.ones_f[:], rhs=t2[:], start=(ec == 0), stop=(ec == 1)), [t2b, g.ones_fb], [PS[6][1]])
                    outs.append((t1, t1b))
                rs_, rsb = fin[4]
                kb.op("dve", lambda e: e.tensor_scalar(out=rs_[:], in0=PS[6][0][:, 0:TH], scalar1=1.0 / 256, scalar2=EPS, op0=ALU.mult, op1=ALU.add), [PS[6][1]], [rsb])
                kb.op("act", lambda e: e.activation(out=rs_[:], in_=rs_[:], func=AF.Sqrt), [rsb], [rsb])
                kb.op("dve", lambda e: e.reciprocal(out=rs_[:], in_=rs_[:]), [rsb], [rsb])
                for ec in range(2):
                    o, ob = ostg.get()
                    kb.op("dve", lambda e: e.scalar_tensor_tensor(out=o[:], in0=outs[ec][0][:], scalar=subg[:, ec:ec + 1], in1=rs_[:], op0=ALU.mult, op1=ALU.mult),
                          [outs[ec][1], rsb, subgb], [ob])
                    kb.dma("pool", g.catT[u * 256 + ec * 128:u * 256 + (ec + 1) * 128, tsl], o[:], [ob], [])
            elif kind == 1:
                kb.op("dve", lambda e: e.reciprocal(out=rden[0][0][:], in_=dbank[0][0][:, 0:TH]), [dbank[0][1]], [rden[0][1]])
                o, ob = ostg.get()
                kb.op("dve", lambda e: e.tensor_tensor(out=o[:], in0=obank[0][0][0][:, 0:TH], in1=rden[0][0][:], op=ALU.mult), [obank[0][0][1], rden[0][1]], [ob])
                kb.dma("pool", g.catT[u * 128:(u + 1) * 128, tsl], o[:], [ob], [])
            else:
                gs = []
                for hh in range(6):
                    h = u * 6 + hh
                    xs_, xsb = fin[hh]
                    sz_, szb = ostg.get()
                    kb.dma("sp", xs_[0:64, :], g.xsT[h * 64:(h + 1) * 64, tsl], [], [xsb])
                    kb.dma("sp", sz_[0:64, :], g.szT[h * 64:(h + 1) * 64, tsl], [], [szb])
                    kb.op("dve", lambda e: e.scalar_tensor_tensor(out=xs_[0:64, :], in0=xs_[0:64, :], scalar=sm[0:64, SM_DSKIP + h:SM_DSKIP + h + 1],
                                                                  in1=ybank[hh][0][0:64, 0:TH], op0=ALU.mult, op1=ALU.add), [xsb, smb, ybank[hh][1]], [xsb])
                    kb.op("dve", lambda e: e.tensor_tensor(out=xs_[0:64, :], in0=xs_[0:64, :], in1=sz_[0:64, :], op=ALU.mult), [xsb, szb], [xsb])
                    sq_, sqb = tmps[hh % 3]
                    kb.op("act", lambda e: e.activation(out=sq_[0:64, :], in_=xs_[0:64, :], func=AF.Square), [xsb], [sqb])
                    kb.op("pe", lambda e: e.matmul(PS[6][0][:, 0:TH], lhsT=g.ones_f[0:64, :], rhs=sq_[0:64, :], start=(hh == 0), stop=(hh == 5)),
                          [sqb, g.ones_fb], [PS[6][1]])
                    gs.append((xs_, xsb))
                rs_, rsb = rden[0]
                kb.op("dve", lambda e: e.tensor_scalar(out=rs_[:], in0=PS[6][0][:, 0:TH], scalar1=1.0 / 384, scalar2=EPS, op0=ALU.mult, op1=ALU.add), [PS[6][1]], [rsb])
                kb.op("act", lambda e: e.activation(out=rs_[:], in_=rs_[:], func=AF.Sqrt), [rsb], [rsb])
                kb.op("dve", lambda e: e.reciprocal(out=rs_[:], in_=rs_[:]), [rsb], [rsb])
                for hh in range(6):
                    h = u * 6 + hh
                    o, ob = ostg.get()
                    kb.op("dve", lambda e: e.scalar_tensor_tensor(out=o[0:64, :], in0=gs[hh][0][0:64, :], scalar=sm[0:64, SM_NORMG + h:SM_NORMG + h + 1],
                                                                  in1=rs_[0:64, :], op0=ALU.mult, op1=ALU.mult), [gs[hh][1], rsb, smb], [ob])
                    kb.dma("pool", g.catT[h * 64:(h + 1) * 64, tsl], o[0:64, :], [ob], [])
    es.close()
    kb.barrier()


def memattn(kb, g):
    scale = 256.0 ** -0.5
    PS = kb.psum
    with ExitStack() as es:
        km, kmb = kb.sb(es, "km", [128, 8, MEMT], BF16)
        vm, vmb = kb.sb(es, "vmm", [128, 2, MEMW], BF16)
        kb.dma("sp", km[:], g.kmT.rearrange("(c p) m -> p c m", p=128), [], [kmb])
        kb.dma("sp", vm[:], g.vm.rearrange("(c p) e -> p c e", p=128), [], [vmb])
        qm = [kb.sb(es, "qm", [128, TH], BF16) for _ in range(4)]
        pts = [kb.sb(es, "mp", [128, TH], BF16) for _ in range(4)]
        rd, rdb = kb.sb(es, "mrd", [128, TH], F32)
        ostg = Stager(kb, es, 3, BF16, "mostg")
        n = 0
        for hm in range(4):
            for th in range(NTH):
                tsl = slice(th * TH, (th + 1) * TH)
                qq = []
                for dc in range(2):
                    q_, qb_ = qm[n % 4]
                    n += 1
                    r0 = hm * 256 + dc * 128
                    kb.dma("sp", q_[:], g.qmT[r0:r0 + 128, tsl], [], [qb_])
                    qq.append((q_, qb_))
                pp = []
                for mc in range(2):
                    sT, sTb = PS[3 + mc]
                    for dc in range(2):
                        kb.op("pe", lambda e: e.matmul(sT[:, 0:TH], lhsT=km[:, hm * 2 + dc, mc * 128:(mc + 1) * 128], rhs=qq[dc][0][:],
                                                        start=(dc == 0), stop=(dc == 1)), [kmb, qq[dc][1]], [sTb])
                    p_, pb_ = pts[(n + mc) % 4]
                    kb.op("act", lambda e: e.activation(out=p_[:], in_=sT[:, 0:TH], func=AF.Exp, scale=scale), [sTb], [pb_])
                    pp.append((p_, pb_))
                for mc in range(2):
                    kb.op("pe", lambda e: e.matmul(PS[0][0][:, 0:TH], lhsT=g.ones_h[:], rhs=pp[mc][0][:], start=(mc == 0), stop=(mc == 1)),
                          [pp[mc][1], g.ones_hb], [PS[0][1]])
                for ec in range(2):
                    for mc in range(2):
                        c0 = hm * 256 + ec * 128
                        kb.op("pe", lambda e: e.matmul(PS[1 + ec][0][:, 0:TH], lhsT=vm[:, mc, c0:c0 + 128], rhs=pp[mc][0][:], start=(mc == 0), stop=(mc == 1)),
                              [pp[mc][1], vmb], [PS[1 + ec][1]])
                kb.op("dve", lambda e: e.reciprocal(out=rd[:], in_=PS[0][0][:, 0:TH]), [PS[0][1]], [rdb])
                for ec in range(2):
                    o, ob = ostg.get()
                    kb.op("dve", lambda e: e.tensor_tensor(out=o[:], in0=PS[1 + ec][0][:, 0:TH], in1=rd[:], op=ALU.mult), [PS[1 + ec][1], rdb], [ob])
                    r0 = TOKW + hm * 256 + ec * 128
                    kb.dma("pool", g.catT[r0:r0 + 128, tsl], o[:], [ob], [])
    kb.barrier()


def host_constants():
    cst = np.zeros((128, CST_COLS), np.float32)
    cst[:, 0:128] = np.eye(128, dtype=np.float32)
    p = np.arange(128)[:, None]
    f = np.arange(512)[None, :]
    for r in range(4):
        cst[:, 128 + r * 512:128 + (r + 1) * 512] = np.where(f >= r * 128 + p, 0.0, NEG)
    sw = np.zeros((32, 32), np.float32)
    for m in range(32):
        sw[(m + 16) % 32, m] = 1.0
    cst[0:32, 128 + 2048:128 + 2048 + 32] = sw
    half = 16
    invf = np.power(np.float32(ROPE_THETA), -np.arange(half, dtype=np.float32) * np.float32(2.0 / 32)).astype(np.float32)
    cst[0:32, 128 + 2048 + 32] = np.concatenate([invf, invf])
    cst[0:32, 128 + 2048 + 33] = np.concatenate([-np.ones(16), np.ones(16)])
    return cst


def host_ctab(c):
    t = np.zeros((128, 80), np.float32)
    for s in range(64):
        t[:, s] = 0.0 if s < 8 * c else NEG
    for j in range(8):
        t[:, 64 + j] = 1.0 if j == c - 1 else 0.0
        t[:, 72 + j] = 1.0 if j < c else 0.0
    return t


def pk(v):
    v = np.asarray(v, np.float32)
    return np.ascontiguousarray(v.reshape(-1, 128).T)


def host_inputs(inp, layers=(0, 1, 2, 3), parts=("mix", "ffn"), x_override=None):
    cst = host_constants()
    gains = np.concatenate([pk(inp["norm_mix"][i]) for i in range(4)] + [pk(inp["norm_ffn"][i]) for i in range(4)] +
                           [pk(inp["norm_mem"][i]) for i in range(4)] + [pk(inp["final_norm"])], axis=1)
    x = inp["x"][0] if x_override is None else x_override
    maps = []
    shared = {}
    for i in layers:
        cw = np.concatenate([inp["conv_ffn_w"][i].T, inp["conv_ffn_b"][i][:, None]], axis=1)
        shared["cwf%d" % i] = np.ascontiguousarray(cw.reshape(2 * FC, 128, 4).transpose(1, 0, 2))
    if "mix" in parts:
        shared.update(host_mixer_shared(inp, layers))
    for c in range(NCORES):
        m = dict(shared)
        m["xT"] = np.ascontiguousarray(x[c * T:(c + 1) * T, :].T)
        m["cst"] = cst
        m["ctab"] = host_ctab(c)
        m["gains"] = gains
        r0, r1 = c * 512, (c + 1) * 512
        for i in layers:
            if "ffn" in parts:
                m["w_upg%d" % i] = np.ascontiguousarray(inp["w_up"][i][r0:r1, :DFF])
                m["w_upv%d" % i] = np.ascontiguousarray(inp["w_up"][i][r0:r1, DFF:])
                m["w_down%d" % i] = np.ascontiguousarray(inp["w_down"][i][c * (DFF // 8):(c + 1) * (DFF // 8), :])
            if "mix" in parts:
                kind, j = i % 3, i // 3
                w = (inp["a_w_in"], inp["b_w_in"], inp["c_w_in"])[kind][j]
                m["w_in%d" % i] = np.ascontiguousarray(w[r0:r1, :])
                m["w_mkv%d" % i] = np.ascontiguousarray(inp["w_mem_kv"][i][r0:r1, :])
                m["w_out%d" % i] = np.ascontiguousarray(inp["w_out"][i][r0:r1, :])
        if "mix" in parts:
            m.update(host_mixer_percore(inp, c))
        maps.append(m)
    return maps


def host_mixer_shared(inp, layers):
    out = {"memT": np.ascontiguousarray(inp["mem"][0].T)}
    for i in layers:
        kind, j = i % 3, i // 3
        sm = np.zeros((128, SMALL_COLS), np.float32)
        if kind == 0:
            sm[:, SM_LAM:SM_LAM + 4] = inp["a_lambda"][j].T
            sm[:, SM_SUBLN:SM_SUBLN + 2] = pk(inp["a_subln"][j])
        elif kind == 1:
            sm[0:24, SM_FB] = inp["b_forget_bias"][j]
        else:
            sm[0:48, SM_DTB] = inp["c_dt_bias"][j]
            sm[0:48, SM_ALOG] = inp["c_a_log"][j]
            sm[:, SM_DSKIP:SM_DSKIP + 48] = np.broadcast_to(inp["c_d_skip"][j][None, :], (128, 48))
            sm[0:64, SM_NORMG:SM_NORMG + 48] = inp["c_norm_gate"][j].reshape(48, 64).T
            cw = np.concatenate([inp["c_conv_w"][j].T, inp["c_conv_b"][j][:, None]], axis=1)
            sm[:, SM_CONV:SM_CONV + 200] = cw.reshape(40, 128, 5).transpose(1, 0, 2).reshape(128, 200)
        out["small%d" % i] = sm
    return out


def host_mixer_percore(inp, c):
    return {"pos": np.ascontiguousarray(inp["positions"][:, c * T:(c + 1) * T].astype(np.int32))}


_NC_CACHE = {}


def kernel(**inputs):
    inp = {k: np.asarray(v) for k, v in inputs.items()}
    if "full" not in _NC_CACHE:
        _NC_CACHE["full"] = build()
    nc = _NC_CACHE["full"]
    maps = host_inputs(inp)
    res = run_bass_kernel_spmd(nc, maps, core_ids=list(range(NCORES)))
    out = np.concatenate([np.asarray(res.results[c]["yT"]).T for c in range(NCORES)], axis=0)
    return np.ascontiguousarray(out[None].astype(np.float32))
```

```python
import math
from contextlib import ExitStack
import numpy as np
import concourse.bass as bass
import concourse.mybir as mybir
from concourse.bass_utils import run_bass_kernel_spmd

F32 = mybir.dt.float32
BF16 = mybir.dt.bfloat16
ALU = mybir.AluOpType
AF = mybir.ActivationFunctionType

NCORES = 8
D = 4096
KC = D // 128
SEQ = 8192
T = SEQ // NCORES
TH = 512
NTH = T // TH
DEPTH = 4
MEMW = 1024
TOKW = 3072
MEMT = 256
DFF = 11008
FC = DFF // 128
EPS = 1e-6
NEG = -1.0e30
A_COLS = 3 * TOKW + MEMW
B_COLS = 3 * TOKW + 24 + MEMW
C_COLS = TOKW + 5120 + 48 + MEMW
NPRE = 56
ROPE_THETA = 500000.0


def lambda_init(i):
    return 0.8 - 0.6 * math.exp(-0.3 * i)


class Buf:
    __slots__ = ("w", "r", "name")

    def __init__(self, name=""):
        self.w = None
        self.r = {}
        self.name = name


class KB:
    def __init__(self, nc, es):
        self.nc = nc
        self.es = es
        self.eng = {"pe": nc.tensor, "act": nc.scalar, "dve": nc.vector, "pool": nc.gpsimd, "sp": nc.sync, "cc": nc.gpsimd}
        self.qmap = {"pool": "sp"}
        self.esem = {}
        self.ecnt = {}
        for q in ("pe", "act", "dve", "pool"):
            self.esem[q] = es.enter_context(nc.semaphore("s_" + q))
            self.ecnt[q] = 0
        self.waited = {q: {} for q in self.eng}
        self.waited["cc"] = self.waited["pool"]
        self.dpool = {}
        self.dnext = {}
        for q, n in (("sp", 32), ("cc", 16), ("act", 4)):
            self.dpool[q] = [[es.enter_context(nc.semaphore("d_%s%d" % (q, i))), 0] for i in range(n)]
            self.dnext[q] = 0
        self.ccpool = [[es.enter_context(nc.semaphore("s_cc%d" % i)), 0] for i in range(6)]
        self.ccnext = 0
        self.psum = []
        for i in range(8):
            t = es.enter_context(nc.psum_tensor("ps%d" % i, [128, 512], F32))
            self.psum.append((t, Buf("ps%d" % i)))
        self.psn = 0
        self.nuid = 0
        self.hslots = [Buf("hs%d" % i) for i in range(16)]
        self.hsn = 0

    def uid(self, p="t"):
        self.nuid += 1
        return "%s%d" % (p, self.nuid)

    def _wait(self, q, tok):
        if tok is None:
            return
        sem, val, src = tok
        if q == "pe" and src == "pe":
            return
        key = id(sem)
        if self.waited[q].get(key, 0) >= val:
            return
        self.eng[q].wait_ge(sem, val)
        self.waited[q][key] = val

    def deps(self, q, reads, writes):
        for b in reads:
            self._wait(q, b.w)
        for b in writes:
            self._wait(q, b.w)
            for t in b.r.values():
                self._wait(q, t)

    def done(self, tok, reads, writes):
        key = id(tok[0])
        for b in reads:
            old = b.r.get(key)
            if old is None or old[1] < tok[1]:
                b.r[key] = tok
        for b in writes:
            b.w = tok
            b.r = {}

    def op(self, q, fn, reads=(), writes=()):
        self.deps(q, reads, writes)
        ins = fn(self.eng[q])
        self.ecnt[q] += 1
        ins.then_inc(self.esem[q], 1)
        self.done((self.esem[q], self.ecnt[q], q), reads, writes)

    def dma(self, q, out, in_, reads=(), writes=()):
        q = self.qmap.get(q, q)
        pool = self.dpool[q]
        i = self.dnext[q]
        self.dnext[q] = (i + 1) % len(pool)
        sem, cnt = pool[i]
        if cnt > 0:
            self._wait(q, (sem, cnt, "dma"))
        self.deps(q, reads, writes)
        self.eng[q].dma_start(out=out, in_=in_).then_inc(sem, 16)
        pool[i][1] = cnt + 16
        self.done((sem, cnt + 16, "dma"), reads, writes)

    def allgather(self, groups, in_ap, out_ap, reads, writes):
        q = "cc"
        i = self.ccnext
        self.ccnext = (i + 1) % len(self.ccpool)
        sem, cnt = self.ccpool[i]
        if cnt > 0:
            self._wait(q, (sem, cnt, "cc"))
        self.deps(q, reads, writes)
        self.nc.gpsimd.collective_compute("AllGather", ALU.bypass, replica_groups=groups,
                                          ins=[in_ap], outs=[out_ap]).then_inc(sem, 1)
        self.ccpool[i][1] = cnt + 1
        self.done((sem, cnt + 1, "cc"), reads, writes)

    def barrier(self, full=False):
        toks = [(self.esem[q], self.ecnt[q], q) for q in self.esem if self.ecnt[q] > 0]
        for q in self.dpool:
            if q == "cc" and not full:
                continue
            for sem, cnt in self.dpool[q]:
                if cnt > 0:
                    toks.append((sem, cnt, "dma"))
        if full:
            for sem, cnt in self.ccpool:
                if cnt > 0:
                    toks.append((sem, cnt, "cc"))
        for q in ("pe", "act", "dve", "sp", "cc"):
            for t in toks:
                sem, val, src = t
                if src == q and q != "pe" and src in self.esem:
                    pass
                key = id(sem)
                if self.waited[q].get(key, 0) >= val:
                    continue
                self.eng[q].wait_ge(sem, val)
                self.waited[q][key] = val

    def ps(self):
        t, b = self.psum[self.psn % 7]
        self.psn += 1
        return t, b

    def hps(self):
        i = self.hsn % 16
        self.hsn += 1
        return self.psum[7][0][:, i * 8:(i + 1) * 8], self.hslots[i]

    def hps2(self):
        return self.psum[7]

    def sb(self, es, name, shape, dt):
        t = es.enter_context(self.nc.sbuf_tensor(self.uid(name), list(shape), dt))
        return t, Buf(name)

    def dram(self, name, shape, dt):
        t = self.nc.dram_tensor(self.uid(name), list(shape), dt)
        return t.ap(), Buf(name)


def two_stage_allgather(kb, src_ap, src_buf, rows, cols, dt, name):
    g1, g1b = kb.dram(name + "_g1", [4 * rows, cols], dt)
    g2, g2b = kb.dram(name + "_g2", [8 * rows, cols], dt)
    kb.allgather([[0, 1, 2, 3], [4, 5, 6, 7]], src_ap, g1, [src_buf], [g1b])
    kb.allgather([[0, 4], [1, 5], [2, 6], [3, 7]], g1, g2, [g1b], [g2b])
    return g2, g2b


class G:
    pass


class GW:
    def __init__(self, rs, cols, pw):
        self.rs, self.cols, self.pw = rs, cols, pw
        self.pieces = []
        self.cpr = (rs + 127) // 128
        self.nkc = 8 * self.cpr

    def ksz(self, kc):
        j = kc % self.cpr
        return min(128, self.rs - j * 128)

    def load(self, kb, qs, wt, wb, c0, w, k0, k1):
        nq = 0
        for (p0, pw, A, B, pbuf) in self.pieces:
            lo, hi = max(c0, p0), min(c0 + w, p0 + pw)
            if lo >= hi:
                continue
            for r in range(8):
                kr0, kr1 = r * self.cpr, (r + 1) * self.cpr
                a0, a1 = max(k0, kr0), min(k1, kr1)
                if a0 >= a1:
                    continue
                buf = A if r in (0, 1, 4, 5) else B
                slot = {0: 0, 1: 1, 4: 2, 5: 3, 2: 0, 3: 1, 6: 2, 7: 3}[r]
                base = slot * self.rs
                nfull = self.rs // 128
                f0, f1 = a0 - kr0, min(a1 - kr0, nfull)
                if f1 > f0:
                    src = buf[base + f0 * 128:base + f1 * 128, lo - p0:hi - p0].rearrange("(kc p) n -> p kc n", p=128)
                    kb.dma(qs[nq % len(qs)], wt[:, a0 - k0:a0 - k0 + (f1 - f0), lo - c0:hi - c0], src, list(pbuf), [wb])
                    nq += 1
                if a1 - kr0 > nfull:
                    rem = self.rs - nfull * 128
                    src = buf[base + nfull * 128:base + self.rs, lo - p0:hi - p0]
                    kb.dma(qs[nq % len(qs)], wt[0:rem, kr0 + nfull - k0, lo - c0:hi - c0], src, list(pbuf), [wb])
                    nq += 1


def prep_weight(kb, g, name, rows, cols, pw):
    nc = kb.nc
    src = nc.dram_tensor(name, [rows, cols], F32, kind="ExternalInput").ap()
    gw = GW(rows, cols, pw)
    gw.buf = Buf(name)
    for c0 in range(0, cols, pw):
        w = min(pw, cols - c0)
        bnc, bb = kb.dram(name + "_b", [rows, w], BF16)
        kb.dma("cc", bnc[:, :], src[:, c0:c0 + w], [], [bb])
        g1, g1b = kb.dram(name + "_g1", [4 * rows, w], BF16)
        kb.allgather([[0, 1, 2, 3], [4, 5, 6, 7]], bnc, g1, [bb], [g1b])
        A, _ = kb.dram(name + "_A", [4 * rows, w], BF16)
        B, _ = kb.dram(name + "_B", [4 * rows, w], BF16)
        pbuf = Buf(name)
        pbuf2 = Buf(name)
        kb.allgather([[0, 4], [1, 5], [2, 6], [3, 7]], g1[0:2 * rows, :], A, [g1b], [pbuf])
        kb.allgather([[0, 4], [1, 5], [2, 6], [3, 7]], g1[2 * rows:4 * rows, :], B, [g1b], [pbuf2])
        gw.pieces.append((c0, w, A, B, (pbuf, pbuf2)))
    return gw


def rmsnorm_fm(kb, g, x_ap, gcol0, out_ap, ncols, nkc=KC, dim=D, tails=None):
    nc = kb.nc
    with ExitStack() as es:
        xt = [kb.sb(es, "nx", [128, ncols], F32) for _ in range(3)]
        sq = [kb.sb(es, "nsq", [128, ncols], F32) for _ in range(2)]
        rstd, rstdb = kb.sb(es, "nrstd", [128, ncols], F32)
        ho = [kb.sb(es, "nh", [128, ncols], BF16) for _ in range(2)]
        nh = (ncols + 511) // 512
        pss = [kb.ps() for _ in range(nh)]
        for kc in range(nkc):
            x, xb = xt[kc % 3]
            kb.dma("sp", x[:], x_ap[kc * 128:(kc + 1) * 128, :], [], [xb])
            s, sb_ = sq[kc % 2]
            kb.op("act", lambda e: e.activation(out=s[:], in_=x[:], func=AF.Square), [xb], [sb_])
            for h in range(nh):
                c0, c1 = h * 512, min(ncols, (h + 1) * 512)
                pt, pb = pss[h]
                kb.op("pe", lambda e: e.matmul(pt[:, 0:c1 - c0], lhsT=g.ones_f[:], rhs=s[:, c0:c1],
                                                start=(kc == 0), stop=(kc == nkc - 1)), [sb_, g.ones_fb], [pb])
        for h in range(nh):
            c0, c1 = h * 512, min(ncols, (h + 1) * 512)
            pt, pb = pss[h]
            kb.op("dve", lambda e: e.tensor_scalar(out=rstd[:, c0:c1], in0=pt[:, 0:c1 - c0], scalar1=1.0 / dim,
                                                   scalar2=EPS, op0=ALU.mult, op1=ALU.add), [pb], [rstdb])
        kb.op("act", lambda e: e.activation(out=rstd[:], in_=rstd[:], func=AF.Sqrt), [rstdb], [rstdb])
        kb.op("dve", lambda e: e.reciprocal(out=rstd[:], in_=rstd[:]), [rstdb], [rstdb])
        for kc in range(nkc):
            x, xb = xt[kc % 3]
            kb.dma("sp", x[:], x_ap[kc * 128:(kc + 1) * 128, :], [], [xb])
            h_, hb = ho[kc % 2]
            kb.op("dve", lambda e: e.scalar_tensor_tensor(out=h_[:], in0=x[:], scalar=g.gains[:, gcol0 + kc:gcol0 + kc + 1],
                                                          in1=rstd[:], op0=ALU.mult, op1=ALU.mult),
                  [xb, rstdb, g.gainsb], [hb])
            kb.dma("pool", out_ap[kc * 128:(kc + 1) * 128, :], h_[:], [hb], [])
            if tails is not None:
                kb.op("act", lambda e: e.activation(out=tails[0][:, kc, :], in_=h_[:, ncols - 3:ncols], func=AF.Copy), [hb], [tails[1]])
    kb.barrier()


def dense(kb, g, rhs_loader, nkc, w_aps, tiles, evac, nth=NTH, ntok=TH, halo=0, halo_evac=None, kgrp=None):
    if kgrp is None:
        kgrp = nkc if nkc <= 43 else (nkc + 1) // 2
    ngrp = (nkc + kgrp - 1) // kgrp
    with ExitStack() as es:
        rhs, rhsb = kb.sb(es, "rhs", [128, nkc, halo + ntok], BF16)
        NW = 3
        wts = [kb.sb(es, "wt", [128, kgrp, 256], BF16) for _ in range(NW)]
        jobs = [(th, ti, kg) for th in range(nth) for ti in range(len(tiles)) for kg in range(ngrp)]

        def load_w(j):
            th, ti, kg = jobs[j]
            mode, wi, c0, w = tiles[ti]
            k0 = kg * kgrp
            k1 = min(nkc, k0 + kgrp)
            wt, wb = wts[j % NW]
            w_aps[wi].load(kb, ("sp", "pool") if j % 2 == 0 else ("pool", "sp"), wt, wb, c0, w, k0, k1)

        for j in range(min(2, len(jobs))):
            load_w(j)
        cur = None
        for j, (th, ti, kg) in enumerate(jobs):
            if j + 2 < len(jobs):
                load_w(j + 2)
            if ti == 0 and kg == 0:
                rhs_loader(th, rhs, rhsb)
            mode, wi, c0, w = tiles[ti]
            k0 = kg * kgrp
            k1 = min(nkc, k0 + kgrp)
            wt, wb = wts[j % NW]
            if mode == "fm":
                nsub = (w + 127) // 128
                if kg == 0:
                    cur = [kb.ps() for _ in range(nsub)]
                    curh = [kb.hps() for _ in range(nsub)] if halo else None
                for sub in range(nsub):
                    ws = min(128, w - sub * 128)
                    pt, pb = cur[sub]
                    for kc in range(k0, k1):
                        kz = w_aps[wi].ksz(kc)
                        kb.op("pe", lambda e: e.matmul(pt[0:ws, 0:ntok], lhsT=wt[0:kz, kc - k0, sub * 128:sub * 128 + ws],
                                                        rhs=rhs[0:kz, kc, halo:halo + ntok], start=(kc == 0), stop=(kc == nkc - 1)),
                              [wb, rhsb], [pb])
                    if halo:
                        ph, phb = curh[sub]
                        for kc in range(k0, k1):
                            kb.op("pe", lambda e: e.matmul(ph[0:ws, 0:halo], lhsT=wt[:, kc - k0, sub * 128:sub * 128 + ws],
                                                            rhs=rhs[:, kc, 0:halo], start=(kc == 0), stop=(kc == nkc - 1)),
                                  [wb, rhsb], [phb])
                if kg == ngrp - 1:
                    for sub in range(nsub):
                        ws = min(128, w - sub * 128)
                        if halo:
                            halo_evac(wi, c0 + sub * 128, ws, th, curh[sub][0], curh[sub][1])
                        evac("fm", wi, c0 + sub * 128, ws, th, cur[sub][0], cur[sub][1])
            else:
                ntt = ntok // 128
                if kg == 0:
                    cur = [kb.ps() for _ in range(ntt)]
                for tt in range(ntt):
                    pt, pb = cur[tt]
                    for kc in range(k0, k1):
                        kb.op("pe", lambda e: e.matmul(pt[:, 0:w], lhsT=rhs[:, kc, halo + tt * 128:halo + (tt + 1) * 128],
                                                        rhs=wt[:, kc - k0, 0:w], start=(kc == 0), stop=(kc == nkc - 1)),
                              [wb, rhsb], [pb])
                if kg == ngrp - 1:
                    for tt in range(ntt):
                        evac("tm", wi, c0, w, (th, tt), cur[tt][0], cur[tt][1])
    kb.barrier()


def exchange_halo(kb, g, tails):
    tl, tlb = tails
    bnc, bb = kb.dram("tailb", [128, KC * 3], BF16)
    kb.dma("cc", bnc[:, :], tl[:].rearrange("p k h -> p (k h)"), [tlb], [bb])
    ga, gab = two_stage_allgather(kb, bnc, bb, 128, KC * 3, BF16, "tail")
    with ExitStack() as es:
        al, alb = kb.sb(es, "tall", [128, 8, KC * 3], BF16)
        acc, accb = kb.sb(es, "tacc", [128, KC * 3], F32)
        kb.dma("pool", al[:], ga.rearrange("(r p) x -> p r x", p=128), [gab], [alb])
        for j in range(8):
            sel = g.ctab[:, 64 + j:65 + j]
            if j == 0:
                kb.op("dve", lambda e: e.tensor_scalar(out=acc[:], in0=al[:, j, :], scalar1=sel, scalar2=None, op0=ALU.mult),
                      [alb, g.ctabb], [accb])
            else:
                kb.op("dve", lambda e: e.scalar_tensor_tensor(out=acc[:], in0=al[:, j, :], scalar=sel, in1=acc[:],
                                                              op0=ALU.mult, op1=ALU.add), [alb, accb, g.ctabb], [accb])
        kb.op("dve", lambda e: e.tensor_copy(out=g.halo[:].rearrange("p k h -> p (k h)"), in_=acc[:]), [accb], [g.halob])
    kb.barrier()


def make_rhs_loader(kb, g, src_ap, nkc, halo, rs=512):
    cpr = (rs + 127) // 128
    nfull = rs // 128

    def loader(th, rhs, rhsb):
        if rs % 128 == 0:
            kb.dma("sp", rhs[:, :, halo:halo + TH], src_ap[:, th * TH:(th + 1) * TH].rearrange("(kc p) t -> p kc t", p=128), [], [rhsb])
        else:
            for r in range(8):
                q = "sp" if r % 2 == 0 else "pool"
                kb.dma(q, rhs[:, r * cpr:r * cpr + nfull, halo:halo + TH],
                       src_ap[r * rs:r * rs + nfull * 128, th * TH:(th + 1) * TH].rearrange("(kc p) t -> p kc t", p=128), [], [rhsb])
                kb.dma(q, rhs[0:rs - nfull * 128, r * cpr + nfull, halo:halo + TH],
                       src_ap[r * rs + nfull * 128:(r + 1) * rs, th * TH:(th + 1) * TH], [], [rhsb])
        if halo:
            if th == 0:
                kb.op("dve", lambda e: e.tensor_copy(out=rhs[:, :, 0:halo], in_=g.halo[:, :, 3 - halo:3]), [g.halob], [rhsb])
            else:
                kb.dma("sp", rhs[:, :, 0:halo], src_ap[:, th * TH - halo:th * TH].rearrange("(kc p) t -> p kc t", p=128), [], [rhsb])
    return loader


class Stager:
    def __init__(self, kb, es, n, dt, name="stg", width=512):
        self.kb = kb
        self.t = [kb.sb(es, name, [128, width], dt) for _ in range(n)]
        self.i = 0

    def get(self):
        r = self.t[self.i % len(self.t)]
        self.i += 1
        return r


def conv_silu(kb, uext, uextb, K, cw, cwb, blk, acc, accb, width):
    for k in range(K - 1, -1, -1):
        src = uext[:, k:k + width]
        wk = cw[:, blk, k:k + 1]
        if k == K - 1:
            kb.op("dve", lambda e: e.tensor_scalar(out=acc[:, 0:width], in0=src, scalar1=wk, scalar2=cw[:, blk, K:K + 1],
                                                   op0=ALU.mult, op1=ALU.add), [uextb, cwb], [accb])
        else:
            kb.op("dve", lambda e: e.scalar_tensor_tensor(out=acc[:, 0:width], in0=src, scalar=wk, in1=acc[:, 0:width],
                                                          op0=ALU.mult, op1=ALU.add), [uextb, cwb, accb], [accb])


def residual_dense(kb, g, src_ap, gw):
    with ExitStack() as es:
        xin = Stager(kb, es, 3, F32, "rx")
        xout = Stager(kb, es, 3, F32, "ro")

        def evac(mode, wi, c0, ws, th, pt, pb):
            xi, xib = xin.get()
            xo, xob = xout.get()
            reg = g.xres[c0:c0 + ws, th * TH:(th + 1) * TH]
            kb.dma("pool", xi[0:ws, :], reg, [], [xib])
            kb.op("dve", lambda e: e.tensor_tensor(out=xo[0:ws, :], in0=pt[0:ws, 0:TH], in1=xi[0:ws, :], op=ALU.add),
                  [pb, xib], [xob])
            kb.dma("pool", reg, xo[0:ws, :], [xob], [])

        tiles = [("fm", 0, c0, 256) for c0 in range(0, D, 256)]
        dense(kb, g, make_rhs_loader(kb, g, src_ap, gw.nkc, 0, rs=gw.rs), gw.nkc, [gw], tiles, evac)


def ffn_phase(kb, g, i):
    nc = kb.nc
    with ExitStack() as es0:
        tails = kb.sb(es0, "tails", [128, KC, 3], BF16)
        rmsnorm_fm(kb, g, g.xres, g.gcol["ffn"][i], g.hT, T, tails=tails)
        exchange_halo(kb, g, tails)
    g.hook()
    with ExitStack() as es:
        cw, cwb = kb.sb(es, "cwf", [128, 2 * FC, 4], F32)
        kb.dma("sp", cw[:], g.in_cwf[i], [], [cwb])
        uexts = [kb.sb(es, "uext", [128, 2 + TH], F32) for _ in range(3)]
        accs = [kb.sb(es, "facc", [128, TH], F32) for _ in range(2)]
        gates = {}
        gate_t = [kb.sb(es, "gate", [128, TH], F32) for _ in range(4)]
        outs = Stager(kb, es, 3, BF16, "fo")
        st = {"u": 0, "a": 0, "g": 0}
        pend = {}

        def halo_evac(wi, c0, ws, th, ph, phb):
            ue, ueb = uexts[st["u"] % 3]
            kb.op("act", lambda e: e.activation(out=ue[:, 0:2], in_=ph[:, 0:2], func=AF.Copy), [phb], [ueb])
            pend[(wi, c0)] = (ue, ueb)
            st["u"] += 1

        def evac(mode, wi, c0, ws, th, pt, pb):
            ue, ueb = pend.pop((wi, c0))
            kb.op("act", lambda e: e.activation(out=ue[:, 2:2 + TH], in_=pt[:, 0:TH], func=AF.Copy), [pb], [ueb])
            ac, acb = accs[st["a"] % 2]
            st["a"] += 1
            blk = c0 // 128 + (FC if wi == 1 else 0)
            conv_silu(kb, ue, ueb, 3, cw, cwb, blk, ac, acb, TH)
            if wi == 0:
                gt, gtb = gate_t[st["g"] % 4]
                st["g"] += 1
                kb.op("act", lambda e: e.activation(out=gt[:], in_=ac[:], func=AF.Silu), [acb], [gtb])
                gates[c0] = (gt, gtb)
            else:
                gt, gtb = gates.pop(c0)
                o, ob = outs.get()
                kb.op("dve", lambda e: e.tensor_tensor(out=o[:], in0=ac[:], in1=gt[:], op=ALU.mult), [acb, gtb], [ob])
                kb.dma("pool", g.actT[c0:c0 + 128, th * TH:(th + 1) * TH], o[:], [ob], [])

        tiles = []
        for c0 in range(0, DFF, 256):
            tiles.append(("fm", 0, c0, 256))
            tiles.append(("fm", 1, c0, 256))
        dense(kb, g, make_rhs_loader(kb, g, g.hT, KC, 2), KC, [g.w_upg[i], g.w_upv[i]], tiles, evac, halo=2, halo_evac=halo_evac)
    residual_dense(kb, g, g.actT, g.w_down[i])


IN_COLS = [A_COLS, B_COLS, C_COLS, A_COLS]
CST_COLS = 128 + 4 * 512 + 32 + 2


def build(layers=(0, 1, 2, 3), parts=("mix", "ffn"), final=True, dbg=()):
    nc = bass.Bass("TRN2", target_bir_lowering=False)
    with ExitStack() as es:
        block = es.enter_context(nc.Block())
        kb = KB(nc, es)
        g = G()
        g.layers = layers

        def ext(name, shape, dt=F32):
            return nc.dram_tensor(name, list(shape), dt, kind="ExternalInput").ap()

        in_xT = ext("xT", [D, T])
        in_cst = ext("cst", [128, CST_COLS])
        in_ctab = ext("ctab", [128, 80])
        in_gains = ext("gains", [128, 13 * KC])
        g.gcol = {"mix": [i * KC for i in range(4)], "ffn": [(4 + i) * KC for i in range(4)],
                  "mem": [(8 + i) * KC for i in range(4)], "final": 12 * KC}
        g.in_cwf = {}
        yT = nc.dram_tensor("yT", [D, T], F32, kind="ExternalOutput").ap()
        g.dbg_out = {}

        cst, cstb = kb.sb(es, "cst", [128, CST_COLS], F32)
        g.cst, g.cstb = cst, cstb
        g.ident = cst[:, 0:128]
        g.maskadd = [cst[:, 128 + r * 512:128 + (r + 1) * 512] for r in range(4)]
        g.swp = cst[0:32, 128 + 2048:128 + 2048 + 32]
        g.ropec = cst[0:32, 128 + 2048 + 32:128 + 2048 + 34]
        g.ctab, g.ctabb = kb.sb(es, "ctab", [128, 80], F32)
        g.gains, g.gainsb = kb.sb(es, "gains", [128, 13 * KC], F32)
        g.ones_f, g.ones_fb = kb.sb(es, "ones_f", [128, 128], F32)
        g.ones_h, g.ones_hb = kb.sb(es, "ones_h", [128, 128], BF16)
        g.halo, g.halob = kb.sb(es, "halo", [128, KC, 3], BF16)
        g.onecol = g.ones_f[:, 0:1]

        def body(_):
            kb.dma("sp", cst[:], in_cst, [], [cstb])
            kb.dma("sp", g.ctab[:], in_ctab, [], [g.ctabb])
            kb.dma("sp", g.gains[:], in_gains, [], [g.gainsb])
            kb.op("dve", lambda e: e.memset(g.ones_f[:], 1.0), [], [g.ones_fb])
            kb.op("dve", lambda e: e.memset(g.ones_h[:], 1.0), [], [g.ones_hb])
            g.xres, _ = kb.dram("xres", [D, T], F32)
            g.hT, _ = kb.dram("hT", [D, T], BF16)
            g.catT, _ = kb.dram("catT", [D, T], BF16)
            g.actT, _ = kb.dram("actT", [DFF, T], BF16)
            for q in range(8):
                kb.dma("sp" if q % 2 else "pool", g.xres[q * 512:(q + 1) * 512, :], in_xT[q * 512:(q + 1) * 512, :], [], [])
            g.w_in, g.w_out, g.w_upg, g.w_upv, g.w_down, g.w_mkv = {}, {}, {}, {}, {}, {}
            units = []
            for i in layers:
                g.in_cwf[i] = ext("cwf%d" % i, [128, 2 * FC, 4])
                if "mix" in parts:
                    units.append(("mix", i))
                if "ffn" in parts:
                    units.append(("ffn", i))

            def prep_unit(u):
                what, i = u
                if what == "mix":
                    g.w_in[i] = prep_weight(kb, g, "w_in%d" % i, 512, IN_COLS[i], 1024)
                    g.w_mkv[i] = prep_weight(kb, g, "w_mkv%d" % i, 512, 2 * MEMW, 1024)
                    g.w_out[i] = prep_weight(kb, g, "w_out%d" % i, 512, D, 1024)
                else:
                    g.w_upg[i] = prep_weight(kb, g, "w_upg%d" % i, 512, DFF, 1024)
                    g.w_upv[i] = prep_weight(kb, g, "w_upv%d" % i, 512, DFF, 1024)
                    g.w_down[i] = prep_weight(kb, g, "w_down%d" % i, DFF // 8, D, 256)

            def hook():
                if units:
                    prep_unit(units.pop(0))

            g.hook = hook
            hook()
            if len(parts) == 1:
                hook()
            if "mix" in parts:
                mixer_setup(kb, g, ext)
            kb.barrier()
            for i in layers:
                if "mix" in parts:
                    mixer_phase(kb, g, i)
                if "ffn" in parts:
                    ffn_phase(kb, g, i)
            if final:
                rmsnorm_final(kb, g, yT)
            else:
                for q in range(8):
                    kb.dma("sp", yT[q * 512:(q + 1) * 512, :], g.xres[q * 512:(q + 1) * 512, :], [], [])
            kb.barrier(full=True)
            if "catT" in dbg:
                dcat = nc.dram_tensor("dbg_catT", [D, T], BF16, kind="ExternalOutput").ap()
                for q in range(8):
                    kb.dma("sp", dcat[q * 512:(q + 1) * 512, :], g.catT[q * 512:(q + 1) * 512, :], [], [])
                kb.barrier(full=True)
            g.stats = dict(kb.ecnt)

        block.gpsimd(body)
        nc._kstats = g.stats
    return nc


def rmsnorm_final(kb, g, yT):
    with ExitStack() as es:
        xt = [kb.sb(es, "fx", [128, T], F32) for _ in range(3)]
        sq = [kb.sb(es, "fsq", [128, T], F32) for _ in range(2)]
        rstd, rstdb = kb.sb(es, "frstd", [128, T], F32)
        ho = [kb.sb(es, "fh", [128, T], F32) for _ in range(2)]
        pss = [kb.ps() for _ in range(2)]
        for kc in range(KC):
            x, xb = xt[kc % 3]
            kb.dma("sp", x[:], g.xres[kc * 128:(kc + 1) * 128, :], [], [xb])
            s, sb_ = sq[kc % 2]
            kb.op("act", lambda e: e.activation(out=s[:], in_=x[:], func=AF.Square), [xb], [sb_])
            for h in range(2):
                pt, pb = pss[h]
                kb.op("pe", lambda e: e.matmul(pt[:, 0:512], lhsT=g.ones_f[:], rhs=s[:, h * 512:(h + 1) * 512],
                                                start=(kc == 0), stop=(kc == KC - 1)), [sb_, g.ones_fb], [pb])
        for h in range(2):
            pt, pb = pss[h]
            kb.op("dve", lambda e: e.tensor_scalar(out=rstd[:, h * 512:(h + 1) * 512], in0=pt[:, 0:512], scalar1=1.0 / D,
                                                   scalar2=EPS, op0=ALU.mult, op1=ALU.add), [pb], [rstdb])
        kb.op("act", lambda e: e.activation(out=rstd[:], in_=rstd[:], func=AF.Sqrt), [rstdb], [rstdb])
        kb.op("dve", lambda e: e.reciprocal(out=rstd[:], in_=rstd[:]), [rstdb], [rstdb])
        gc = g.gcol["final"]
        for kc in range(KC):
            x, xb = xt[kc % 3]
            kb.dma("sp", x[:], g.xres[kc * 128:(kc + 1) * 128, :], [], [xb])
            h_, hb = ho[kc % 2]
            kb.op("dve", lambda e: e.scalar_tensor_tensor(out=h_[:], in0=x[:], scalar=g.gains[:, gc + kc:gc + kc + 1],
                                                          in1=rstd[:], op0=ALU.mult, op1=ALU.mult), [xb, rstdb, g.gainsb], [hb])
            kb.dma("pool", yT[kc * 128:(kc + 1) * 128, :], h_[:], [hb], [])


def gather_chunks(kb, name, src_ap, rows, cols, dt, by, csz):
    out = []
    n = rows if by == "r" else cols
    for lo in range(0, n, csz):
        hi = min(n, lo + csz)
        if by == "r":
            cr, cc = hi - lo, cols
            piece = src_ap[lo:hi, :]
        else:
            cr, cc = rows, hi - lo
            piece = src_ap[:, lo:hi]
        bnc, bb = kb.dram(name + "_b", [cr, cc], dt)
        kb.dma("cc", bnc[:, :], piece, [], [bb])
        g1, g1b = kb.dram(name + "_g1", [4 * cr, cc], dt)
        kb.allgather([[0, 1, 2, 3], [4, 5, 6, 7]], bnc, g1, [bb], [g1b])
        A, _ = kb.dram(name + "_A", [4 * cr, cc], dt)
        B, _ = kb.dram(name + "_B", [4 * cr, cc], dt)
        b1, b2 = Buf(name), Buf(name)
        kb.allgather([[0, 4], [1, 5], [2, 6], [3, 7]], g1[0:2 * cr, :], A, [g1b], [b1])
        kb.allgather([[0, 4], [1, 5], [2, 6], [3, 7]], g1[2 * cr:4 * cr, :], B, [g1b], [b2])
        out.append((lo, hi, A, B, (b1, b2)))
    return out


PAIRS = [("A", 0, 0), ("B", 0, 2), ("A", 2, 4), ("B", 2, 6)]


def mixer_setup(kb, g, ext):
    nc = kb.nc
    es = kb.es
    g.qT, _ = kb.dram("qT", [TOKW, T], BF16)
    g.kTb, _ = kb.dram("kTb", [TOKW, T], BF16)
    g.vb, _ = kb.dram("vb", [T, TOKW], BF16)
    g.qmT, _ = kb.dram("qmT", [MEMW, T], BF16)
    g.szT, _ = kb.dram("szT", [TOKW, T], BF16)
    g.xsT, _ = kb.dram("xsT", [TOKW, T], F32)
    g.cumb, _ = kb.dram("cumb", [T, 48], F32)
    g.cumHMd, _ = kb.dram("cumHMd", [48, T], F32)
    g.hmT, _ = kb.dram("hmT", [D, MEMT], BF16)
    g.kmT, _ = kb.dram("kmT", [MEMW, MEMT], BF16)
    g.vm, _ = kb.dram("vm", [MEMT, MEMW], BF16)
    g.memT = ext("memT", [D, MEMT])
    in_pos = nc.dram_tensor("pos", [1, T], mybir.dt.int32, kind="ExternalInput").ap()
    g.in_small = {}
    for i in g.layers:
        g.in_small[i] = ext("small%d" % i, [128, SMALL_COLS])
    g.cos_t, g.cos_tb = kb.sb(es, "cos_t", [32, T], F32)
    g.sin_t, g.sin_tb = kb.sb(es, "sin_t", [32, T], F32)
    g.aHM, g.aHMb = kb.sb(es, "aHM", [48, T], F32)
    g.dtHM, g.dtHMb = kb.sb(es, "dtHM", [48, T], F32)
    with ExitStack() as es2:
        pi_, pib = kb.sb(es2, "posi", [32, T], mybir.dt.int32)
        pf, pfb = kb.sb(es2, "posf", [32, T], F32)
        tmp, tmpb = kb.sb(es2, "postmp", [32, T], F32)
        kb.dma("sp", pi_[:], in_pos[0, :].partition_broadcast(32), [], [pib])
        kb.op("dve", lambda e: e.tensor_copy(out=pf[:], in_=pi_[:]), [pib], [pfb])
        kb.op("dve", lambda e: e.tensor_scalar(out=pf[:], in0=pf[:], scalar1=g.ropec[:, 0:1], scalar2=None, op0=ALU.mult),
              [pfb, g.cstb], [pfb])
        ki, kib = kb.sb(es2, "poski", [32, T], mybir.dt.int32)
        s1, s1b = kb.sb(es2, "poss1", [32, T], F32)
        hp, hpb = kb.sb(es2, "halfpi", [32, 1], F32)
        kb.op("dve", lambda e: e.memset(hp[:], 0.5 * math.pi), [], [hpb])
        for (dst, dstb, shift) in ((g.sin_t, g.sin_tb, 0.0), (g.cos_t, g.cos_tb, 0.5 * math.pi)):
            kb.op("dve", lambda e: e.tensor_scalar(out=tmp[:], in0=pf[:], scalar1=shift, scalar2=1.0 / (2.0 * math.pi),
                                                   op0=ALU.add, op1=ALU.mult), [pfb], [tmpb])
            kb.op("dve", lambda e: e.tensor_copy(out=ki[:], in_=tmp[:]), [tmpb], [kib])
            kb.op("dve", lambda e: e.tensor_copy(out=tmp[:], in_=ki[:]), [kib], [tmpb])
            kb.op("dve", lambda e: e.tensor_scalar(out=s1[:], in0=pf[:], scalar1=shift, scalar2=None, op0=ALU.add), [pfb], [s1b])
            kb.op("dve", lambda e: e.scalar_tensor_tensor(out=tmp[:], in0=tmp[:], scalar=-2.0 * math.pi, in1=s1[:], op0=ALU.mult, op1=ALU.add),
                  [tmpb, s1b], [tmpb])
            kb.op("act", lambda e: e.activation(out=s1[:], in_=tmp[:], func=AF.Sin, scale=0.5), [tmpb], [s1b])
            kb.op("act", lambda e: e.activation(out=tmp[:], in_=tmp[:], func=AF.Sin, scale=-0.5, bias=hp[:, 0:1]), [tmpb, hpb], [tmpb])
            kb.op("dve", lambda e: e.scalar_tensor_tensor(out=dst[:], in0=s1[:], scalar=2.0, in1=tmp[:], op0=ALU.mult, op1=ALU.mult),
                  [s1b, tmpb], [dstb])
        kb.op("dve", lambda e: e.tensor_scalar(out=g.sin_t[:], in0=g.sin_t[:], scalar1=g.ropec[:, 1:2], scalar2=None,
                                               op0=ALU.mult), [g.sin_tb, g.cstb], [g.sin_tb])
        kb.barrier()


SM_LAM = 0
SM_SUBLN = 4
SM_FB = 6
SM_DTB = 7
SM_ALOG = 8
SM_DSKIP = 9
SM_NORMG = 57
SM_CONV = 105
SMALL_COLS = 105 + 200


def mixer_phase(kb, g, i):
    kind, j = i % 3, i // 3
    es0 = ExitStack()
    sm, smb = kb.sb(es0, "small", [128, SMALL_COLS], F32)
    kb.dma("sp", sm[:], g.in_small[i], [], [smb])
    g.sm, g.smb = sm, smb
    with ExitStack() as es1:
        tails = kb.sb(es1, "tails", [128, KC, 3], BF16)
        rmsnorm_fm(kb, g, g.xres, g.gcol["mix"][i], g.hT, T, tails=tails if kind == 2 else None)
        if kind == 2:
            exchange_halo(kb, g, tails)
    rmsnorm_fm(kb, g, g.memT, g.gcol["mem"][i], g.hmT, MEMT)
    with ExitStack() as es:
        stg = Stager(kb, es, 3, BF16, "mstg")

        def evac_m(mode, wi, c0, ws, th, pt, pb):
            o, ob = stg.get()
            if mode == "fm":
                kb.op("act", lambda e: e.activation(out=o[0:ws, 0:MEMT], in_=pt[0:ws, 0:MEMT], func=AF.Copy), [pb], [ob])
                kb.dma("pool", g.kmT[c0:c0 + ws, :], o[0:ws, 0:MEMT], [ob], [])
            else:
                th_, tt = th
                kb.op("act", lambda e: e.activation(out=o[:, 0:ws], in_=pt[:, 0:ws], func=AF.Copy), [pb], [ob])
                kb.dma("pool", g.vm[tt * 128:(tt + 1) * 128, c0 - MEMW:c0 - MEMW + ws], o[:, 0:ws], [ob], [])

        def ld_m(th, rhs, rhsb):
            kb.dma("sp", rhs[:, :, 0:MEMT], g.hmT.rearrange("(kc p) t -> p kc t", p=128), [], [rhsb])

        tiles = [("fm", 0, c0, 256) for c0 in range(0, MEMW, 256)] + [("tm", 0, c0, 256) for c0 in range(MEMW, 2 * MEMW, 256)]
        dense(kb, g, ld_m, KC, [g.w_mkv[i]], tiles, evac_m, nth=1, ntok=MEMT)
    inproj(kb, g, i, kind)
    if kind == 2:
        ssd_post(kb, g)
    attention(kb, g, i, kind)
    memattn(kb, g)
    es0.close()
    kb.barrier()
    residual_dense(kb, g, g.catT, g.w_out[i])


def inproj(kb, g, i, kind):
    sm, smb = g.sm, g.smb
    with ExitStack() as es:
        stg = Stager(kb, es, 4, BF16, "pstg")
        q32s = [kb.sb(es, "q32", [32, TH], F32) for _ in range(2)]
        rt = [kb.sb(es, "rt", [32, TH], F32) for _ in range(4)]
        f32s = [kb.sb(es, "f32s", [128, 3 + TH], F32) for _ in range(3)]
        accs = [kb.sb(es, "iacc", [128, TH], F32) for _ in range(2)]
        st = {"n": 0}
        pend = {}
        if kind == 0:
            segs = {"q": (0, TOKW), "k": (TOKW, 2 * TOKW), "v": (2 * TOKW, 3 * TOKW), "m": (3 * TOKW, A_COLS)}
        elif kind == 1:
            segs = {"q": (0, TOKW), "k": (TOKW, 2 * TOKW), "v": (2 * TOKW, 3 * TOKW), "f": (3 * TOKW, 3 * TOKW + 24),
                    "m": (3 * TOKW + 24, B_COLS)}
        else:
            segs = {"z": (0, TOKW), "x": (TOKW, TOKW + 5120), "d": (TOKW + 5120, TOKW + 5168), "m": (TOKW + 5168, C_COLS)}

        def seg_of(c0):
            for k, (a, b) in segs.items():
                if a <= c0 < b:
                    return k, a
            raise ValueError(c0)

        def store(o, ob, ws, dst):
            kb.dma("pool", dst, o, [ob], [])

        def halo_evac(wi, c0, ws, th, ph, phb):
            sg, a = seg_of(c0)
            if sg != "x":
                return
            ue, ueb = f32s[st["n"] % 3]
            st["n"] += 1
            kb.op("act", lambda e: e.activation(out=ue[:, 0:3], in_=ph[:, 0:3], func=AF.Copy), [phb], [ueb])
            pend[c0] = (ue, ueb)

        def evac(mode, wi, c0, ws, th, pt, pb):
            sg, a = seg_of(c0)
            if mode == "tm":
                th_, tt = th
                o, ob = stg.get()
                kb.op("act", lambda e: e.activation(out=o[:, 0:ws], in_=pt[:, 0:ws], func=AF.Copy), [pb], [ob])
                t0 = th_ * TH + tt * 128
                store(o[:, 0:ws], ob, ws, g.vb[t0:t0 + 128, c0 - a:c0 - a + ws])
                return
            tsl = slice(th * TH, (th + 1) * TH)
            if sg in ("q", "k") and kind == 0:
                o, ob = stg.get()
                q3, q3b = q32s[st["n"] % 2]
                st["n"] += 1
                kb.op("act", lambda e: e.activation(out=q3[:], in_=pt[0:32, 0:TH], func=AF.Copy), [pb], [q3b])
                kb.op("act", lambda e: e.activation(out=o[32:64, :], in_=pt[32:64, 0:TH], func=AF.Copy), [pb], [ob])
                kb.op("act", lambda e: e.activation(out=o[64:128, :], in_=pt[64:128, 0:TH], func=AF.Copy), [pb], [ob])
                sw, swb = kb.hps2()
                kb.op("pe", lambda e: e.matmul(sw[0:32, 0:TH], lhsT=g.swp, rhs=q3[:], start=True, stop=True), [q3b, g.cstb], [swb])
                t1, t1b = rt[st["n"] % 4]
                t2, t2b = rt[(st["n"] + 2) % 4]
                kb.op("dve", lambda e: e.tensor_tensor(out=t1[:], in0=q3[:], in1=g.cos_t[:, tsl], op=ALU.mult), [q3b, g.cos_tb], [t1b])
                kb.op("dve", lambda e: e.tensor_tensor(out=t2[:], in0=sw[0:32, 0:TH], in1=g.sin_t[:, tsl], op=ALU.mult), [swb, g.sin_tb], [t2b])
                kb.op("dve", lambda e: e.tensor_tensor(out=o[0:32, :], in0=t1[:], in1=t2[:], op=ALU.add), [t1b, t2b], [ob])
                dst = (g.qT if sg == "q" else g.kTb)[c0 - a:c0 - a + ws, tsl]
                store(o[0:ws, :], ob, ws, dst)
            elif sg in ("q", "k", "m"):
                o, ob = stg.get()
                kb.op("act", lambda e: e.activation(out=o[0:ws, :], in_=pt[0:ws, 0:TH], func=AF.Copy), [pb], [ob])
                dst = {"q": g.qT, "k": g.kTb, "m": g.qmT}[sg][c0 - a:c0 - a + ws, tsl]
                store(o[0:ws, :], ob, ws, dst)
            elif sg == "f":
                e1, e1b = f32s[st["n"] % 3]
                st["n"] += 1
                kb.op("act", lambda e: e.activation(out=e1[0:24, 0:TH], in_=pt[0:24, 0:TH], func=AF.Exp, scale=-1.0,
                                                    bias=g.negfb[:, 0:1]), [pb, g.negfbb], [e1b])
                kb.op("act", lambda e: e.activation(out=e1[0:24, 0:TH], in_=e1[0:24, 0:TH], func=AF.Ln, bias=g.onecol[0:24, :]),
                      [e1b, g.ones_fb], [e1b])
                kb.op("dve", lambda e: e.tensor_scalar(out=g.aHM[0:24, tsl], in0=e1[0:24, 0:TH], scalar1=-1.0, scalar2=None,
                                                       op0=ALU.mult), [e1b], [g.aHMb])
            elif sg == "d":
                e1, e1b = f32s[st["n"] % 3]
                st["n"] += 1
                kb.op("act", lambda e: e.activation(out=e1[0:48, 0:TH], in_=pt[0:48, 0:TH], func=AF.Exp,
                                                    bias=sm[0:48, SM_DTB:SM_DTB + 1]), [pb, smb], [e1b])
                kb.op("act", lambda e: e.activation(out=g.dtHM[:, tsl], in_=e1[0:48, 0:TH], func=AF.Ln, bias=g.onecol[0:48, :]),
                      [e1b, g.ones_fb], [g.dtHMb])
                kb.op("dve", lambda e: e.tensor_scalar(out=g.aHM[:, tsl], in0=g.dtHM[:, tsl], scalar1=g.negA[:, 0:1], scalar2=None,
                                                       op0=ALU.mult), [g.dtHMb, g.negAb], [g.aHMb])
            elif sg == "z":
                o, ob = stg.get()
                kb.op("act", lambda e: e.activation(out=o[0:ws, :], in_=pt[0:ws, 0:TH], func=AF.Silu), [pb], [ob])
                store(o[0:ws, :], ob, ws, g.szT[c0 - a:c0 - a + ws, tsl])
            elif sg == "x":
                ue, ueb = pend.pop(c0)
                kb.op("act", lambda e: e.activation(out=ue[:, 3:3 + TH], in_=pt[:, 0:TH], func=AF.Copy), [pb], [ueb])
                ac, acb = accs[st["n"] % 2]
                st["n"] += 1
                blk = (c0 - a) // 128
                cwv = sm[:, SM_CONV:SM_CONV + 200].rearrange("p (b k) -> p b k", k=5)
                conv_silu(kb, ue, ueb, 4, cwv, smb, blk, ac, acb, TH)
                if blk < 24:
                    xo, xob = f32s[st["n"] % 3]
                    st["n"] += 1
                    kb.op("act", lambda e: e.activation(out=xo[:, 0:TH], in_=ac[:], func=AF.Silu), [acb], [xob])
                    kb.dma("pool", g.xsT[blk * 128:(blk + 1) * 128, tsl], xo[:, 0:TH], [xob], [])
                else:
                    o, ob = stg.get()
                    kb.op("act", lambda e: e.activation(out=o[:], in_=ac[:], func=AF.Silu), [acb], [ob])
                    if blk < 32:
                        store(o[:], ob, 128, g.kTb[(blk - 24) * 128:(blk - 23) * 128, tsl])
                    else:
                        store(o[:], ob, 128, g.qT[(blk - 32) * 128:(blk - 31) * 128, tsl])

        tiles = []
        for sgn, (a, b) in segs.items():
            mode = "tm" if sgn == "v" else "fm"
            for c0 in range(a, b, 256):
                tiles.append((mode, 0, c0, min(256, b - c0)))
        halo = 3 if kind == 2 else 0
        if kind == 1:
            g.negfb, g.negfbb = kb.sb(es, "negfb", [24, 1], F32)
            kb.op("dve", lambda e: e.tensor_scalar(out=g.negfb[:], in0=sm[0:24, SM_FB:SM_FB + 1], scalar1=-1.0, scalar2=None, op0=ALU.mult),
                  [smb], [g.negfbb])
        if kind == 2:
            g.negA, g.negAb = kb.sb(es, "negA", [48, 1], F32)
            kb.op("act", lambda e: e.activation(out=g.negA[:], in_=sm[0:48, SM_ALOG:SM_ALOG + 1], func=AF.Exp), [smb], [g.negAb])
            kb.op("dve", lambda e: e.tensor_scalar(out=g.negA[:], in0=g.negA[:], scalar1=-1.0, scalar2=None, op0=ALU.mult),
                  [g.negAb], [g.negAb])
        dense(kb, g, make_rhs_loader(kb, g, g.hT, KC, halo), KC, [g.w_in[i]], tiles, evac, halo=halo,
              halo_evac=halo_evac if halo else None)


def ssd_post(kb, g):
    with ExitStack() as es:
        dtTM, dtTMb = kb.sb(es, "dtTM", [128, 8, 48], F32)
        for tt in range(8):
            pt, pb = kb.ps()
            kb.op("pe", lambda e: e.transpose(out=pt[:, 0:48], in_=g.dtHM[:, tt * 128:(tt + 1) * 128], identity=g.ident[0:48, 0:48]),
                  [g.dtHMb, g.cstb], [pb])
            kb.op("act", lambda e: e.activation(out=dtTM[:, tt, :], in_=pt[:, 0:48], func=AF.Copy), [pb], [dtTMb])
        xs = [kb.sb(es, "xsl", [128, T], F32) for _ in range(2)]
        stg = Stager(kb, es, 3, BF16, "xstg", width=128)
        for cc in range(24):
            x, xb = xs[cc % 2]
            kb.dma("sp", x[:], g.xsT[cc * 128:(cc + 1) * 128, :], [], [xb])
            for tt in range(8):
                pt, pb = kb.ps()
                kb.op("pe", lambda e: e.transpose(out=pt[:, 0:128], in_=x[:, tt * 128:(tt + 1) * 128], identity=g.ident), [xb, g.cstb], [pb])
                o, ob = stg.get()
                for hh in range(2):
                    h = cc * 2 + hh
                    kb.op("dve", lambda e: e.tensor_scalar(out=o[:, hh * 64:(hh + 1) * 64], in0=pt[:, hh * 64:(hh + 1) * 64],
                                                           scalar1=dtTM[:, tt, h:h + 1], scalar2=None, op0=ALU.mult), [pb, dtTMb], [ob])
                kb.dma("pool", g.vb[tt * 128:(tt + 1) * 128, cc * 128:(cc + 1) * 128], o[:, 0:128], [ob], [])
    kb.barrier()


def attention(kb, g, i, kind):
    sm, smb = g.sm, g.smb
    NU = {0: 12, 1: 24, 2: 8}[kind]
    NM = 2 if kind == 0 else 1
    E = {0: 256, 1: 128, 2: 384}[kind]
    KR = 1024 if kind == 2 else TOKW
    H = {0: 0, 1: 24, 2: 48}[kind]
    scale = 128.0 ** -0.5
    kb.barrier()
    kch = gather_chunks(kb, "kg", g.kTb[0:KR, :], KR, T, BF16, "r", 512)
    vch = gather_chunks(kb, "vg", g.vb, T, TOKW, BF16, "c", 512)
    es = ExitStack()
    if H:
        cumHM, cumHMb = kb.sb(es, "cumHM", [48, T], F32)
        onesT, onesTb = kb.sb(es, "onesT", [48, T], F32)
        zc, zcb = kb.sb(es, "zc", [48, 1], F32)
        cumown, cumownb = kb.sb(es, "cumown", [128, 8, 48], F32)
        cumpre, cumpreb = kb.sb(es, "cumpre", [128, NPRE, 48], F32)
        totb, totbb = kb.sb(es, "totb", [128, 7, 48], F32)
        offb, offbb = kb.sb(es, "offb", [128, 7, 48], F32)
        offown, offownb = kb.sb(es, "offown", [128, 48], F32)
        doff, doffb = kb.sb(es, "doff", [128, 7, 48], F32)
        kb.op("dve", lambda e: e.memset(onesT[:], 1.0), [], [onesTb])
        kb.op("dve", lambda e: e.memset(zc[:], 0.0), [], [zcb])
        kb.op("dve", lambda e: e.memset(cumown[:], 0.0), [], [cumownb])
        kb.op("dve", lambda e: e.tensor_tensor_scan(out=cumHM[0:H, :], data0=onesT[0:H, :], data1=g.aHM[0:H, :], initial=zc[0:H, :],
                                                    op0=ALU.mult, op1=ALU.add), [onesTb, g.aHMb, zcb], [cumHMb])
        kb.dma("sp", g.cumHMd[0:H, :], cumHM[0:H, :], [cumHMb], [])
        for tt in range(8):
            pt, pb = kb.ps()
            kb.op("pe", lambda e: e.transpose(out=pt[:, 0:H], in_=cumHM[0:H, tt * 128:(tt + 1) * 128], identity=g.ident[0:H, 0:H]),
                  [cumHMb, g.cstb], [pb])
            kb.op("act", lambda e: e.activation(out=cumown[:, tt, 0:H], in_=pt[:, 0:H], func=AF.Copy), [pb], [cumownb])
        kb.dma("sp", g.cumb.rearrange("(j p) h -> p j h", p=128), cumown[:], [cumownb], [])
        kb.barrier()
        (lo, hi, CA, CB, cbufs) = gather_chunks(kb, "cg", g.cumb, T, 48, F32, "r", T)[0]
        bufsel = {"A": CA, "B": CB}
        for (bn, s0, r0) in PAIRS:
            n = 2 if r0 < 6 else 1
            src = bufsel[bn][s0 * T:(s0 + n) * T, :].rearrange("(x p) h -> p x h", p=128)
            kb.dma("sp", cumpre[:, r0 * 8:(r0 + n) * 8, :], src, list(cbufs), [cumpreb])
            for k in range(n):
                row = (s0 + k) * T + T - 1
                kb.dma("pool", totb[:, r0 + k, :], bufsel[bn][row, :].partition_broadcast(128), list(cbufs), [totbb])
        kb.op("dve", lambda e: e.memset(offb[:], 0.0), [], [offbb])
        for r in range(1, 7):
            kb.op("dve", lambda e: e.tensor_tensor(out=offb[:, r, :], in0=offb[:, r - 1, :], in1=totb[:, r - 1, :], op=ALU.add),
                  [offbb, totbb], [offbb])
        kb.op("dve", lambda e: e.tensor_scalar(out=offown[:], in0=totb[:, 0, :], scalar1=g.ctab[:, 72:73], scalar2=None, op0=ALU.mult),
              [totbb, g.ctabb], [offownb])
        for r in range(1, 7):
            kb.op("dve", lambda e: e.scalar_tensor_tensor(out=offown[:], in0=totb[:, r, :], scalar=g.ctab[:, 72 + r:73 + r], in1=offown[:],
                                                          op0=ALU.mult, op1=ALU.add), [totbb, offownb, g.ctabb], [offownb])
        for r in range(7):
            kb.op("dve", lambda e: e.tensor_tensor(out=doff[:, r, :], in0=offown[:], in1=offb[:, r, :], op=ALU.subtract),
                  [offownb, offbb], [doffb])
    g.hook()
    if kind == 0:
        lam, lamb = kb.sb(es, "lam", [128, 4], F32)
        kb.op("dve", lambda e: e.tensor_tensor(out=lam[:, 0:1], in0=sm[:, SM_LAM:SM_LAM + 1], in1=sm[:, SM_LAM + 1:SM_LAM + 2], op=ALU.mult), [smb], [lamb])
        kb.op("dve", lambda e: e.tensor_tensor(out=lam[:, 1:2], in0=sm[:, SM_LAM + 2:SM_LAM + 3], in1=sm[:, SM_LAM + 3:SM_LAM + 4], op=ALU.mult), [smb], [lamb])
        pt, pb = kb.ps()
        kb.op("pe", lambda e: e.matmul(pt[:, 0:2], lhsT=g.ones_f[:], rhs=lam[:, 0:2], start=True, stop=True), [lamb, g.ones_fb], [pb])
        kb.op("act", lambda e: e.activation(out=lam[:, 2:4], in_=pt[:, 0:2], func=AF.Exp), [pb], [lamb])
        kb.op("dve", lambda e: e.tensor_tensor(out=lam[:, 0:1], in0=lam[:, 3:4], in1=lam[:, 2:3], op=ALU.subtract), [lamb], [lamb])
        kb.op("dve", lambda e: e.tensor_scalar(out=lam[:, 0:1], in0=lam[:, 0:1], scalar1=-lambda_init(i), scalar2=None, op0=ALU.add), [lamb], [lamb])
        subg, subgb = kb.sb(es, "subg", [128, 2], F32)
        kb.op("dve", lambda e: e.tensor_scalar(out=subg[:], in0=sm[:, SM_SUBLN:SM_SUBLN + 2], scalar1=1.0 - lambda_init(i), scalar2=None,
                                               op0=ALU.mult), [smb], [subgb])

    Kpre = [kb.sb(es, "Kpre", [128, 7, T], BF16) for _ in range(NM)]
    Kown = [kb.sb(es, "Kown", [128, T], BF16) for _ in range(NM)]
    Vpre, Vpreb = kb.sb(es, "Vpre", [128, NPRE, E], BF16)
    Vown, Vownb = kb.sb(es, "Vown", [128, 8, E], BF16)
    qt = [kb.sb(es, "qt", [128, TH], BF16) for _ in range(2 * NM)]
    NH = 6 if kind == 2 else 1
    pts = [kb.sb(es, "pT", [128, TH], BF16) for _ in range(2 * NM * NH + 1)]
    tmps = [kb.sb(es, "atmp", [128, TH], F32) for _ in range(3)]
    npt = [0]
    if H:
        cqb = [kb.sb(es, "cqb", [128, TH], F32) for _ in range(NH)]
        bpre = [kb.sb(es, "bpre", [128, NPRE], F32) for _ in range(NH)]
        bown = [kb.sb(es, "bown", [128, 8], F32) for _ in range(NH)]
    ostg = Stager(kb, es, 3, BF16, "aostg")
    fin = [kb.sb(es, "afin", [128, TH], F32) for _ in range(6)]
    rden = [kb.sb(es, "rden", [128, TH], F32) for _ in range(2)]
    PS = kb.psum
    zero_col = None

    for u in range(NU):
        for m in range(NM):
            blk = (2 * u + m) if kind == 0 else u
            (lo, hi, A, B, kbufs) = kch[blk // 4]
            bsel = {"A": A, "B": B}
            Kp, Kpb = Kpre[m]
            for (bn, s0, r0) in PAIRS:
                n = 2 if r0 < 6 else 1
                v4 = bsel[bn].rearrange("(s c p) t -> p s c t", s=4, c=4, p=128)
                kb.dma("sp" if r0 % 4 == 0 else "pool", Kp[:, r0:r0 + n, :], v4[:, s0:s0 + n, blk % 4, :], list(kbufs), [Kpb])
            Ko, Kob = Kown[m]
            kb.dma("sp", Ko[:], g.kTb[blk * 128:(blk + 1) * 128, :], [], [Kob])
        c_lo, c_hi = u * E, (u + 1) * E
        Vp5 = Vpre[:].rearrange("p (r j) e -> p r j e", j=8)
        for (lo, hi, A, B, vbufs) in vch:
            a0, a1 = max(lo, c_lo), min(hi, c_hi)
            if a0 >= a1:
                continue
            bsel = {"A": A, "B": B}
            for (bn, s0, r0) in PAIRS:
                n = 2 if r0 < 6 else 1
                v4 = bsel[bn].rearrange("(s j p) c -> p s j c", s=4, j=8, p=128)
                kb.dma("sp" if r0 % 4 == 0 else "pool", Vp5[:, r0:r0 + n, :, a0 - c_lo:a1 - c_lo], v4[:, s0:s0 + n, :, a0 - lo:a1 - lo],
                       list(vbufs), [Vpreb])
        kb.dma("sp", Vown[:], g.vb[:, c_lo:c_hi].rearrange("(j p) c -> p j c", p=128), [], [Vownb])
        if H:
            for hh in range(NH):
                h = u * NH + hh
                bp, bpb = bpre[hh]
                bo, bob = bown[hh]
                kb.op("dve", lambda e: e.scalar_tensor_tensor(out=bp[:], in0=cumpre[:, :, h], scalar=-1.0, in1=g.ctab[:, 0:NPRE],
                                                              op0=ALU.mult, op1=ALU.add), [cumpreb, g.ctabb], [bpb])
                for r in range(7):
                    kb.op("dve", lambda e: e.tensor_scalar(out=bp[:, r * 8:(r + 1) * 8], in0=bp[:, r * 8:(r + 1) * 8],
                                                           scalar1=doff[:, r, h:h + 1], scalar2=None, op0=ALU.add), [bpb, doffb], [bpb])
                kb.op("dve", lambda e: e.tensor_scalar(out=bo[:], in0=cumown[:, :, h], scalar1=-1.0, scalar2=None, op0=ALU.mult),
                      [cumownb], [bob])
        for th in range(NTH):
            tsl = slice(th * TH, (th + 1) * TH)
            qs = []
            for m in range(NM):
                blk = (2 * u + m) if kind == 0 else u
                q_, qb_ = qt[(th * NM + m) % len(qt)]
                kb.dma("sp", q_[:], g.qT[blk * 128:(blk + 1) * 128, tsl], [], [qb_])
                qs.append((q_, qb_))
            if H:
                for hh in range(NH):
                    h = u * NH + hh
                    kb.dma("pool", cqb[hh][0][:], g.cumHMd[h, tsl].partition_broadcast(128), [], [cqb[hh][1]])
            slots = []
            for s_ in range(NPRE):
                slots.append(("pre", s_, None))
            for j in range(4 * th + 4):
                slots.append(("own", j, (j - 4 * th) if j >= 4 * th else None))
            nsl = len(slots)
            if kind == 0:
                obank = [[PS[0], PS[1]], [PS[2], PS[3]]]
                dbank = [PS[4], PS[5]]
                sbank = [PS[6], PS[7]]
            elif kind == 1:
                obank = [[PS[0]]]
                dbank = [PS[1]]
                sbank = [PS[2], PS[3], PS[4]]
            else:
                ybank = [PS[0], PS[1], PS[2], PS[3], PS[4], PS[5]]
                sbank = [PS[6], PS[7]]
            def stage_a(si):
                typ, idx, diag = slots[si]
                outl = []
                for m in range(NM):
                    if typ == "pre":
                        Kt = Kpre[m][0][:, idx // 8, (idx % 8) * 128:(idx % 8 + 1) * 128]
                        Ktb = Kpre[m][1]
                        Vsrc, Vtb = Vpre, Vpreb
                    else:
                        Kt = Kown[m][0][:, idx * 128:(idx + 1) * 128]
                        Ktb = Kown[m][1]
                        Vsrc, Vtb = Vown, Vownb
                    sT, sTb = sbank[(si * NM + m) % len(sbank)]
                    kb.op("pe", lambda e: e.matmul(sT[:, 0:TH], lhsT=Kt, rhs=qs[m][0][:], start=True, stop=True), [Ktb, qs[m][1]], [sTb])
                    if kind == 0:
                        p_, pb_ = pts[npt[0] % len(pts)]
                        npt[0] += 1
                        if diag is not None:
                            t_, tb_ = tmps[npt[0] % 3]
                            kb.op("dve", lambda e: e.tensor_tensor(out=t_[:], in0=sT[:, 0:TH], in1=g.maskadd[diag], op=ALU.add), [sTb, g.cstb], [tb_])
                            kb.op("act", lambda e: e.activation(out=p_[:], in_=t_[:], func=AF.Exp, scale=scale), [tb_], [pb_])
                        elif typ == "pre":
                            kb.op("act", lambda e: e.activation(out=p_[:], in_=sT[:, 0:TH], func=AF.Exp, scale=scale, bias=g.ctab[:, idx:idx + 1]),
                                  [sTb, g.ctabb], [pb_])
                        else:
                            kb.op("act", lambda e: e.activation(out=p_[:], in_=sT[:, 0:TH], func=AF.Exp, scale=scale), [sTb], [pb_])
                        outl.append((p_, pb_, Vsrc, Vtb, idx, m, 0))
                    elif kind == 1:
                        bcol = (bpre[0][0][:, idx:idx + 1], bpre[0][1]) if typ == "pre" else (bown[0][0][:, idx:idx + 1], bown[0][1])
                        t_, tb_ = tmps[npt[0] % 3]
                        p_, pb_ = pts[npt[0] % len(pts)]
                        npt[0] += 1
                        kb.op("dve", lambda e: e.scalar_tensor_tensor(out=t_[:], in0=sT[:, 0:TH], scalar=scale, in1=cqb[0][0][:], op0=ALU.mult, op1=ALU.add),
                              [sTb, cqb[0][1]], [tb_])
                        if diag is not None:
                            kb.op("dve", lambda e: e.tensor_tensor(out=t_[:], in0=t_[:], in1=g.maskadd[diag], op=ALU.add), [tb_, g.cstb], [tb_])
                        kb.op("act", lambda e: e.activation(out=p_[:], in_=t_[:], func=AF.Exp, bias=bcol[0]), [tb_, bcol[1]], [pb_])
                        outl.append((p_, pb_, Vsrc, Vtb, idx, 0, 0))
                    else:
                        for hh in range(6):
                            bcol = (bpre[hh][0][:, idx:idx + 1], bpre[hh][1]) if typ == "pre" else (bown[hh][0][:, idx:idx + 1], bown[hh][1])
                            t_, tb_ = tmps[npt[0] % 3]
                            p_, pb_ = pts[npt[0] % len(pts)]
                            npt[0] += 1
                            if diag is not None:
                                kb.op("dve", lambda e: e.tensor_tensor(out=t_[:], in0=cqb[hh][0][:], in1=g.maskadd[diag], op=ALU.add), [cqb[hh][1], g.cstb], [tb_])
                                kb.op("act", lambda e: e.activation(out=t_[:], in_=t_[:], func=AF.Exp, bias=bcol[0]), [tb_, bcol[1]], [tb_])
                            else:
                                kb.op("act", lambda e: e.activation(out=t_[:], in_=cqb[hh][0][:], func=AF.Exp, bias=bcol[0]), [cqb[hh][1], bcol[1]], [tb_])
                            kb.op("dve", lambda e: e.tensor_tensor(out=p_[:], in0=sT[:, 0:TH], in1=t_[:], op=ALU.mult), [sTb, tb_], [pb_])
                            outl.append((p_, pb_, Vsrc, Vtb, idx, 0, hh))
                return outl

            def stage_b(si, outl):
                first, last = (si == 0), (si == nsl - 1)
                for (p_, pb_, Vsrc, Vtb, idx, m, hh) in outl:
                    if kind == 0:
                        kb.op("pe", lambda e: e.matmul(dbank[m][0][:, 0:TH], lhsT=g.ones_h[:], rhs=p_[:], start=first, stop=last), [pb_, g.ones_hb], [dbank[m][1]])
                        for ec in range(2):
                            kb.op("pe", lambda e: e.matmul(obank[m][ec][0][:, 0:TH], lhsT=Vsrc[:, idx, ec * 128:(ec + 1) * 128], rhs=p_[:], start=first, stop=last),
                                  [pb_, Vtb], [obank[m][ec][1]])
                    elif kind == 1:
                        kb.op("pe", lambda e: e.matmul(dbank[0][0][:, 0:TH], lhsT=g.ones_h[:], rhs=p_[:], start=first, stop=last), [pb_, g.ones_hb], [dbank[0][1]])
                        kb.op("pe", lambda e: e.matmul(obank[0][0][0][:, 0:TH], lhsT=Vsrc[:, idx, 0:128], rhs=p_[:], start=first, stop=last), [pb_, Vtb], [obank[0][0][1]])
                    else:
                        yb = ybank[hh]
                        kb.op("pe", lambda e: e.matmul(yb[0][0:64, 0:TH], lhsT=Vsrc[:, idx, hh * 64:(hh + 1) * 64], rhs=p_[:],
                                                        start=first, stop=last), [pb_, Vtb], [yb[1]])

            prev = None
            for si in range(nsl + 1):
                cur = stage_a(si) if si < nsl else None
                if prev is not None:
                    stage_b(si - 1, prev)
                prev = cur
            if kind == 0:
                for m in range(2):
                    kb.op("dve", lambda e: e.reciprocal(out=rden[m][0][:], in_=dbank[m][0][:, 0:TH]), [dbank[m][1]], [rden[m][1]])
                outs = []
                for ec in range(2):
                    t1, t1b = fin[ec * 2]
                    t2, t2b = fin[ec * 2 + 1]
                    kb.op("dve", lambda e: e.tensor_tensor(out=t1[:], in0=obank[0][ec][0][:, 0:TH], in1=rden[0][0][:], op=ALU.mult), [obank[0][ec][1], rden[0][1]], [t1b])
                    kb.op("dve", lambda e: e.tensor_tensor(out=t2[:], in0=obank[1][ec][0][:, 0:TH], in1=rden[1][0][:], op=ALU.mult), [obank[1][ec][1], rden[1][1]], [t2b])
                    kb.op("dve", lambda e: e.scalar_tensor_tensor(out=t1[:], in0=t2[:], scalar=lam[:, 0:1], in1=t1[:], op0=ALU.mult, op1=ALU.add), [t2b, t1b, lamb], [t1b])
                    kb.op("act", lambda e: e.activation(out=t2[:], in_=t1[:], func=AF.Square), [t1b], [t2b])
                    kb.op("pe", lambda e: e.matmul(PS[6][0][:, 0:TH], lhsT=g.ones_f[:], rhs=t2[:], start=(ec == 0), stop=(ec == 1)), [t2b, g.ones_fb], [PS[6][1]])
                    outs.append((t1, t1b))
                rs_, rsb = fin[4]
                kb.op("dve", lambda e: e.tensor_scalar(out=rs_[:], in0=PS[6][0][:, 0:TH], scalar1=1.0 / 256, scalar2=EPS, op0=ALU.mult, op1=ALU.add), [PS[6][1]], [rsb])
                kb.op("act", lambda e: e.activation(out=rs_[:], in_=rs_[:], func=AF.Sqrt), [rsb], [rsb])
                kb.op("dve", lambda e: e.reciprocal(out=rs_[:], in_=rs_[:]), [rsb], [rsb])
                for ec in range(2):
                    o, ob = ostg.get()
                    kb.op("dve", lambda e: e.scalar_tensor_tensor(out=o[:], in0=outs[ec][0][:], scalar=subg[:, ec:ec + 1], in1=rs_[:], op0=ALU.mult, op1=ALU.mult),
                          [outs[ec][1], rsb, subgb], [ob])
                    kb.dma("pool", g.catT[u * 256 + ec * 128:u * 256 + (ec + 1) * 128, tsl], o[:], [ob], [])
            elif kind == 1:
                kb.op("dve", lambda e: e.reciprocal(out=rden[0][0][:], in_=dbank[0][0][:, 0:TH]), [dbank[0][1]], [rden[0][1]])
                o, ob = ostg.get()
                kb.op("dve", lambda e: e.tensor_tensor(out=o[:], in0=obank[0][0][0][:, 0:TH], in1=rden[0][0][:], op=ALU.mult), [obank[0][0][1], rden[0][1]], [ob])
                kb.dma("pool", g.catT[u * 128:(u + 1) * 128, tsl], o[:], [ob], [])
            else:
                gs = []
                for hh in range(6):
                    h = u * 6 + hh
                    xs_, xsb = fin[hh]
                    sz_, szb = ostg.get()
                    kb.dma("sp", xs_[0:64, :], g.xsT[h * 64:(h + 1) * 64, tsl], [], [xsb])
                    kb.dma("sp", sz_[0:64, :], g.szT[h * 64:(h + 1) * 64, tsl], [], [szb])
                    kb.op("dve", lambda e: e.scalar_tensor_tensor(out=xs_[0:64, :], in0=xs_[0:64, :], scalar=sm[0:64, SM_DSKIP + h:SM_DSKIP + h + 1],
                                                                  in1=ybank[hh][0][0:64, 0:TH], op0=ALU.mult, op1=ALU.add), [xsb, smb, ybank[hh][1]], [xsb])
                    kb.op("dve", lambda e: e.tensor_tensor(out=xs_[0:64, :], in0=xs_[0:64, :], in1=sz_[0:64, :], op=ALU.mult), [xsb, szb], [xsb])
                    sq_, sqb = tmps[hh % 3]
                    kb.op("act", lambda e: e.activation(out=sq_[0:64, :], in_=xs_[0:64, :], func=AF.Square), [xsb], [sqb])
                    kb.op("pe", lambda e: e.matmul(PS[6][0][:, 0:TH], lhsT=g.ones_f[0:64, :], rhs=sq_[0:64, :], start=(hh == 0), stop=(hh == 5)),
                          [sqb, g.ones_fb], [PS[6][1]])
                    gs.append((xs_, xsb))
                rs_, rsb = rden[0]
                kb.op("dve", lambda e: e.tensor_scalar(out=rs_[:], in0=PS[6][0][:, 0:TH], scalar1=1.0 / 384, scalar2=EPS, op0=ALU.mult, op1=ALU.add), [PS[6][1]], [rsb])
                kb.op("act", lambda e: e.activation(out=rs_[:], in_=rs_[:], func=AF.Sqrt), [rsb], [rsb])
                kb.op("dve", lambda e: e.reciprocal(out=rs_[:], in_=rs_[:]), [rsb], [rsb])
                for hh in range(6):
                    h = u * 6 + hh
                    o, ob = ostg.get()
                    kb.op("dve", lambda e: e.scalar_tensor_tensor(out=o[0:64, :], in0=gs[hh][0][0:64, :], scalar=sm[0:64, SM_NORMG + h:SM_NORMG + h + 1],
                                                                  in1=rs_[0:64, :], op0=ALU.mult, op1=ALU.mult), [gs[hh][1], rsb, smb], [ob])
                    kb.dma("pool", g.catT[h * 64:(h + 1) * 64, tsl], o[0:64, :], [ob], [])
    es.close()
    kb.barrier()


def memattn(kb, g):
    scale = 256.0 ** -0.5
    PS = kb.psum
    with ExitStack() as es:
        km, kmb = kb.sb(es, "km", [128, 8, MEMT], BF16)
        vm, vmb = kb.sb(es, "vmm", [128, 2, MEMW], BF16)
        kb.dma("sp", km[:], g.kmT.rearrange("(c p) m -> p c m", p=128), [], [kmb])
        kb.dma("sp", vm[:], g.vm.rearrange("(c p) e -> p c e", p=128), [], [vmb])
        qm = [kb.sb(es, "qm", [128, TH], BF16) for _ in range(4)]
        pts = [kb.sb(es, "mp", [128, TH], BF16) for _ in range(4)]
        rd, rdb = kb.sb(es, "mrd", [128, TH], F32)
        ostg = Stager(kb, es, 3, BF16, "mostg")
        n = 0
        for hm in range(4):
            for th in range(NTH):
                tsl = slice(th * TH, (th + 1) * TH)
                qq = []
                for dc in range(2):
                    q_, qb_ = qm[n % 4]
                    n += 1
                    r0 = hm * 256 + dc * 128
                    kb.dma("sp", q_[:], g.qmT[r0:r0 + 128, tsl], [], [qb_])
                    qq.append((q_, qb_))
                pp = []
                for mc in range(2):
                    sT, sTb = PS[3 + mc]
                    for dc in range(2):
                        kb.op("pe", lambda e: e.matmul(sT[:, 0:TH], lhsT=km[:, hm * 2 + dc, mc * 128:(mc + 1) * 128], rhs=qq[dc][0][:],
                                                        start=(dc == 0), stop=(dc == 1)), [kmb, qq[dc][1]], [sTb])
                    p_, pb_ = pts[(n + mc) % 4]
                    kb.op("act", lambda e: e.activation(out=p_[:], in_=sT[:, 0:TH], func=AF.Exp, scale=scale), [sTb], [pb_])
                    pp.append((p_, pb_))
                for mc in range(2):
                    kb.op("pe", lambda e: e.matmul(PS[0][0][:, 0:TH], lhsT=g.ones_h[:], rhs=pp[mc][0][:], start=(mc == 0), stop=(mc == 1)),
                          [pp[mc][1], g.ones_hb], [PS[0][1]])
                for ec in range(2):
                    for mc in range(2):
                        c0 = hm * 256 + ec * 128
                        kb.op("pe", lambda e: e.matmul(PS[1 + ec][0][:, 0:TH], lhsT=vm[:, mc, c0:c0 + 128], rhs=pp[mc][0][:], start=(mc == 0), stop=(mc == 1)),
                              [pp[mc][1], vmb], [PS[1 + ec][1]])
                kb.op("dve", lambda e: e.reciprocal(out=rd[:], in_=PS[0][0][:, 0:TH]), [PS[0][1]], [rdb])
                for ec in range(2):
                    o, ob = ostg.get()
                    kb.op("dve", lambda e: e.tensor_tensor(out=o[:], in0=PS[1 + ec][0][:, 0:TH], in1=rd[:], op=ALU.mult), [PS[1 + ec][1], rdb], [ob])
                    r0 = TOKW + hm * 256 + ec * 128
                    kb.dma("pool", g.catT[r0:r0 + 128, tsl], o[:], [ob], [])
    kb.barrier()


def host_constants():
    cst = np.zeros((128, CST_COLS), np.float32)
    cst[:, 0:128] = np.eye(128, dtype=np.float32)
    p = np.arange(128)[:, None]
    f = np.arange(512)[None, :]
    for r in range(4):
        cst[:, 128 + r * 512:128 + (r + 1) * 512] = np.where(f >= r * 128 + p, 0.0, NEG)
    sw = np.zeros((32, 32), np.float32)
    for m in range(32):
        sw[(m + 16) % 32, m] = 1.0
    cst[0:32, 128 + 2048:128 + 2048 + 32] = sw
    half = 16
    invf = np.power(np.float32(ROPE_THETA), -np.arange(half, dtype=np.float32) * np.float32(2.0 / 32)).astype(np.float32)
    cst[0:32, 128 + 2048 + 32] = np.concatenate([invf, invf])
    cst[0:32, 128 + 2048 + 33] = np.concatenate([-np.ones(16), np.ones(16)])
    return cst


def host_ctab(c):
    t = np.zeros((128, 80), np.float32)
    for s in range(64):
        t[:, s] = 0.0 if s < 8 * c else NEG
    for j in range(8):
        t[:, 64 + j] = 1.0 if j == c - 1 else 0.0
        t[:, 72 + j] = 1.0 if j < c else 0.0
    return t


def pk(v):
    v = np.asarray(v, np.float32)
    return np.ascontiguousarray(v.reshape(-1, 128).T)


def host_inputs(inp, layers=(0, 1, 2, 3), parts=("mix", "ffn"), x_override=None):
    cst = host_constants()
    gains = np.concatenate([pk(inp["norm_mix"][i]) for i in range(4)] + [pk(inp["norm_ffn"][i]) for i in range(4)] +
                           [pk(inp["norm_mem"][i]) for i in range(4)] + [pk(inp["final_norm"])], axis=1)
    x = inp["x"][0] if x_override is None else x_override
    maps = []
    shared = {}
    for i in layers:
        cw = np.concatenate([inp["conv_ffn_w"][i].T, inp["conv_ffn_b"][i][:, None]], axis=1)
        shared["cwf%d" % i] = np.ascontiguousarray(cw.reshape(2 * FC, 128, 4).transpose(1, 0, 2))
    if "mix" in parts:
        shared.update(host_mixer_shared(inp, layers))
    for c in range(NCORES):
        m = dict(shared)
        m["xT"] = np.ascontiguousarray(x[c * T:(c + 1) * T, :].T)
        m["cst"] = cst
        m["ctab"] = host_ctab(c)
        m["gains"] = gains
        r0, r1 = c * 512, (c + 1) * 512
        for i in layers:
            if "ffn" in parts:
                m["w_upg%d" % i] = np.ascontiguousarray(inp["w_up"][i][r0:r1, :DFF])
                m["w_upv%d" % i] = np.ascontiguousarray(inp["w_up"][i][r0:r1, DFF:])
                m["w_down%d" % i] = np.ascontiguousarray(inp["w_down"][i][c * (DFF // 8):(c + 1) * (DFF // 8), :])
            if "mix" in parts:
                kind, j = i % 3, i // 3
                w = (inp["a_w_in"], inp["b_w_in"], inp["c_w_in"])[kind][j]
                m["w_in%d" % i] = np.ascontiguousarray(w[r0:r1, :])
                m["w_mkv%d" % i] = np.ascontiguousarray(inp["w_mem_kv"][i][r0:r1, :])
                m["w_out%d" % i] = np.ascontiguousarray(inp["w_out"][i][r0:r1, :])
        if "mix" in parts:
            m.update(host_mixer_percore(inp, c))
        maps.append(m)
    return maps


def host_mixer_shared(inp, layers):
    out = {"memT": np.ascontiguousarray(inp["mem"][0].T)}
    for i in layers:
        kind, j = i % 3, i // 3
        sm = np.zeros((128, SMALL_COLS), np.float32)
        if kind == 0:
            sm[:, SM_LAM:SM_LAM + 4] = inp["a_lambda"][j].T
            sm[:, SM_SUBLN:SM_SUBLN + 2] = pk(inp["a_subln"][j])
        elif kind == 1:
            sm[0:24, SM_FB] = inp["b_forget_bias"][j]
        else:
            sm[0:48, SM_DTB] = inp["c_dt_bias"][j]
            sm[0:48, SM_ALOG] = inp["c_a_log"][j]
            sm[:, SM_DSKIP:SM_DSKIP + 48] = np.broadcast_to(inp["c_d_skip"][j][None, :], (128, 48))
            sm[0:64, SM_NORMG:SM_NORMG + 48] = inp["c_norm_gate"][j].reshape(48, 64).T
            cw = np.concatenate([inp["c_conv_w"][j].T, inp["c_conv_b"][j][:, None]], axis=1)
            sm[:, SM_CONV:SM_CONV + 200] = cw.reshape(40, 128, 5).transpose(1, 0, 2).reshape(128, 200)
        out["small%d" % i] = sm
    return out


def host_mixer_percore(inp, c):
    return {"pos": np.ascontiguousarray(inp["positions"][:, c * T:(c + 1) * T].astype(np.int32))}


_NC_CACHE = {}


def kernel(**inputs):
    inp = {k: np.asarray(v) for k, v in inputs.items()}
    if "full" not in _NC_CACHE:
        _NC_CACHE["full"] = build()
    nc = _NC_CACHE["full"]
    maps = host_inputs(inp)
    res = run_bass_kernel_spmd(nc, maps, core_ids=list(range(NCORES)))
    out = np.concatenate([np.asarray(res.results[c]["yT"]).T for c in range(NCORES)], axis=0)
    return np.ascontiguousarray(out[None].astype(np.float32))
```

```python
import math
from contextlib import ExitStack
import numpy as np
import concourse.bass as bass
import concourse.mybir as mybir
from concourse.bass_utils import run_bass_kernel_spmd

F32 = mybir.dt.float32
BF16 = mybir.dt.bfloat16
ALU = mybir.AluOpType
AF = mybir.ActivationFunctionType

NCORES = 8
D = 4096
KC = D // 128
SEQ = 8192
T = SEQ // NCORES
TH = 512
NTH = T // TH
DEPTH = 4
MEMW = 1024
TOKW = 3072
MEMT = 256
DFF = 11008
FC = DFF // 128
EPS = 1e-6
NEG = -1.0e30
A_COLS = 3 * TOKW + MEMW
B_COLS = 3 * TOKW + 24 + MEMW
C_COLS = TOKW + 5120 + 48 + MEMW
NPRE = 56
ROPE_THETA = 500000.0


def lambda_init(i):
    return 0.8 - 0.6 * math.exp(-0.3 * i)


class Buf:
    __slots__ = ("w", "r", "name")

    def __init__(self, name=""):
        self.w = None
        self.r = {}
        self.name = name


class KB:
    def __init__(self, nc, es):
        self.nc = nc
        self.es = es
        self.eng = {"pe": nc.tensor, "act": nc.scalar, "dve": nc.vector, "pool": nc.gpsimd, "sp": nc.sync, "cc": nc.gpsimd}
        self.qmap = {"pool": "sp"}
        self.esem = {}
        self.ecnt = {}
        for q in ("pe", "act", "dve", "pool"):
            self.esem[q] = es.enter_context(nc.semaphore("s_" + q))
            self.ecnt[q] = 0
        self.waited = {q: {} for q in self.eng}
        self.waited["cc"] = self.waited["pool"]
        self.dpool = {}
        self.dnext = {}
        for q, n in (("sp", 32), ("cc", 16), ("act", 4)):
            self.dpool[q] = [[es.enter_context(nc.semaphore("d_%s%d" % (q, i))), 0] for i in range(n)]
            self.dnext[q] = 0
        self.ccpool = [[es.enter_context(nc.semaphore("s_cc%d" % i)), 0] for i in range(6)]
        self.ccnext = 0
        self.psum = []
        for i in range(8):
            t = es.enter_context(nc.psum_tensor("ps%d" % i, [128, 512], F32))
            self.psum.append((t, Buf("ps%d" % i)))
        self.psn = 0
        self.nuid = 0
        self.hslots = [Buf("hs%d" % i) for i in range(16)]
        self.hsn = 0

    def uid(self, p="t"):
        self.nuid += 1
        return "%s%d" % (p, self.nuid)

    def _wait(self, q, tok):
        if tok is None:
            return
        sem, val, src = tok
        if q == "pe" and src == "pe":
            return
        key = id(sem)
        if self.waited[q].get(key, 0) >= val:
            return
        self.eng[q].wait_ge(sem, val)
        self.waited[q][key] = val

    def deps(self, q, reads, writes):
        for b in reads:
            self._wait(q, b.w)
        for b in writes:
            self._wait(q, b.w)
            for t in b.r.values():
                self._wait(q, t)

    def done(self, tok, reads, writes):
        key = id(tok[0])
        for b in reads:
            old = b.r.get(key)
            if old is None or old[1] < tok[1]:
                b.r[key] = tok
        for b in writes:
            b.w = tok
            b.r = {}

    def op(self, q, fn, reads=(), writes=()):
        self.deps(q, reads, writes)
        ins = fn(self.eng[q])
        self.ecnt[q] += 1
        ins.then_inc(self.esem[q], 1)
        self.done((self.esem[q], self.ecnt[q], q), reads, writes)

    def dma(self, q, out, in_, reads=(), writes=()):
        q = self.qmap.get(q, q)
        pool = self.dpool[q]
        i = self.dnext[q]
        self.dnext[q] = (i + 1) % len(pool)
        sem, cnt = pool[i]
        if cnt > 0:
            self._wait(q, (sem, cnt, "dma"))
        self.deps(q, reads, writes)
        self.eng[q].dma_start(out=out, in_=in_).then_inc(sem, 16)
        pool[i][1] = cnt + 16
        self.done((sem, cnt + 16, "dma"), reads, writes)

    def allgather(self, groups, in_ap, out_ap, reads, writes):
        q = "cc"
        i = self.ccnext
        self.ccnext = (i + 1) % len(self.ccpool)
        sem, cnt = self.ccpool[i]
        if cnt > 0:
            self._wait(q, (sem, cnt, "cc"))
        self.deps(q, reads, writes)
        self.nc.gpsimd.collective_compute("AllGather", ALU.bypass, replica_groups=groups,
                                          ins=[in_ap], outs=[out_ap]).then_inc(sem, 1)
        self.ccpool[i][1] = cnt + 1
        self.done((sem, cnt + 1, "cc"), reads, writes)

    def barrier(self, full=False):
        toks = [(self.esem[q], self.ecnt[q], q) for q in self.esem if self.ecnt[q] > 0]
        for q in self.dpool:
            if q == "cc" and not full:
                continue
            for sem, cnt in self.dpool[q]:
                if cnt > 0:
                    toks.append((sem, cnt, "dma"))
        if full:
            for sem, cnt in self.ccpool:
                if cnt > 0:
                    toks.append((sem, cnt, "cc"))
        for q in ("pe", "act", "dve", "sp", "cc"):
            for t in toks:
                sem, val, src = t
                if src == q and q != "pe" and src in self.esem:
                    pass
                key = id(sem)
                if self.waited[q].get(key, 0) >= val:
                    continue
                self.eng[q].wait_ge(sem, val)
                self.waited[q][key] = val

    def ps(self):
        t, b = self.psum[self.psn % 7]
        self.psn += 1
        return t, b

    def hps(self):
        i = self.hsn % 16
        self.hsn += 1
        return self.psum[7][0][:, i * 8:(i + 1) * 8], self.hslots[i]

    def hps2(self):
        return self.psum[7]

    def sb(self, es, name, shape, dt):
        t = es.enter_context(self.nc.sbuf_tensor(self.uid(name), list(shape), dt))
        return t, Buf(name)

    def dram(self, name, shape, dt):
        t = self.nc.dram_tensor(self.uid(name), list(shape), dt)
        return t.ap(), Buf(name)


def two_stage_allgather(kb, src_ap, src_buf, rows, cols, dt, name):
    g1, g1b = kb.dram(name + "_g1", [4 * rows, cols], dt)
    g2, g2b = kb.dram(name + "_g2", [8 * rows, cols], dt)
    kb.allgather([[0, 1, 2, 3], [4, 5, 6, 7]], src_ap, g1, [src_buf], [g1b])
    kb.allgather([[0, 4], [1, 5], [2, 6], [3, 7]], g1, g2, [g1b], [g2b])
    return g2, g2b


class G:
    pass


class GW:
    def __init__(self, rs, cols, pw):
        self.rs, self.cols, self.pw = rs, cols, pw
        self.pieces = []
        self.cpr = (rs + 127) // 128
        self.nkc = 8 * self.cpr

    def ksz(self, kc):
        j = kc % self.cpr
        return min(128, self.rs - j * 128)

    def load(self, kb, qs, wt, wb, c0, w, k0, k1):
        nq = 0
        for (p0, pw, A, B, pbuf) in self.pieces:
            lo, hi = max(c0, p0), min(c0 + w, p0 + pw)
            if lo >= hi:
                continue
            for r in range(8):
                kr0, kr1 = r * self.cpr, (r + 1) * self.cpr
                a0, a1 = max(k0, kr0), min(k1, kr1)
                if a0 >= a1:
                    continue
                buf = A if r in (0, 1, 4, 5) else B
                slot = {0: 0, 1: 1, 4: 2, 5: 3, 2: 0, 3: 1, 6: 2, 7: 3}[r]
                base = slot * self.rs
                nfull = self.rs // 128
                f0, f1 = a0 - kr0, min(a1 - kr0, nfull)
                if f1 > f0:
                    src = buf[base + f0 * 128:base + f1 * 128, lo - p0:hi - p0].rearrange("(kc p) n -> p kc n", p=128)
                    kb.dma(qs[nq % len(qs)], wt[:, a0 - k0:a0 - k0 + (f1 - f0), lo - c0:hi - c0], src, list(pbuf), [wb])
                    nq += 1
                if a1 - kr0 > nfull:
                    rem = self.rs - nfull * 128
                    src = buf[base + nfull * 128:base + self.rs, lo - p0:hi - p0]
                    kb.dma(qs[nq % len(qs)], wt[0:rem, kr0 + nfull - k0, lo - c0:hi - c0], src, list(pbuf), [wb])
                    nq += 1


def prep_weight(kb, g, name, rows, cols, pw):
    nc = kb.nc
    src = nc.dram_tensor(name, [rows, cols], F32, kind="ExternalInput").ap()
    gw = GW(rows, cols, pw)
    gw.buf = Buf(name)
    for c0 in range(0, cols, pw):
        w = min(pw, cols - c0)
        bnc, bb = kb.dram(name + "_b", [rows, w], BF16)
        kb.dma("cc", bnc[:, :], src[:, c0:c0 + w], [], [bb])
        g1, g1b = kb.dram(name + "_g1", [4 * rows, w], BF16)
        kb.allgather([[0, 1, 2, 3], [4, 5, 6, 7]], bnc, g1, [bb], [g1b])
        A, _ = kb.dram(name + "_A", [4 * rows, w], BF16)
        B, _ = kb.dram(name + "_B", [4 * rows, w], BF16)
        pbuf = Buf(name)
        pbuf2 = Buf(name)
        kb.allgather([[0, 4], [1, 5], [2, 6], [3, 7]], g1[0:2 * rows, :], A, [g1b], [pbuf])
        kb.allgather([[0, 4], [1, 5], [2, 6], [3, 7]], g1[2 * rows:4 * rows, :], B, [g1b], [pbuf2])
        gw.pieces.append((c0, w, A, B, (pbuf, pbuf2)))
    return gw


def rmsnorm_fm(kb, g, x_ap, gcol0, out_ap, ncols, nkc=KC, dim=D, tails=None):
    nc = kb.nc
    with ExitStack() as es:
        xt = [kb.sb(es, "nx", [128, ncols], F32) for _ in range(3)]
        sq = [kb.sb(es, "nsq", [128, ncols], F32) for _ in range(2)]
        rstd, rstdb = kb.sb(es, "nrstd", [128, ncols], F32)
        ho = [kb.sb(es, "nh", [128, ncols], BF16) for _ in range(2)]
        nh = (ncols + 511) // 512
        pss = [kb.ps() for _ in range(nh)]
        for kc in range(nkc):
            x, xb = xt[kc % 3]
            kb.dma("sp", x[:], x_ap[kc * 128:(kc + 1) * 128, :], [], [xb])
            s, sb_ = sq[kc % 2]
            kb.op("act", lambda e: e.activation(out=s[:], in_=x[:], func=AF.Square), [xb], [sb_])
            for h in range(nh):
                c0, c1 = h * 512, min(ncols, (h + 1) * 512)
                pt, pb = pss[h]
                kb.op("pe", lambda e: e.matmul(pt[:, 0:c1 - c0], lhsT=g.ones_f[:], rhs=s[:, c0:c1],
                                                start=(kc == 0), stop=(kc == nkc - 1)), [sb_, g.ones_fb], [pb])
        for h in range(nh):
            c0, c1 = h * 512, min(ncols, (h + 1) * 512)
            pt, pb = pss[h]
            kb.op("dve", lambda e: e.tensor_scalar(out=rstd[:, c0:c1], in0=pt[:, 0:c1 - c0], scalar1=1.0 / dim,
                                                   scalar2=EPS, op0=ALU.mult, op1=ALU.add), [pb], [rstdb])
        kb.op("act", lambda e: e.activation(out=rstd[:], in_=rstd[:], func=AF.Sqrt), [rstdb], [rstdb])
        kb.op("dve", lambda e: e.reciprocal(out=rstd[:], in_=rstd[:]), [rstdb], [rstdb])
        for kc in range(nkc):
            x, xb = xt[kc % 3]
            kb.dma("sp", x[:], x_ap[kc * 128:(kc + 1) * 128, :], [], [xb])
            h_, hb = ho[kc % 2]
            kb.op("dve", lambda e: e.scalar_tensor_tensor(out=h_[:], in0=x[:], scalar=g.gains[:, gcol0 + kc:gcol0 + kc + 1],
                                                          in1=rstd[:], op0=ALU.mult, op1=ALU.mult),
                  [xb, rstdb, g.gainsb], [hb])
            kb.dma("pool", out_ap[kc * 128:(kc + 1) * 128, :], h_[:], [hb], [])
            if tails is not None:
                kb.op("act", lambda e: e.activation(out=tails[0][:, kc, :], in_=h_[:, ncols - 3:ncols], func=AF.Copy), [hb], [tails[1]])
    kb.barrier()


def dense(kb, g, rhs_loader, nkc, w_aps, tiles, evac, nth=NTH, ntok=TH, halo=0, halo_evac=None, kgrp=None, halo_ths=None):
    if kgrp is None:
        kgrp = nkc if nkc <= 43 else (nkc + 1) // 2
    ngrp = (nkc + kgrp - 1) // kgrp
    with ExitStack() as es:
        rhs, rhsb = kb.sb(es, "rhs", [128, nkc, halo + ntok], BF16)
        NW = 3
        wts = [kb.sb(es, "wt", [128, kgrp, 256], BF16) for _ in range(NW)]
        jobs = [(th, ti, kg) for th in range(nth) for ti in range(len(tiles)) for kg in range(ngrp)]

        def load_w(j):
            th, ti, kg = jobs[j]
            mode, wi, c0, w = tiles[ti]
            k0 = kg * kgrp
            k1 = min(nkc, k0 + kgrp)
            wt, wb = wts[j % NW]
            w_aps[wi].load(kb, ("sp", "pool") if j % 2 == 0 else ("pool", "sp"), wt, wb, c0, w, k0, k1)

        for j in range(min(2, len(jobs))):
            load_w(j)
        cur = None
        for j, (th, ti, kg) in enumerate(jobs):
            if j + 2 < len(jobs):
                load_w(j + 2)
            if ti == 0 and kg == 0:
                rhs_loader(th, rhs, rhsb)
            mode, wi, c0, w = tiles[ti]
            k0 = kg * kgrp
            k1 = min(nkc, k0 + kgrp)
            wt, wb = wts[j % NW]
            if mode == "fm":
                nsub = (w + 127) // 128
                if kg == 0:
                    cur = [kb.ps() for _ in range(nsub)]
                    do_h = bool(halo) and (halo_ths is None or th in halo_ths)
                    curh = [kb.hps() for _ in range(nsub)] if do_h else None
                for sub in range(nsub):
                    ws = min(128, w - sub * 128)
                    pt, pb = cur[sub]
                    for kc in range(k0, k1):
                        kz = w_aps[wi].ksz(kc)
                        kb.op("pe", lambda e: e.matmul(pt[0:ws, 0:ntok], lhsT=wt[0:kz, kc - k0, sub * 128:sub * 128 + ws],
                                                        rhs=rhs[0:kz, kc, halo:halo + ntok], start=(kc == 0), stop=(kc == nkc - 1)),
                              [wb, rhsb], [pb])
                    if halo and (halo_ths is None or th in halo_ths):
                        ph, phb = curh[sub]
                        for kc in range(k0, k1):
                            kb.op("pe", lambda e: e.matmul(ph[0:ws, 0:halo], lhsT=wt[:, kc - k0, sub * 128:sub * 128 + ws],
                                                            rhs=rhs[:, kc, 0:halo], start=(kc == 0), stop=(kc == nkc - 1)),
                                  [wb, rhsb], [phb])
                if kg == ngrp - 1:
                    for sub in range(nsub):
                        ws = min(128, w - sub * 128)
                        if halo:
                            if halo_ths is None or th in halo_ths:
                                halo_evac(wi, c0 + sub * 128, ws, th, curh[sub][0], curh[sub][1])
                            else:
                                halo_evac(wi, c0 + sub * 128, ws, th, None, None)
                        evac("fm", wi, c0 + sub * 128, ws, th, cur[sub][0], cur[sub][1])
            else:
                ntt = ntok // 128
                if kg == 0:
                    cur = [kb.ps() for _ in range(ntt)]
                for tt in range(ntt):
                    pt, pb = cur[tt]
                    for kc in range(k0, k1):
                        kb.op("pe", lambda e: e.matmul(pt[:, 0:w], lhsT=rhs[:, kc, halo + tt * 128:halo + (tt + 1) * 128],
                                                        rhs=wt[:, kc - k0, 0:w], start=(kc == 0), stop=(kc == nkc - 1)),
                              [wb, rhsb], [pb])
                if kg == ngrp - 1:
                    for tt in range(ntt):
                        evac("tm", wi, c0, w, (th, tt), cur[tt][0], cur[tt][1])
    kb.barrier()


def exchange_halo(kb, g, tails):
    tl, tlb = tails
    bnc, bb = kb.dram("tailb", [128, KC * 3], BF16)
    kb.dma("cc", bnc[:, :], tl[:].rearrange("p k h -> p (k h)"), [tlb], [bb])
    ga, gab = two_stage_allgather(kb, bnc, bb, 128, KC * 3, BF16, "tail")
    with ExitStack() as es:
        al, alb = kb.sb(es, "tall", [128, 8, KC * 3], BF16)
        acc, accb = kb.sb(es, "tacc", [128, KC * 3], F32)
        kb.dma("pool", al[:], ga.rearrange("(r p) x -> p r x", p=128), [gab], [alb])
        for j in range(8):
            sel = g.ctab[:, 64 + j:65 + j]
            if j == 0:
                kb.op("dve", lambda e: e.tensor_scalar(out=acc[:], in0=al[:, j, :], scalar1=sel, scalar2=None, op0=ALU.mult),
                      [alb, g.ctabb], [accb])
            else:
                kb.op("dve", lambda e: e.scalar_tensor_tensor(out=acc[:], in0=al[:, j, :], scalar=sel, in1=acc[:],
                                                              op0=ALU.mult, op1=ALU.add), [alb, accb, g.ctabb], [accb])
        kb.op("dve", lambda e: e.tensor_copy(out=g.halo[:].rearrange("p k h -> p (k h)"), in_=acc[:]), [accb], [g.halob])
    kb.barrier()


def make_rhs_loader(kb, g, src_ap, nkc, halo, rs=512):
    cpr = (rs + 127) // 128
    nfull = rs // 128

    def loader(th, rhs, rhsb):
        if rs % 128 == 0:
            kb.dma("sp", rhs[:, :, halo:halo + TH], src_ap[:, th * TH:(th + 1) * TH].rearrange("(kc p) t -> p kc t", p=128), [], [rhsb])
        else:
            for r in range(8):
                q = "sp" if r % 2 == 0 else "pool"
                kb.dma(q, rhs[:, r * cpr:r * cpr + nfull, halo:halo + TH],
                       src_ap[r * rs:r * rs + nfull * 128, th * TH:(th + 1) * TH].rearrange("(kc p) t -> p kc t", p=128), [], [rhsb])
                kb.dma(q, rhs[0:rs - nfull * 128, r * cpr + nfull, halo:halo + TH],
                       src_ap[r * rs + nfull * 128:(r + 1) * rs, th * TH:(th + 1) * TH], [], [rhsb])
        if halo:
            if th == 0:
                kb.op("dve", lambda e: e.tensor_copy(out=rhs[:, :, 0:halo], in_=g.halo[:, :, 3 - halo:3]), [g.halob], [rhsb])
            else:
                kb.dma("sp", rhs[:, :, 0:halo], src_ap[:, th * TH - halo:th * TH].rearrange("(kc p) t -> p kc t", p=128), [], [rhsb])
    return loader


class Stager:
    def __init__(self, kb, es, n, dt, name="stg", width=512):
        self.kb = kb
        self.t = [kb.sb(es, name, [128, width], dt) for _ in range(n)]
        self.i = 0

    def get(self):
        r = self.t[self.i % len(self.t)]
        self.i += 1
        return r


def conv_silu(kb, uext, uextb, K, cw, cwb, blk, acc, accb, width):
    for k in range(K - 1, -1, -1):
        src = uext[:, k:k + width]
        wk = cw[:, blk, k:k + 1]
        if k == K - 1:
            kb.op("dve", lambda e: e.tensor_scalar(out=acc[:, 0:width], in0=src, scalar1=wk, scalar2=cw[:, blk, K:K + 1],
                                                   op0=ALU.mult, op1=ALU.add), [uextb, cwb], [accb])
        else:
            kb.op("dve", lambda e: e.scalar_tensor_tensor(out=acc[:, 0:width], in0=src, scalar=wk, in1=acc[:, 0:width],
                                                          op0=ALU.mult, op1=ALU.add), [uextb, cwb, accb], [accb])


def residual_dense(kb, g, src_ap, gw):
    with ExitStack() as es:
        xin = Stager(kb, es, 3, F32, "rx")
        xout = Stager(kb, es, 3, F32, "ro")

        def evac(mode, wi, c0, ws, th, pt, pb):
            xi, xib = xin.get()
            xo, xob = xout.get()
            reg = g.xres[c0:c0 + ws, th * TH:(th + 1) * TH]
            kb.dma("pool", xi[0:ws, :], reg, [], [xib])
            kb.op("dve", lambda e: e.tensor_tensor(out=xo[0:ws, :], in0=pt[0:ws, 0:TH], in1=xi[0:ws, :], op=ALU.add),
                  [pb, xib], [xob])
            kb.dma("pool", reg, xo[0:ws, :], [xob], [])

        tiles = [("fm", 0, c0, 256) for c0 in range(0, D, 256)]
        dense(kb, g, make_rhs_loader(kb, g, src_ap, gw.nkc, 0, rs=gw.rs), gw.nkc, [gw], tiles, evac)


def ffn_phase(kb, g, i):
    nc = kb.nc
    with ExitStack() as es0:
        tails = kb.sb(es0, "tails", [128, KC, 3], BF16)
        rmsnorm_fm(kb, g, g.xres, g.gcol["ffn"][i], g.hT, T, tails=tails)
        exchange_halo(kb, g, tails)
    g.hook()
    with ExitStack() as es:
        cw, cwb = kb.sb(es, "cwf", [128, 2 * FC, 4], F32)
        kb.dma("sp", cw[:], g.in_cwf[i], [], [cwb])
        uexts = [kb.sb(es, "uext", [128, 2 + TH], F32) for _ in range(3)]
        accs = [kb.sb(es, "facc", [128, TH], F32) for _ in range(2)]
        gates = {}
        gate_t = [kb.sb(es, "gate", [128, TH], F32) for _ in range(4)]
        outs = Stager(kb, es, 3, BF16, "fo")
        st = {"u": 0, "a": 0, "g": 0}
        pend = {}

        usave, usaveb = kb.sb(es, "usave", [128, 2 * FC, 2], F32)

        def halo_evac(wi, c0, ws, th, ph, phb):
            ue, ueb = uexts[st["u"] % 3]
            blk_ = c0 // 128 + (FC if wi == 1 else 0)
            if ph is None:
                kb.op("act", lambda e: e.activation(out=ue[:, 0:2], in_=usave[:, blk_, :], func=AF.Copy), [usaveb], [ueb])
            else:
                kb.op("act", lambda e: e.activation(out=ue[:, 0:2], in_=ph[:, 0:2], func=AF.Copy), [phb], [ueb])
            pend[(wi, c0)] = (ue, ueb)
            st["u"] += 1

        def evac(mode, wi, c0, ws, th, pt, pb):
            ue, ueb = pend.pop((wi, c0))
            kb.op("act", lambda e: e.activation(out=ue[:, 2:2 + TH], in_=pt[:, 0:TH], func=AF.Copy), [pb], [ueb])
            if th == 0:
                blk_ = c0 // 128 + (FC if wi == 1 else 0)
                kb.op("act", lambda e: e.activation(out=usave[:, blk_, :], in_=pt[:, TH - 2:TH], func=AF.Copy), [pb], [usaveb])
            ac, acb = accs[st["a"] % 2]
            st["a"] += 1
            blk = c0 // 128 + (FC if wi == 1 else 0)
            conv_silu(kb, ue, ueb, 3, cw, cwb, blk, ac, acb, TH)
            if wi == 0:
                gt, gtb = gate_t[st["g"] % 4]
                st["g"] += 1
                kb.op("act", lambda e: e.activation(out=gt[:], in_=ac[:], func=AF.Silu), [acb], [gtb])
                gates[c0] = (gt, gtb)
            else:
                gt, gtb = gates.pop(c0)
                o, ob = outs.get()
                kb.op("dve", lambda e: e.tensor_tensor(out=o[:], in0=ac[:], in1=gt[:], op=ALU.mult), [acb, gtb], [ob])
                kb.dma("pool", g.actT[c0:c0 + 128, th * TH:(th + 1) * TH], o[:], [ob], [])

        tiles = []
        for c0 in range(0, DFF, 256):
            tiles.append(("fm", 0, c0, 256))
            tiles.append(("fm", 1, c0, 256))
        dense(kb, g, make_rhs_loader(kb, g, g.hT, KC, 2), KC, [g.w_upg[i], g.w_upv[i]], tiles, evac, halo=2, halo_evac=halo_evac, halo_ths=(0,))
    residual_dense(kb, g, g.actT, g.w_down[i])


IN_COLS = [A_COLS, B_COLS, C_COLS, A_COLS]
CST_COLS = 128 + 4 * 512 + 32 + 2


def build(layers=(0, 1, 2, 3), parts=("mix", "ffn"), final=True, dbg=()):
    nc = bass.Bass("TRN2", target_bir_lowering=False)
    with ExitStack() as es:
        block = es.enter_context(nc.Block())
        kb = KB(nc, es)
        g = G()
        g.layers = layers

        def ext(name, shape, dt=F32):
            return nc.dram_tensor(name, list(shape), dt, kind="ExternalInput").ap()

        in_xT = ext("xT", [D, T])
        in_cst = ext("cst", [128, CST_COLS])
        in_ctab = ext("ctab", [128, 80])
        in_gains = ext("gains", [128, 13 * KC])
        g.gcol = {"mix": [i * KC for i in range(4)], "ffn": [(4 + i) * KC for i in range(4)],
                  "mem": [(8 + i) * KC for i in range(4)], "final": 12 * KC}
        g.in_cwf = {}
        yT = nc.dram_tensor("yT", [D, T], F32, kind="ExternalOutput").ap()
        g.dbg_out = {}

        cst, cstb = kb.sb(es, "cst", [128, CST_COLS], F32)
        g.cst, g.cstb = cst, cstb
        g.ident = cst[:, 0:128]
        g.maskadd = [cst[:, 128 + r * 512:128 + (r + 1) * 512] for r in range(4)]
        g.swp = cst[0:32, 128 + 2048:128 + 2048 + 32]
        g.ropec = cst[0:32, 128 + 2048 + 32:128 + 2048 + 34]
        g.ctab, g.ctabb = kb.sb(es, "ctab", [128, 80], F32)
        g.gains, g.gainsb = kb.sb(es, "gains", [128, 13 * KC], F32)
        g.ones_f, g.ones_fb = kb.sb(es, "ones_f", [128, 128], F32)
        g.ones_h, g.ones_hb = kb.sb(es, "ones_h", [128, 128], BF16)
        g.halo, g.halob = kb.sb(es, "halo", [128, KC, 3], BF16)
        g.onecol = g.ones_f[:, 0:1]

        def body(_):
            kb.dma("sp", cst[:], in_cst, [], [cstb])
            kb.dma("sp", g.ctab[:], in_ctab, [], [g.ctabb])
            kb.dma("sp", g.gains[:], in_gains, [], [g.gainsb])
            kb.op("dve", lambda e: e.memset(g.ones_f[:], 1.0), [], [g.ones_fb])
            kb.op("dve", lambda e: e.memset(g.ones_h[:], 1.0), [], [g.ones_hb])
            g.xres, _ = kb.dram("xres", [D, T], F32)
            g.hT, _ = kb.dram("hT", [D, T], BF16)
            g.catT, _ = kb.dram("catT", [D, T], BF16)
            g.actT, _ = kb.dram("actT", [DFF, T], BF16)
            for q in range(8):
                kb.dma("sp" if q % 2 else "pool", g.xres[q * 512:(q + 1) * 512, :], in_xT[q * 512:(q + 1) * 512, :], [], [])
            g.w_in, g.w_out, g.w_upg, g.w_upv, g.w_down, g.w_mkv = {}, {}, {}, {}, {}, {}
            units = []
            for i in layers:
                g.in_cwf[i] = ext("cwf%d" % i, [128, 2 * FC, 4])
                if "mix" in parts:
                    units.append(("mix", i))
                if "ffn" in parts:
                    units.append(("ffn", i))

            def prep_unit(u):
                what, i = u
                if what == "mix":
                    g.w_in[i] = prep_weight(kb, g, "w_in%d" % i, 512, IN_COLS[i], 1024)
                    g.w_mkv[i] = prep_weight(kb, g, "w_mkv%d" % i, 512, 2 * MEMW, 1024)
                    g.w_out[i] = prep_weight(kb, g, "w_out%d" % i, 512, D, 1024)
                else:
                    g.w_upg[i] = prep_weight(kb, g, "w_upg%d" % i, 512, DFF, 1024)
                    g.w_upv[i] = prep_weight(kb, g, "w_upv%d" % i, 512, DFF, 1024)
                    g.w_down[i] = prep_weight(kb, g, "w_down%d" % i, DFF // 8, D, 256)

            def hook():
                if units:
                    prep_unit(units.pop(0))

            g.hook = hook
            hook()
            if len(parts) == 1:
                hook()
            if "mix" in parts:
                mixer_setup(kb, g, ext)
            kb.barrier()
            for i in layers:
                if "mix" in parts:
                    mixer_phase(kb, g, i)
                if "ffn" in parts:
                    ffn_phase(kb, g, i)
            if final:
                rmsnorm_final(kb, g, yT)
            else:
                for q in range(8):
                    kb.dma("sp", yT[q * 512:(q + 1) * 512, :], g.xres[q * 512:(q + 1) * 512, :], [], [])
            kb.barrier(full=True)
            if "catT" in dbg:
                dcat = nc.dram_tensor("dbg_catT", [D, T], BF16, kind="ExternalOutput").ap()
                for q in range(8):
                    kb.dma("sp", dcat[q * 512:(q + 1) * 512, :], g.catT[q * 512:(q + 1) * 512, :], [], [])
                kb.barrier(full=True)
            g.stats = dict(kb.ecnt)

        block.gpsimd(body)
        nc._kstats = g.stats
    return nc


def rmsnorm_final(kb, g, yT):
    with ExitStack() as es:
        xt = [kb.sb(es, "fx", [128, T], F32) for _ in range(3)]
        sq = [kb.sb(es, "fsq", [128, T], F32) for _ in range(2)]
        rstd, rstdb = kb.sb(es, "frstd", [128, T], F32)
        ho = [kb.sb(es, "fh", [128, T], F32) for _ in range(2)]
        pss = [kb.ps() for _ in range(2)]
        for kc in range(KC):
            x, xb = xt[kc % 3]
            kb.dma("sp", x[:], g.xres[kc * 128:(kc + 1) * 128, :], [], [xb])
            s, sb_ = sq[kc % 2]
            kb.op("act", lambda e: e.activation(out=s[:], in_=x[:], func=AF.Square), [xb], [sb_])
            for h in range(2):
                pt, pb = pss[h]
                kb.op("pe", lambda e: e.matmul(pt[:, 0:512], lhsT=g.ones_f[:], rhs=s[:, h * 512:(h + 1) * 512],
                                                start=(kc == 0), stop=(kc == KC - 1)), [sb_, g.ones_fb], [pb])
        for h in range(2):
            pt, pb = pss[h]
            kb.op("dve", lambda e: e.tensor_scalar(out=rstd[:, h * 512:(h + 1) * 512], in0=pt[:, 0:512], scalar1=1.0 / D,
                                                   scalar2=EPS, op0=ALU.mult, op1=ALU.add), [pb], [rstdb])
        kb.op("act", lambda e: e.activation(out=rstd[:], in_=rstd[:], func=AF.Sqrt), [rstdb], [rstdb])
        kb.op("dve", lambda e: e.reciprocal(out=rstd[:], in_=rstd[:]), [rstdb], [rstdb])
        gc = g.gcol["final"]
        for kc in range(KC):
            x, xb = xt[kc % 3]
            kb.dma("sp", x[:], g.xres[kc * 128:(kc + 1) * 128, :], [], [xb])
            h_, hb = ho[kc % 2]
            kb.op("dve", lambda e: e.scalar_tensor_tensor(out=h_[:], in0=x[:], scalar=g.gains[:, gc + kc:gc + kc + 1],
                                                          in1=rstd[:], op0=ALU.mult, op1=ALU.mult), [xb, rstdb, g.gainsb], [hb])
            kb.dma("pool", yT[kc * 128:(kc + 1) * 128, :], h_[:], [hb], [])


def gather_chunks(kb, name, src_ap, rows, cols, dt, by, csz):
    out = []
    n = rows if by == "r" else cols
    for lo in range(0, n, csz):
        hi = min(n, lo + csz)
        if by == "r":
            cr, cc = hi - lo, cols
            piece = src_ap[lo:hi, :]
        else:
            cr, cc = rows, hi - lo
            piece = src_ap[:, lo:hi]
        bnc, bb = kb.dram(name + "_b", [cr, cc], dt)
        kb.dma("cc", bnc[:, :], piece, [], [bb])
        g1, g1b = kb.dram(name + "_g1", [4 * cr, cc], dt)
        kb.allgather([[0, 1, 2, 3], [4, 5, 6, 7]], bnc, g1, [bb], [g1b])
        A, _ = kb.dram(name + "_A", [4 * cr, cc], dt)
        B, _ = kb.dram(name + "_B", [4 * cr, cc], dt)
        b1, b2 = Buf(name), Buf(name)
        kb.allgather([[0, 4], [1, 5], [2, 6], [3, 7]], g1[0:2 * cr, :], A, [g1b], [b1])
        kb.allgather([[0, 4], [1, 5], [2, 6], [3, 7]], g1[2 * cr:4 * cr, :], B, [g1b], [b2])
        out.append((lo, hi, A, B, (b1, b2)))
    return out


PAIRS = [("A", 0, 0), ("B", 0, 2), ("A", 2, 4), ("B", 2, 6)]


def mixer_setup(kb, g, ext):
    nc = kb.nc
    es = kb.es
    g.qT, _ = kb.dram("qT", [TOKW, T], BF16)
    g.kTb, _ = kb.dram("kTb", [TOKW, T], BF16)
    g.vb, _ = kb.dram("vb", [T, TOKW], BF16)
    g.qmT, _ = kb.dram("qmT", [MEMW, T], BF16)
    g.szT, _ = kb.dram("szT", [TOKW, T], BF16)
    g.xsT, _ = kb.dram("xsT", [TOKW, T], F32)
    g.cumb, _ = kb.dram("cumb", [T, 48], F32)
    g.cumHMd, _ = kb.dram("cumHMd", [48, T], F32)
    g.hmT, _ = kb.dram("hmT", [D, MEMT], BF16)
    g.kmT, _ = kb.dram("kmT", [MEMW, MEMT], BF16)
    g.vm, _ = kb.dram("vm", [MEMT, MEMW], BF16)
    g.memT = ext("memT", [D, MEMT])
    in_pos = nc.dram_tensor("pos", [1, T], mybir.dt.int32, kind="ExternalInput").ap()
    g.in_small = {}
    for i in g.layers:
        g.in_small[i] = ext("small%d" % i, [128, SMALL_COLS])
    g.cos_t, g.cos_tb = kb.sb(es, "cos_t", [32, T], F32)
    g.sin_t, g.sin_tb = kb.sb(es, "sin_t", [32, T], F32)
    g.aHM, g.aHMb = kb.sb(es, "aHM", [48, T], F32)
    g.dtHM, g.dtHMb = kb.sb(es, "dtHM", [48, T], F32)
    with ExitStack() as es2:
        pi_, pib = kb.sb(es2, "posi", [32, T], mybir.dt.int32)
        pf, pfb = kb.sb(es2, "posf", [32, T], F32)
        tmp, tmpb = kb.sb(es2, "postmp", [32, T], F32)
        kb.dma("sp", pi_[:], in_pos[0, :].partition_broadcast(32), [], [pib])
        kb.op("dve", lambda e: e.tensor_copy(out=pf[:], in_=pi_[:]), [pib], [pfb])
        kb.op("dve", lambda e: e.tensor_scalar(out=pf[:], in0=pf[:], scalar1=g.ropec[:, 0:1], scalar2=None, op0=ALU.mult),
              [pfb, g.cstb], [pfb])
        ki, kib = kb.sb(es2, "poski", [32, T], mybir.dt.int32)
        s1, s1b = kb.sb(es2, "poss1", [32, T], F32)
        hp, hpb = kb.sb(es2, "halfpi", [32, 1], F32)
        kb.op("dve", lambda e: e.memset(hp[:], 0.5 * math.pi), [], [hpb])
        for (dst, dstb, shift) in ((g.sin_t, g.sin_tb, 0.0), (g.cos_t, g.cos_tb, 0.5 * math.pi)):
            kb.op("dve", lambda e: e.tensor_scalar(out=tmp[:], in0=pf[:], scalar1=shift, scalar2=1.0 / (2.0 * math.pi),
                                                   op0=ALU.add, op1=ALU.mult), [pfb], [tmpb])
            kb.op("dve", lambda e: e.tensor_copy(out=ki[:], in_=tmp[:]), [tmpb], [kib])
            kb.op("dve", lambda e: e.tensor_copy(out=tmp[:], in_=ki[:]), [kib], [tmpb])
            kb.op("dve", lambda e: e.tensor_scalar(out=s1[:], in0=pf[:], scalar1=shift, scalar2=None, op0=ALU.add), [pfb], [s1b])
            kb.op("dve", lambda e: e.scalar_tensor_tensor(out=tmp[:], in0=tmp[:], scalar=-2.0 * math.pi, in1=s1[:], op0=ALU.mult, op1=ALU.add),
                  [tmpb, s1b], [tmpb])
            kb.op("act", lambda e: e.activation(out=s1[:], in_=tmp[:], func=AF.Sin, scale=0.5), [tmpb], [s1b])
            kb.op("act", lambda e: e.activation(out=tmp[:], in_=tmp[:], func=AF.Sin, scale=-0.5, bias=hp[:, 0:1]), [tmpb, hpb], [tmpb])
            kb.op("dve", lambda e: e.scalar_tensor_tensor(out=dst[:], in0=s1[:], scalar=2.0, in1=tmp[:], op0=ALU.mult, op1=ALU.mult),
                  [s1b, tmpb], [dstb])
        kb.op("dve", lambda e: e.tensor_scalar(out=g.sin_t[:], in0=g.sin_t[:], scalar1=g.ropec[:, 1:2], scalar2=None,
                                               op0=ALU.mult), [g.sin_tb, g.cstb], [g.sin_tb])
        kb.barrier()


SM_LAM = 0
SM_SUBLN = 4
SM_FB = 6
SM_DTB = 7
SM_ALOG = 8
SM_DSKIP = 9
SM_NORMG = 57
SM_CONV = 105
SMALL_COLS = 105 + 200


def mixer_phase(kb, g, i):
    kind, j = i % 3, i // 3
    es0 = ExitStack()
    sm, smb = kb.sb(es0, "small", [128, SMALL_COLS], F32)
    kb.dma("sp", sm[:], g.in_small[i], [], [smb])
    g.sm, g.smb = sm, smb
    with ExitStack() as es1:
        tails = kb.sb(es1, "tails", [128, KC, 3], BF16)
        rmsnorm_fm(kb, g, g.xres, g.gcol["mix"][i], g.hT, T, tails=tails if kind == 2 else None)
        if kind == 2:
            exchange_halo(kb, g, tails)
    rmsnorm_fm(kb, g, g.memT, g.gcol["mem"][i], g.hmT, MEMT)
    with ExitStack() as es:
        stg = Stager(kb, es, 3, BF16, "mstg")

        def evac_m(mode, wi, c0, ws, th, pt, pb):
            o, ob = stg.get()
            if mode == "fm":
                kb.op("act", lambda e: e.activation(out=o[0:ws, 0:MEMT], in_=pt[0:ws, 0:MEMT], func=AF.Copy), [pb], [ob])
                kb.dma("pool", g.kmT[c0:c0 + ws, :], o[0:ws, 0:MEMT], [ob], [])
            else:
                th_, tt = th
                kb.op("act", lambda e: e.activation(out=o[:, 0:ws], in_=pt[:, 0:ws], func=AF.Copy), [pb], [ob])
                kb.dma("pool", g.vm[tt * 128:(tt + 1) * 128, c0 - MEMW:c0 - MEMW + ws], o[:, 0:ws], [ob], [])

        def ld_m(th, rhs, rhsb):
            kb.dma("sp", rhs[:, :, 0:MEMT], g.hmT.rearrange("(kc p) t -> p kc t", p=128), [], [rhsb])

        tiles = [("fm", 0, c0, 256) for c0 in range(0, MEMW, 256)] + [("tm", 0, c0, 256) for c0 in range(MEMW, 2 * MEMW, 256)]
        dense(kb, g, ld_m, KC, [g.w_mkv[i]], tiles, evac_m, nth=1, ntok=MEMT)
    inproj(kb, g, i, kind)
    if kind == 2:
        ssd_post(kb, g)
    attention(kb, g, i, kind)
    memattn(kb, g)
    es0.close()
    kb.barrier()
    residual_dense(kb, g, g.catT, g.w_out[i])


def inproj(kb, g, i, kind):
    sm, smb = g.sm, g.smb
    with ExitStack() as es:
        stg = Stager(kb, es, 4, BF16, "pstg")
        q32s = [kb.sb(es, "q32", [32, TH], F32) for _ in range(2)]
        rt = [kb.sb(es, "rt", [32, TH], F32) for _ in range(4)]
        f32s = [kb.sb(es, "f32s", [128, 3 + TH], F32) for _ in range(3)]
        accs = [kb.sb(es, "iacc", [128, TH], F32) for _ in range(2)]
        st = {"n": 0}
        pend = {}
        if kind == 0:
            segs = {"q": (0, TOKW), "k": (TOKW, 2 * TOKW), "v": (2 * TOKW, 3 * TOKW), "m": (3 * TOKW, A_COLS)}
        elif kind == 1:
            segs = {"q": (0, TOKW), "k": (TOKW, 2 * TOKW), "v": (2 * TOKW, 3 * TOKW), "f": (3 * TOKW, 3 * TOKW + 24),
                    "m": (3 * TOKW + 24, B_COLS)}
        else:
            segs = {"z": (0, TOKW), "x": (TOKW, TOKW + 5120), "d": (TOKW + 5120, TOKW + 5168), "m": (TOKW + 5168, C_COLS)}

        def seg_of(c0):
            for k, (a, b) in segs.items():
                if a <= c0 < b:
                    return k, a
            raise ValueError(c0)

        def store(o, ob, ws, dst):
            kb.dma("pool", dst, o, [ob], [])

        def halo_evac(wi, c0, ws, th, ph, phb):
            sg, a = seg_of(c0)
            if sg != "x":
                return
            ue, ueb = f32s[st["n"] % 3]
            st["n"] += 1
            kb.op("act", lambda e: e.activation(out=ue[:, 0:3], in_=ph[:, 0:3], func=AF.Copy), [phb], [ueb])
            pend[c0] = (ue, ueb)

        def evac(mode, wi, c0, ws, th, pt, pb):
            sg, a = seg_of(c0)
            if mode == "tm":
                th_, tt = th
                o, ob = stg.get()
                kb.op("act", lambda e: e.activation(out=o[:, 0:ws], in_=pt[:, 0:ws], func=AF.Copy), [pb], [ob])
                t0 = th_ * TH + tt * 128
                store(o[:, 0:ws], ob, ws, g.vb[t0:t0 + 128, c0 - a:c0 - a + ws])
                return
            tsl = slice(th * TH, (th + 1) * TH)
            if sg in ("q", "k") and kind == 0:
                o, ob = stg.get()
                q3, q3b = q32s[st["n"] % 2]
                st["n"] += 1
                kb.op("act", lambda e: e.activation(out=q3[:], in_=pt[0:32, 0:TH], func=AF.Copy), [pb], [q3b])
                kb.op("act", lambda e: e.activation(out=o[32:64, :], in_=pt[32:64, 0:TH], func=AF.Copy), [pb], [ob])
                kb.op("act", lambda e: e.activation(out=o[64:128, :], in_=pt[64:128, 0:TH], func=AF.Copy), [pb], [ob])
                sw, swb = kb.hps2()
                kb.op("pe", lambda e: e.matmul(sw[0:32, 0:TH], lhsT=g.swp, rhs=q3[:], start=True, stop=True), [q3b, g.cstb], [swb])
                t1, t1b = rt[st["n"] % 4]
                t2, t2b = rt[(st["n"] + 2) % 4]
                kb.op("dve", lambda e: e.tensor_tensor(out=t1[:], in0=q3[:], in1=g.cos_t[:, tsl], op=ALU.mult), [q3b, g.cos_tb], [t1b])
                kb.op("dve", lambda e: e.tensor_tensor(out=t2[:], in0=sw[0:32, 0:TH], in1=g.sin_t[:, tsl], op=ALU.mult), [swb, g.sin_tb], [t2b])
                kb.op("dve", lambda e: e.tensor_tensor(out=o[0:32, :], in0=t1[:], in1=t2[:], op=ALU.add), [t1b, t2b], [ob])
                dst = (g.qT if sg == "q" else g.kTb)[c0 - a:c0 - a + ws, tsl]
                store(o[0:ws, :], ob, ws, dst)
            elif sg in ("q", "k", "m"):
                o, ob = stg.get()
                kb.op("act", lambda e: e.activation(out=o[0:ws, :], in_=pt[0:ws, 0:TH], func=AF.Copy), [pb], [ob])
                dst = {"q": g.qT, "k": g.kTb, "m": g.qmT}[sg][c0 - a:c0 - a + ws, tsl]
                store(o[0:ws, :], ob, ws, dst)
            elif sg == "f":
                e1, e1b = f32s[st["n"] % 3]
                st["n"] += 1
                kb.op("act", lambda e: e.activation(out=e1[0:24, 0:TH], in_=pt[0:24, 0:TH], func=AF.Exp, scale=-1.0,
                                                    bias=g.negfb[:, 0:1]), [pb, g.negfbb], [e1b])
                kb.op("act", lambda e: e.activation(out=e1[0:24, 0:TH], in_=e1[0:24, 0:TH], func=AF.Ln, bias=g.onecol[0:24, :]),
                      [e1b, g.ones_fb], [e1b])
                kb.op("dve", lambda e: e.tensor_scalar(out=g.aHM[0:24, tsl], in0=e1[0:24, 0:TH], scalar1=-1.0, scalar2=None,
                                                       op0=ALU.mult), [e1b], [g.aHMb])
            elif sg == "d":
                e1, e1b = f32s[st["n"] % 3]
                st["n"] += 1
                kb.op("act", lambda e: e.activation(out=e1[0:48, 0:TH], in_=pt[0:48, 0:TH], func=AF.Exp,
                                                    bias=sm[0:48, SM_DTB:SM_DTB + 1]), [pb, smb], [e1b])
                kb.op("act", lambda e: e.activation(out=g.dtHM[:, tsl], in_=e1[0:48, 0:TH], func=AF.Ln, bias=g.onecol[0:48, :]),
                      [e1b, g.ones_fb], [g.dtHMb])
                kb.op("dve", lambda e: e.tensor_scalar(out=g.aHM[:, tsl], in0=g.dtHM[:, tsl], scalar1=g.negA[:, 0:1], scalar2=None,
                                                       op0=ALU.mult), [g.dtHMb, g.negAb], [g.aHMb])
            elif sg == "z":
                o, ob = stg.get()
                kb.op("act", lambda e: e.activation(out=o[0:ws, :], in_=pt[0:ws, 0:TH], func=AF.Silu), [pb], [ob])
                store(o[0:ws, :], ob, ws, g.szT[c0 - a:c0 - a + ws, tsl])
            elif sg == "x":
                ue, ueb = pend.pop(c0)
                kb.op("act", lambda e: e.activation(out=ue[:, 3:3 + TH], in_=pt[:, 0:TH], func=AF.Copy), [pb], [ueb])
                ac, acb = accs[st["n"] % 2]
                st["n"] += 1
                blk = (c0 - a) // 128
                cwv = sm[:, SM_CONV:SM_CONV + 200].rearrange("p (b k) -> p b k", k=5)
                conv_silu(kb, ue, ueb, 4, cwv, smb, blk, ac, acb, TH)
                if blk < 24:
                    xo, xob = f32s[st["n"] % 3]
                    st["n"] += 1
                    kb.op("act", lambda e: e.activation(out=xo[:, 0:TH], in_=ac[:], func=AF.Silu), [acb], [xob])
                    kb.dma("pool", g.xsT[blk * 128:(blk + 1) * 128, tsl], xo[:, 0:TH], [xob], [])
                else:
                    o, ob = stg.get()
                    kb.op("act", lambda e: e.activation(out=o[:], in_=ac[:], func=AF.Silu), [acb], [ob])
                    if blk < 32:
                        store(o[:], ob, 128, g.kTb[(blk - 24) * 128:(blk - 23) * 128, tsl])
                    else:
                        store(o[:], ob, 128, g.qT[(blk - 32) * 128:(blk - 31) * 128, tsl])

        tiles = []
        for sgn, (a, b) in segs.items():
            mode = "tm" if sgn == "v" else "fm"
            for c0 in range(a, b, 256):
                tiles.append((mode, 0, c0, min(256, b - c0)))
        halo = 3 if kind == 2 else 0
        if kind == 1:
            g.negfb, g.negfbb = kb.sb(es, "negfb", [24, 1], F32)
            kb.op("dve", lambda e: e.tensor_scalar(out=g.negfb[:], in0=sm[0:24, SM_FB:SM_FB + 1], scalar1=-1.0, scalar2=None, op0=ALU.mult),
                  [smb], [g.negfbb])
        if kind == 2:
            g.negA, g.negAb = kb.sb(es, "negA", [48, 1], F32)
            kb.op("act", lambda e: e.activation(out=g.negA[:], in_=sm[0:48, SM_ALOG:SM_ALOG + 1], func=AF.Exp), [smb], [g.negAb])
            kb.op("dve", lambda e: e.tensor_scalar(out=g.negA[:], in0=g.negA[:], scalar1=-1.0, scalar2=None, op0=ALU.mult),
                  [g.negAb], [g.negAb])
        dense(kb, g, make_rhs_loader(kb, g, g.hT, KC, halo), KC, [g.w_in[i]], tiles, evac, halo=halo,
              halo_evac=halo_evac if halo else None)


def ssd_post(kb, g):
    with ExitStack() as es:
        dtTM, dtTMb = kb.sb(es, "dtTM", [128, 8, 48], F32)
        for tt in range(8):
            pt, pb = kb.ps()
            kb.op("pe", lambda e: e.transpose(out=pt[:, 0:48], in_=g.dtHM[:, tt * 128:(tt + 1) * 128], identity=g.ident[0:48, 0:48]),
                  [g.dtHMb, g.cstb], [pb])
            kb.op("act", lambda e: e.activation(out=dtTM[:, tt, :], in_=pt[:, 0:48], func=AF.Copy), [pb], [dtTMb])
        xs = [kb.sb(es, "xsl", [128, T], F32) for _ in range(2)]
        stg = Stager(kb, es, 3, BF16, "xstg", width=128)
        for cc in range(24):
            x, xb = xs[cc % 2]
            kb.dma("sp", x[:], g.xsT[cc * 128:(cc + 1) * 128, :], [], [xb])
            for tt in range(8):
                pt, pb = kb.ps()
                kb.op("pe", lambda e: e.transpose(out=pt[:, 0:128], in_=x[:, tt * 128:(tt + 1) * 128], identity=g.ident), [xb, g.cstb], [pb])
                o, ob = stg.get()
                for hh in range(2):
                    h = cc * 2 + hh
                    kb.op("dve", lambda e: e.tensor_scalar(out=o[:, hh * 64:(hh + 1) * 64], in0=pt[:, hh * 64:(hh + 1) * 64],
                                                           scalar1=dtTM[:, tt, h:h + 1], scalar2=None, op0=ALU.mult), [pb, dtTMb], [ob])
                kb.dma("pool", g.vb[tt * 128:(tt + 1) * 128, cc * 128:(cc + 1) * 128], o[:, 0:128], [ob], [])
    kb.barrier()


def attention(kb, g, i, kind):
    sm, smb = g.sm, g.smb
    NU = {0: 12, 1: 24, 2: 8}[kind]
    NM = 2 if kind == 0 else 1
    E = {0: 256, 1: 128, 2: 384}[kind]
    KR = 1024 if kind == 2 else TOKW
    H = {0: 0, 1: 24, 2: 48}[kind]
    scale = 128.0 ** -0.5
    kb.barrier()
    kch = gather_chunks(kb, "kg", g.kTb[0:KR, :], KR, T, BF16, "r", 512)
    vch = gather_chunks(kb, "vg", g.vb, T, TOKW, BF16, "c", 512)
    es = ExitStack()
    if H:
        cumHM, cumHMb = kb.sb(es, "cumHM", [48, T], F32)
        onesT, onesTb = kb.sb(es, "onesT", [48, T], F32)
        zc, zcb = kb.sb(es, "zc", [48, 1], F32)
        cumown, cumownb = kb.sb(es, "cumown", [128, 8, 48], F32)
        cumpre, cumpreb = kb.sb(es, "cumpre", [128, NPRE, 48], F32)
        totb, totbb = kb.sb(es, "totb", [128, 7, 48], F32)
        offb, offbb = kb.sb(es, "offb", [128, 7, 48], F32)
        offown, offownb = kb.sb(es, "offown", [128, 48], F32)
        doff, doffb = kb.sb(es, "doff", [128, 7, 48], F32)
        kb.op("dve", lambda e: e.memset(onesT[:], 1.0), [], [onesTb])
        kb.op("dve", lambda e: e.memset(zc[:], 0.0), [], [zcb])
        kb.op("dve", lambda e: e.memset(cumown[:], 0.0), [], [cumownb])
        kb.op("dve", lambda e: e.tensor_tensor_scan(out=cumHM[0:H, :], data0=onesT[0:H, :], data1=g.aHM[0:H, :], initial=zc[0:H, :],
                                                    op0=ALU.mult, op1=ALU.add), [onesTb, g.aHMb, zcb], [cumHMb])
        kb.dma("sp", g.cumHMd[0:H, :], cumHM[0:H, :], [cumHMb], [])
        for tt in range(8):
            pt, pb = kb.ps()
            kb.op("pe", lambda e: e.transpose(out=pt[:, 0:H], in_=cumHM[0:H, tt * 128:(tt + 1) * 128], identity=g.ident[0:H, 0:H]),
                  [cumHMb, g.cstb], [pb])
            kb.op("act", lambda e: e.activation(out=cumown[:, tt, 0:H], in_=pt[:, 0:H], func=AF.Copy), [pb], [cumownb])
        kb.dma("sp", g.cumb.rearrange("(j p) h -> p j h", p=128), cumown[:], [cumownb], [])
        kb.barrier()
        (lo, hi, CA, CB, cbufs) = gather_chunks(kb, "cg", g.cumb, T, 48, F32, "r", T)[0]
        bufsel = {"A": CA, "B": CB}
        for (bn, s0, r0) in PAIRS:
            n = 2 if r0 < 6 else 1
            src = bufsel[bn][s0 * T:(s0 + n) * T, :].rearrange("(x p) h -> p x h", p=128)
            kb.dma("sp", cumpre[:, r0 * 8:(r0 + n) * 8, :], src, list(cbufs), [cumpreb])
            for k in range(n):
                row = (s0 + k) * T + T - 1
                kb.dma("pool", totb[:, r0 + k, :], bufsel[bn][row, :].partition_broadcast(128), list(cbufs), [totbb])
        kb.op("dve", lambda e: e.memset(offb[:], 0.0), [], [offbb])
        for r in range(1, 7):
            kb.op("dve", lambda e: e.tensor_tensor(out=offb[:, r, :], in0=offb[:, r - 1, :], in1=totb[:, r - 1, :], op=ALU.add),
                  [offbb, totbb], [offbb])
        kb.op("dve", lambda e: e.tensor_scalar(out=offown[:], in0=totb[:, 0, :], scalar1=g.ctab[:, 72:73], scalar2=None, op0=ALU.mult),
              [totbb, g.ctabb], [offownb])
        for r in range(1, 7):
            kb.op("dve", lambda e: e.scalar_tensor_tensor(out=offown[:], in0=totb[:, r, :], scalar=g.ctab[:, 72 + r:73 + r], in1=offown[:],
                                                          op0=ALU.mult, op1=ALU.add), [totbb, offownb, g.ctabb], [offownb])
        for r in range(7):
            kb.op("dve", lambda e: e.tensor_tensor(out=doff[:, r, :], in0=offown[:], in1=offb[:, r, :], op=ALU.subtract),
                  [offownb, offbb], [doffb])
    g.hook()
    if kind == 0:
        lam, lamb = kb.sb(es, "lam", [128, 4], F32)
        kb.op("dve", lambda e: e.tensor_tensor(out=lam[:, 0:1], in0=sm[:, SM_LAM:SM_LAM + 1], in1=sm[:, SM_LAM + 1:SM_LAM + 2], op=ALU.mult), [smb], [lamb])
        kb.op("dve", lambda e: e.tensor_tensor(out=lam[:, 1:2], in0=sm[:, SM_LAM + 2:SM_LAM + 3], in1=sm[:, SM_LAM + 3:SM_LAM + 4], op=ALU.mult), [smb], [lamb])
        pt, pb = kb.ps()
        kb.op("pe", lambda e: e.matmul(pt[:, 0:2], lhsT=g.ones_f[:], rhs=lam[:, 0:2], start=True, stop=True), [lamb, g.ones_fb], [pb])
        kb.op("act", lambda e: e.activation(out=lam[:, 2:4], in_=pt[:, 0:2], func=AF.Exp), [pb], [lamb])
        kb.op("dve", lambda e: e.tensor_tensor(out=lam[:, 0:1], in0=lam[:, 3:4], in1=lam[:, 2:3], op=ALU.subtract), [lamb], [lamb])
        kb.op("dve", lambda e: e.tensor_scalar(out=lam[:, 0:1], in0=lam[:, 0:1], scalar1=-lambda_init(i), scalar2=None, op0=ALU.add), [lamb], [lamb])
        subg, subgb = kb.sb(es, "subg", [128, 2], F32)
        kb.op("dve", lambda e: e.tensor_scalar(out=subg[:], in0=sm[:, SM_SUBLN:SM_SUBLN + 2], scalar1=1.0 - lambda_init(i), scalar2=None,
                                               op0=ALU.mult), [smb], [subgb])

    Kpre = [kb.sb(es, "Kpre", [128, 7, T], BF16) for _ in range(NM)]
    Kown = [kb.sb(es, "Kown", [128, T], BF16) for _ in range(NM)]
    Vpre, Vpreb = kb.sb(es, "Vpre", [128, NPRE, E], BF16)
    Vown, Vownb = kb.sb(es, "Vown", [128, 8, E], BF16)
    qt = [kb.sb(es, "qt", [128, TH], BF16) for _ in range(2 * NM)]
    NH = 6 if kind == 2 else 1
    pts = [kb.sb(es, "pT", [128, TH], BF16) for _ in range(2 * NM * NH + 1)]
    tmps = [kb.sb(es, "atmp", [128, TH], F32) for _ in range(3)]
    npt = [0]
    if H:
        cqb = [kb.sb(es, "cqb", [128, TH], F32) for _ in range(NH)]
        bpre = [kb.sb(es, "bpre", [128, NPRE], F32) for _ in range(NH)]
        bown = [kb.sb(es, "bown", [128, 8], F32) for _ in range(NH)]
    ostg = Stager(kb, es, 3, BF16, "aostg")
    fin = [kb.sb(es, "afin", [128, TH], F32) for _ in range(6)]
    rden = [kb.sb(es, "rden", [128, TH], F32) for _ in range(2)]
    PS = kb.psum
    zero_col = None

    for u in range(NU):
        for m in range(NM):
            blk = (2 * u + m) if kind == 0 else u
            (lo, hi, A, B, kbufs) = kch[blk // 4]
            bsel = {"A": A, "B": B}
            Kp, Kpb = Kpre[m]
            for (bn, s0, r0) in PAIRS:
                n = 2 if r0 < 6 else 1
                v4 = bsel[bn].rearrange("(s c p) t -> p s c t", s=4, c=4, p=128)
                kb.dma("sp" if r0 % 4 == 0 else "pool", Kp[:, r0:r0 + n, :], v4[:, s0:s0 + n, blk % 4, :], list(kbufs), [Kpb])
            Ko, Kob = Kown[m]
            kb.dma("sp", Ko[:], g.kTb[blk * 128:(blk + 1) * 128, :], [], [Kob])
        c_lo, c_hi = u * E, (u + 1) * E
        Vp5 = Vpre[:].rearrange("p (r j) e -> p r j e", j=8)
        for (lo, hi, A, B, vbufs) in vch:
            a0, a1 = max(lo, c_lo), min(hi, c_hi)
            if a0 >= a1:
                continue
            bsel = {"A": A, "B": B}
            for (bn, s0, r0) in PAIRS:
                n = 2 if r0 < 6 else 1
                v4 = bsel[bn].rearrange("(s j p) c -> p s j c", s=4, j=8, p=128)
                kb.dma("sp" if r0 % 4 == 0 else "pool", Vp5[:, r0:r0 + n, :, a0 - c_lo:a1 - c_lo], v4[:, s0:s0 + n, :, a0 - lo:a1 - lo],
                       list(vbufs), [Vpreb])
        kb.dma("sp", Vown[:], g.vb[:, c_lo:c_hi].rearrange("(j p) c -> p j c", p=128), [], [Vownb])
        if H:
            for hh in range(NH):
                h = u * NH + hh
                bp, bpb = bpre[hh]
                bo, bob = bown[hh]
                kb.op("dve", lambda e: e.scalar_tensor_tensor(out=bp[:], in0=cumpre[:, :, h], scalar=-1.0, in1=g.ctab[:, 0:NPRE],
                                                              op0=ALU.mult, op1=ALU.add), [cumpreb, g.ctabb], [bpb])
                for r in range(7):
                    kb.op("dve", lambda e: e.tensor_scalar(out=bp[:, r * 8:(r + 1) * 8], in0=bp[:, r * 8:(r + 1) * 8],
                                                           scalar1=doff[:, r, h:h + 1], scalar2=None, op0=ALU.add), [bpb, doffb], [bpb])
                kb.op("dve", lambda e: e.tensor_scalar(out=bo[:], in0=cumown[:, :, h], scalar1=-1.0, scalar2=None, op0=ALU.mult),
                      [cumownb], [bob])
        for th in range(NTH):
            tsl = slice(th * TH, (th + 1) * TH)
            qs = []
            for m in range(NM):
                blk = (2 * u + m) if kind == 0 else u
                q_, qb_ = qt[(th * NM + m) % len(qt)]
                kb.dma("sp", q_[:], g.qT[blk * 128:(blk + 1) * 128, tsl], [], [qb_])
                qs.append((q_, qb_))
            if H:
                for hh in range(NH):
                    h = u * NH + hh
                    kb.dma("pool", cqb[hh][0][:], g.cumHMd[h, tsl].partition_broadcast(128), [], [cqb[hh][1]])
            slots = []
            for s_ in range(NPRE):
                slots.append(("pre", s_, None))
            for j in range(4 * th + 4):
                slots.append(("own", j, (j - 4 * th) if j >= 4 * th else None))
            nsl = len(slots)
            if kind == 0:
                obank = [[PS[0], PS[1]], [PS[2], PS[3]]]
                dbank = [PS[4], PS[5]]
                sbank = [PS[6], PS[7]]
            elif kind == 1:
                obank = [[PS[0]]]
                dbank = [PS[1]]
                sbank = [PS[2], PS[3], PS[4]]
            else:
                ybank = [PS[0], PS[1], PS[2], PS[3], PS[4], PS[5]]
                sbank = [PS[6], PS[7]]
            def stage_a(si):
                typ, idx, diag = slots[si]
                outl = []
                for m in range(NM):
                    if typ == "pre":
                        Kt = Kpre[m][0][:, idx // 8, (idx % 8) * 128:(idx % 8 + 1) * 128]
                        Ktb = Kpre[m][1]
                        Vsrc, Vtb = Vpre, Vpreb
                    else:
                        Kt = Kown[m][0][:, idx * 128:(idx + 1) * 128]
                        Ktb = Kown[m][1]
                        Vsrc, Vtb = Vown, Vownb
                    sT, sTb = sbank[(si * NM + m) % len(sbank)]
                    kb.op("pe", lambda e: e.matmul(sT[:, 0:TH], lhsT=Kt, rhs=qs[m][0][:], start=True, stop=True), [Ktb, qs[m][1]], [sTb])
                    if kind == 0:
                        p_, pb_ = pts[npt[0] % len(pts)]
                        npt[0] += 1
                        if diag is not None:
                            t_, tb_ = tmps[npt[0] % 3]
                            kb.op("dve", lambda e: e.tensor_tensor(out=t_[:], in0=sT[:, 0:TH], in1=g.maskadd[diag], op=ALU.add), [sTb, g.cstb], [tb_])
                            kb.op("act", lambda e: e.activation(out=p_[:], in_=t_[:], func=AF.Exp, scale=scale), [tb_], [pb_])
                        elif typ == "pre":
                            kb.op("act", lambda e: e.activation(out=p_[:], in_=sT[:, 0:TH], func=AF.Exp, scale=scale, bias=g.ctab[:, idx:idx + 1]),
                                  [sTb, g.ctabb], [pb_])
                        else:
                            kb.op("act", lambda e: e.activation(out=p_[:], in_=sT[:, 0:TH], func=AF.Exp, scale=scale), [sTb], [pb_])
                        outl.append((p_, pb_, Vsrc, Vtb, idx, m, 0))
                    elif kind == 1:
                        bcol = (bpre[0][0][:, idx:idx + 1], bpre[0][1]) if typ == "pre" else (bown[0][0][:, idx:idx + 1], bown[0][1])
                        t_, tb_ = tmps[npt[0] % 3]
                        p_, pb_ = pts[npt[0] % len(pts)]
                        npt[0] += 1
                        kb.op("dve", lambda e: e.scalar_tensor_tensor(out=t_[:], in0=sT[:, 0:TH], scalar=scale, in1=cqb[0][0][:], op0=ALU.mult, op1=ALU.add),
                              [sTb, cqb[0][1]], [tb_])
                        if diag is not None:
                            kb.op("dve", lambda e: e.tensor_tensor(out=t_[:], in0=t_[:], in1=g.maskadd[diag], op=ALU.add), [tb_, g.cstb], [tb_])
                        kb.op("act", lambda e: e.activation(out=p_[:], in_=t_[:], func=AF.Exp, bias=bcol[0]), [tb_, bcol[1]], [pb_])
                        outl.append((p_, pb_, Vsrc, Vtb, idx, 0, 0))
                    else:
                        for hh in range(6):
                            bcol = (bpre[hh][0][:, idx:idx + 1], bpre[hh][1]) if typ == "pre" else (bown[hh][0][:, idx:idx + 1], bown[hh][1])
                            t_, tb_ = tmps[npt[0] % 3]
                            p_, pb_ = pts[npt[0] % len(pts)]
                            npt[0] += 1
                            if diag is not None:
                                kb.op("dve", lambda e: e.tensor_tensor(out=t_[:], in0=cqb[hh][0][:], in1=g.maskadd[diag], op=ALU.add), [cqb[hh][1], g.cstb], [tb_])
                                kb.op("act", lambda e: e.activation(out=t_[:], in_=t_[:], func=AF.Exp, bias=bcol[0]), [tb_, bcol[1]], [tb_])
                            else:
                                kb.op("act", lambda e: e.activation(out=t_[:], in_=cqb[hh][0][:], func=AF.Exp, bias=bcol[0]), [cqb[hh][1], bcol[1]], [tb_])
                            kb.op("dve", lambda e: e.tensor_tensor(out=p_[:], in0=sT[:, 0:TH], in1=t_[:], op=ALU.mult), [sTb, tb_], [pb_])
                            outl.append((p_, pb_, Vsrc, Vtb, idx, 0, hh))
                return outl

            def stage_b(si, outl):
                first, last = (si == 0), (si == nsl - 1)
                for (p_, pb_, Vsrc, Vtb, idx, m, hh) in outl:
                    if kind == 0:
                        kb.op("pe", lambda e: e.matmul(dbank[m][0][:, 0:TH], lhsT=g.ones_h[:], rhs=p_[:], start=first, stop=last), [pb_, g.ones_hb], [dbank[m][1]])
                        for ec in range(2):
                            kb.op("pe", lambda e: e.matmul(obank[m][ec][0][:, 0:TH], lhsT=Vsrc[:, idx, ec * 128:(ec + 1) * 128], rhs=p_[:], start=first, stop=last),
                                  [pb_, Vtb], [obank[m][ec][1]])
                    elif kind == 1:
                        kb.op("pe", lambda e: e.matmul(dbank[0][0][:, 0:TH], lhsT=g.ones_h[:], rhs=p_[:], start=first, stop=last), [pb_, g.ones_hb], [dbank[0][1]])
                        kb.op("pe", lambda e: e.matmul(obank[0][0][0][:, 0:TH], lhsT=Vsrc[:, idx, 0:128], rhs=p_[:], start=first, stop=last), [pb_, Vtb], [obank[0][0][1]])
                    else:
                        yb = ybank[hh]
                        kb.op("pe", lambda e: e.matmul(yb[0][0:64, 0:TH], lhsT=Vsrc[:, idx, hh * 64:(hh + 1) * 64], rhs=p_[:],
                                                        start=first, stop=last), [pb_, Vtb], [yb[1]])

            prev = None
            for si in range(nsl + 1):
                cur = stage_a(si) if si < nsl else None
                if prev is not None:
                    stage_b(si - 1, prev)
                prev = cur
            if kind == 0:
                for m in range(2):
                    kb.op("dve", lambda e: e.reciprocal(out=rden[m][0][:], in_=dbank[m][0][:, 0:TH]), [dbank[m][1]], [rden[m][1]])
                outs = []
                for ec in range(2):
                    t1, t1b = fin[ec * 2]
                    t2, t2b = fin[ec * 2 + 1]
                    kb.op("dve", lambda e: e.tensor_tensor(out=t1[:], in0=obank[0][ec][0][:, 0:TH], in1=rden[0][0][:], op=ALU.mult), [obank[0][ec][1], rden[0][1]], [t1b])
                    kb.op("dve", lambda e: e.tensor_tensor(out=t2[:], in0=obank[1][ec][0][:, 0:TH], in1=rden[1][0][:], op=ALU.mult), [obank[1][ec][1], rden[1][1]], [t2b])
                    kb.op("dve", lambda e: e.scalar_tensor_tensor(out=t1[:], in0=t2[:], scalar=lam[:, 0:1], in1=t1[:], op0=ALU.mult, op1=ALU.add), [t2b, t1b, lamb], [t1b])
                    kb.op("act", lambda e: e.activation(out=t2[:], in_=t1[:], func=AF.Square), [t1b], [t2b])
                    kb.op("pe", lambda e: e.matmul(PS[6][0][:, 0:TH], lhsT=g.ones_f[:], rhs=t2[:], start=(ec == 0), stop=(ec == 1)), [t2b, g.ones_fb], [PS[6][1]])
                    outs.append((t1, t1b))
                rs_, rsb = fin[4]
                kb.op("dve", lambda e: e.tensor_scalar(out=rs_[:], in0=PS[6][0][:, 0:TH], scalar1=1.0 / 256, scalar2=EPS, op0=ALU.mult, op1=ALU.add), [PS[6][1]], [rsb])
                kb.op("act", lambda e: e.activation(out=rs_[:], in_=rs_[:], func=AF.Sqrt), [rsb], [rsb])
                kb.op("dve", lambda e: e.reciprocal(out=rs_[:], in_=rs_[:]), [rsb], [rsb])
                for ec in range(2):
                    o, ob = ostg.get()
                    kb.op("dve", lambda e: e.scalar_tensor_tensor(out=o[:], in0=outs[ec][0][:], scalar=subg[:, ec:ec + 1], in1=rs_[:], op0=ALU.mult, op1=ALU.mult),
                          [outs[ec][1], rsb, subgb], [ob])
                    kb.dma("pool", g.catT[u * 256 + ec * 128:u * 256 + (ec + 1) * 128, tsl], o[:], [ob], [])
            elif kind == 1:
                kb.op("dve", lambda e: e.reciprocal(out=rden[0][0][:], in_=dbank[0][0][:, 0:TH]), [dbank[0][1]], [rden[0][1]])
                o, ob = ostg.get()
                kb.op("dve", lambda e: e.tensor_tensor(out=o[:], in0=obank[0][0][0][:, 0:TH], in1=rden[0][0][:], op=ALU.mult), [obank[0][0][1], rden[0][1]], [ob])
                kb.dma("pool", g.catT[u * 128:(u + 1) * 128, tsl], o[:], [ob], [])
            else:
                gs = []
                for hh in range(6):
                    h = u * 6 + hh
                    xs_, xsb = fin[hh]
                    sz_, szb = ostg.get()
                    kb.dma("sp", xs_[0:64, :], g.xsT[h * 64:(h + 1) * 64, tsl], [], [xsb])
                    kb.dma("sp", sz_[0:64, :], g.szT[h * 64:(h + 1) * 64, tsl], [], [szb])
                    kb.op("dve", lambda e: e.scalar_tensor_tensor(out=xs_[0:64, :], in0=xs_[0:64, :], scalar=sm[0:64, SM_DSKIP + h:SM_DSKIP + h + 1],
                                                                  in1=ybank[hh][0][0:64, 0:TH], op0=ALU.mult, op1=ALU.add), [xsb, smb, ybank[hh][1]], [xsb])
                    kb.op("dve", lambda e: e.tensor_tensor(out=xs_[0:64, :], in0=xs_[0:64, :], in1=sz_[0:64, :], op=ALU.mult), [xsb, szb], [xsb])
                    sq_, sqb = tmps[hh % 3]
                    kb.op("act", lambda e: e.activation(out=sq_[0:64, :], in_=xs_[0:64, :], func=AF.Square), [xsb], [sqb])
                    kb.op("pe", lambda e: e.matmul(PS[6][0][:, 0:TH], lhsT=g.ones_f[0:64, :], rhs=sq_[0:64, :], start=(hh == 0), stop=(hh == 5)),
                          [sqb, g.ones_fb], [PS[6][1]])
                    gs.append((xs_, xsb))
                rs_, rsb = rden[0]
                kb.op("dve", lambda e: e.tensor_scalar(out=rs_[:], in0=PS[6][0][:, 0:TH], scalar1=1.0 / 384, scalar2=EPS, op0=ALU.mult, op1=ALU.add), [PS[6][1]], [rsb])
                kb.op("act", lambda e: e.activation(out=rs_[:], in_=rs_[:], func=AF.Sqrt), [rsb], [rsb])
                kb.op("dve", lambda e: e.reciprocal(out=rs_[:], in_=rs_[:]), [rsb], [rsb])
                for hh in range(6):
                    h = u * 6 + hh
                    o, ob = ostg.get()
                    kb.op("dve", lambda e: e.scalar_tensor_tensor(out=o[0:64, :], in0=gs[hh][0][0:64, :], scalar=sm[0:64, SM_NORMG + h:SM_NORMG + h + 1],
                                                                  in1=rs_[0:64, :], op0=ALU.mult, op1=ALU.mult), [gs[hh][1], rsb, smb], [ob])
                    kb.dma("pool", g.catT[h * 64:(h + 1) * 64, tsl], o[0:64, :], [ob], [])
    es.close()
    kb.barrier()


def memattn(kb, g):
    scale = 256.0 ** -0.5
    PS = kb.psum
    with ExitStack() as es:
        km, kmb = kb.sb(es, "km", [128, 8, MEMT], BF16)
        vm, vmb = kb.sb(es, "vmm", [128, 2, MEMW], BF16)
        kb.dma("sp", km[:], g.kmT.rearrange("(c p) m -> p c m", p=128), [], [kmb])
        kb.dma("sp", vm[:], g.vm.rearrange("(c p) e -> p c e", p=128), [], [vmb])
        qm = [kb.sb(es, "qm", [128, TH], BF16) for _ in range(4)]
        pts = [kb.sb(es, "mp", [128, TH], BF16) for _ in range(4)]
        rd, rdb = kb.sb(es, "mrd", [128, TH], F32)
        ostg = Stager(kb, es, 3, BF16, "mostg")
        n = 0
        for hm in range(4):
            for th in range(NTH):
                tsl = slice(th * TH, (th + 1) * TH)
                qq = []
                for dc in range(2):
                    q_, qb_ = qm[n % 4]
                    n += 1
                    r0 = hm * 256 + dc * 128
                    kb.dma("sp", q_[:], g.qmT[r0:r0 + 128, tsl], [], [qb_])
                    qq.append((q_, qb_))
                pp = []
                for mc in range(2):
                    sT, sTb = PS[3 + mc]
                    for dc in range(2):
                        kb.op("pe", lambda e: e.matmul(sT[:, 0:TH], lhsT=km[:, hm * 2 + dc, mc * 128:(mc + 1) * 128], rhs=qq[dc][0][:],
                                                        start=(dc == 0), stop=(dc == 1)), [kmb, qq[dc][1]], [sTb])
                    p_, pb_ = pts[(n + mc) % 4]
                    kb.op("act", lambda e: e.activation(out=p_[:], in_=sT[:, 0:TH], func=AF.Exp, scale=scale), [sTb], [pb_])
                    pp.append((p_, pb_))
                for mc in range(2):
                    kb.op("pe", lambda e: e.matmul(PS[0][0][:, 0:TH], lhsT=g.ones_h[:], rhs=pp[mc][0][:], start=(mc == 0), stop=(mc == 1)),
                          [pp[mc][1], g.ones_hb], [PS[0][1]])
                for ec in range(2):
                    for mc in range(2):
                        c0 = hm * 256 + ec * 128
                        kb.op("pe", lambda e: e.matmul(PS[1 + ec][0][:, 0:TH], lhsT=vm[:, mc, c0:c0 + 128], rhs=pp[mc][0][:], start=(mc == 0), stop=(mc == 1)),
                              [pp[mc][1], vmb], [PS[1 + ec][1]])
                kb.op("dve", lambda e: e.reciprocal(out=rd[:], in_=PS[0][0][:, 0:TH]), [PS[0][1]], [rdb])
                for ec in range(2):
                    o, ob = ostg.get()
                    kb.op("dve", lambda e: e.tensor_tensor(out=o[:], in0=PS[1 + ec][0][:, 0:TH], in1=rd[:], op=ALU.mult), [PS[1 + ec][1], rdb], [ob])
                    r0 = TOKW + hm * 256 + ec * 128
                    kb.dma("pool", g.catT[r0:r0 + 128, tsl], o[:], [ob], [])
    kb.barrier()


def host_constants():
    cst = np.zeros((128, CST_COLS), np.float32)
    cst[:, 0:128] = np.eye(128, dtype=np.float32)
    p = np.arange(128)[:, None]
    f = np.arange(512)[None, :]
    for r in range(4):
        cst[:, 128 + r * 512:128 + (r + 1) * 512] = np.where(f >= r * 128 + p, 0.0, NEG)
    sw = np.zeros((32, 32), np.float32)
    for m in range(32):
        sw[(m + 16) % 32, m] = 1.0
    cst[0:32, 128 + 2048:128 + 2048 + 32] = sw
    half = 16
    invf = np.power(np.float32(ROPE_THETA), -np.arange(half, dtype=np.float32) * np.float32(2.0 / 32)).astype(np.float32)
    cst[0:32, 128 + 2048 + 32] = np.concatenate([invf, invf])
    cst[0:32, 128 + 2048 + 33] = np.concatenate([-np.ones(16), np.ones(16)])
    return cst


def host_ctab(c):
    t = np.zeros((128, 80), np.float32)
    for s in range(64):
        t[:, s] = 0.0 if s < 8 * c else NEG
    for j in range(8):
        t[:, 64 + j] = 1.0 if j == c - 1 else 0.0
        t[:, 72 + j] = 1.0 if j < c else 0.0
    return t


def pk(v):
    v = np.asarray(v, np.float32)
    return np.ascontiguousarray(v.reshape(-1, 128).T)


def host_inputs(inp, layers=(0, 1, 2, 3), parts=("mix", "ffn"), x_override=None):
    cst = host_constants()
    gains = np.concatenate([pk(inp["norm_mix"][i]) for i in range(4)] + [pk(inp["norm_ffn"][i]) for i in range(4)] +
                           [pk(inp["norm_mem"][i]) for i in range(4)] + [pk(inp["final_norm"])], axis=1)
    x = inp["x"][0] if x_override is None else x_override
    maps = []
    shared = {}
    for i in layers:
        cw = np.concatenate([inp["conv_ffn_w"][i].T, inp["conv_ffn_b"][i][:, None]], axis=1)
        shared["cwf%d" % i] = np.ascontiguousarray(cw.reshape(2 * FC, 128, 4).transpose(1, 0, 2))
    if "mix" in parts:
        shared.update(host_mixer_shared(inp, layers))
    for c in range(NCORES):
        m = dict(shared)
        m["xT"] = np.ascontiguousarray(x[c * T:(c + 1) * T, :].T)
        m["cst"] = cst
        m["ctab"] = host_ctab(c)
        m["gains"] = gains
        r0, r1 = c * 512, (c + 1) * 512
        for i in layers:
            if "ffn" in parts:
                m["w_upg%d" % i] = np.ascontiguousarray(inp["w_up"][i][r0:r1, :DFF])
                m["w_upv%d" % i] = np.ascontiguousarray(inp["w_up"][i][r0:r1, DFF:])
                m["w_down%d" % i] = np.ascontiguousarray(inp["w_down"][i][c * (DFF // 8):(c + 1) * (DFF // 8), :])
            if "mix" in parts:
                kind, j = i % 3, i // 3
                w = (inp["a_w_in"], inp["b_w_in"], inp["c_w_in"])[kind][j]
                m["w_in%d" % i] = np.ascontiguousarray(w[r0:r1, :])
                m["w_mkv%d" % i] = np.ascontiguousarray(inp["w_mem_kv"][i][r0:r1, :])
                m["w_out%d" % i] = np.ascontiguousarray(inp["w_out"][i][r0:r1, :])
        if "mix" in parts:
            m.update(host_mixer_percore(inp, c))
        maps.append(m)
    return maps


def host_mixer_shared(inp, layers):
    out = {"memT": np.ascontiguousarray(inp["mem"][0].T)}
    for i in layers:
        kind, j = i % 3, i // 3
        sm = np.zeros((128, SMALL_COLS), np.float32)
        if kind == 0:
            sm[:, SM_LAM:SM_LAM + 4] = inp["a_lambda"][j].T
            sm[:, SM_SUBLN:SM_SUBLN + 2] = pk(inp["a_subln"][j])
        elif kind == 1:
            sm[0:24, SM_FB] = inp["b_forget_bias"][j]
        else:
            sm[0:48, SM_DTB] = inp["c_dt_bias"][j]
            sm[0:48, SM_ALOG] = inp["c_a_log"][j]
            sm[:, SM_DSKIP:SM_DSKIP + 48] = np.broadcast_to(inp["c_d_skip"][j][None, :], (128, 48))
            sm[0:64, SM_NORMG:SM_NORMG + 48] = inp["c_norm_gate"][j].reshape(48, 64).T
            cw = np.concatenate([inp["c_conv_w"][j].T, inp["c_conv_b"][j][:, None]], axis=1)
            sm[:, SM_CONV:SM_CONV + 200] = cw.reshape(40, 128, 5).transpose(1, 0, 2).reshape(128, 200)
        out["small%d" % i] = sm
    return out


def host_mixer_percore(inp, c):
    return {"pos": np.ascontiguousarray(inp["positions"][:, c * T:(c + 1) * T].astype(np.int32))}


_NC_CACHE = {}


def kernel(**inputs):
    inp = {k: np.asarray(v) for k, v in inputs.items()}
    if "full" not in _NC_CACHE:
        _NC_CACHE["full"] = build()
    nc = _NC_CACHE["full"]
    maps = host_inputs(inp)
    res = run_bass_kernel_spmd(nc, maps, core_ids=list(range(NCORES)))
    out = np.concatenate([np.asarray(res.results[c]["yT"]).T for c in range(NCORES)], axis=0)
    return np.ascontiguousarray(out[None].astype(np.float32))
```
